# Optimizing a Trainium2 kernel written in Bass

```python
import jax, jax.numpy as jnp
from jax import lax
import numpy as np

D_MODEL = 1024
BATCH = 8
SEQ = 2048
DEPTH = 2

CHUNK = 64
HEAD_DIM = 64
EPS = 1e-6
NEG_INF = -1e30

A_HEADS = 8
A_KV_HEADS = 2
A_WINDOW = 128
A_PREV_CHUNKS = A_WINDOW // CHUNK
A_BAND_CHUNKS = A_PREV_CHUNKS + 1

B_HEADS = 8
B_BLOCK = 128
FORGET_BIAS_INIT = 3.0

C_HEADS = 8
C_PREV_CHUNKS = 8
C_BAND_CHUNKS = C_PREV_CHUNKS + 1
REL_CLIP = 128
N_REL = 2 * REL_CLIP + 1

N_BRANCH = 3
BRANCH_WIDTH = 8 * HEAD_DIM
FFN_HIDDEN = ((-(-8 * D_MODEL // 3)) + 255) // 256 * 256

IN_SPLIT_SIZES = (
    A_HEADS * HEAD_DIM, A_KV_HEADS * HEAD_DIM, A_KV_HEADS * HEAD_DIM,
    B_HEADS * HEAD_DIM, B_HEADS * HEAD_DIM, B_HEADS * HEAD_DIM, B_HEADS,
    C_HEADS * HEAD_DIM, C_HEADS * HEAD_DIM, C_HEADS * HEAD_DIM,
    N_BRANCH * D_MODEL,
)
N_IN_COLS = sum(IN_SPLIT_SIZES)

kernel_name = "chunk_causal_hybrid_swa_fox_relpos_adaln"


def rms_norm(x, g):
    xf = x.astype(jnp.float32)
    y = xf * lax.rsqrt(jnp.mean(xf * xf, axis=-1, keepdims=True) + EPS)
    return (y * g.astype(jnp.float32)).astype(x.dtype)


def modulate(h, shift, scale):
    return h * (1.0 + scale[:, None, :]) + shift[:, None, :]


def alibi_slopes(n_heads):
    return jnp.exp2(-8.0 * jnp.arange(1, n_heads + 1, dtype=jnp.float32) / n_heads)


def sliding_window_sink_attention(q, k, v, sinks):
    b, s, _, dh = q.shape
    nc = s // CHUNK
    g = A_HEADS // A_KV_HEADS
    band_len = A_BAND_CHUNKS * CHUNK
    qc = q.reshape(b, nc, CHUNK, A_KV_HEADS, g, dh)

    def band(t):
        tp = jnp.pad(t, ((0, 0), (A_PREV_CHUNKS * CHUNK, 0), (0, 0), (0, 0)))
        tp = tp.reshape(b, nc + A_PREV_CHUNKS, CHUNK, A_KV_HEADS, dh)
        return jnp.concatenate([tp[:, j:j + nc] for j in range(A_BAND_CHUNKS)], axis=2)

    kb, vb = band(k), band(v)
    scores = jnp.einsum('bnqkgd,bnskd->bnkgqs', qc, kb).astype(jnp.float32) * (dh ** -0.5)
    qi = jnp.arange(CHUNK)
    si = jnp.arange(band_len)
    dist = A_PREV_CHUNKS * CHUNK + qi[:, None] - si[None, :]
    alibi = -alibi_slopes(A_HEADS).reshape(A_KV_HEADS, g, 1, 1) * jnp.abs(dist).astype(jnp.float32)
    key_chunk = jnp.arange(nc)[:, None] - A_PREV_CHUNKS + si[None, :] // CHUNK
    valid = (key_chunk >= 0)[None, :, None, None, None, :]
    scores = jnp.where(valid, scores + alibi, NEG_INF)
    sink_col = jnp.broadcast_to(sinks.astype(jnp.float32).reshape(1, 1, A_KV_HEADS, g, 1, 1),
                                scores.shape[:-1] + (1,))
    probs = jax.nn.softmax(jnp.concatenate([scores, sink_col], axis=-1), axis=-1)[..., :-1]
    out = jnp.einsum('bnkgqs,bnskd->bnqkgd', probs.astype(v.dtype), vb)
    return out.reshape(b, s, A_HEADS * dh)


def forgetting_attention(q, k, v, f_logit):
    b, s, h, dh = q.shape
    log_f = jax.nn.log_sigmoid(f_logit.astype(jnp.float32))
    cum = lax.cumsum(log_f, axis=1).transpose(0, 2, 1)
    kpos = jnp.arange(s)
    scale = dh ** -0.5

    def block(i):
        start = i * B_BLOCK
        qb = lax.dynamic_slice_in_dim(q, start, B_BLOCK, axis=1)
        cq = lax.dynamic_slice_in_dim(cum, start, B_BLOCK, axis=2)
        sc = jnp.einsum('bqhd,bshd->bhqs', qb, k).astype(jnp.float32) * scale
        sc = sc + cq[..., :, None] - cum[:, :, None, :]
        qpos = start + jnp.arange(B_BLOCK)
        sc = jnp.where(kpos[None, :] <= qpos[:, None], sc, NEG_INF)
        p = jax.nn.softmax(sc, axis=-1)
        return jnp.einsum('bhqs,bshd->bqhd', p.astype(v.dtype), v)

    out = lax.map(block, jnp.arange(s // B_BLOCK))
    return out.transpose(1, 0, 2, 3, 4).reshape(b, s, h * dh)


def chunked_relpos_attention(q, k, v, rel_table):
    b, s, h, dh = q.shape
    nc = s // CHUNK
    band_len = C_BAND_CHUNKS * CHUNK
    pad = C_PREV_CHUNKS * CHUNK
    kp = jnp.pad(k, ((0, 0), (pad, 0), (0, 0), (0, 0)))
    vp = jnp.pad(v, ((0, 0), (pad, 0), (0, 0), (0, 0)))
    qi = jnp.arange(CHUNK)
    si = jnp.arange(band_len)
    dist = pad + qi[:, None] - si[None, :]
    rel_idx = jnp.clip(dist, -REL_CLIP, REL_CLIP) + REL_CLIP
    bias = rel_table[:, rel_idx].astype(jnp.float32)
    scale = dh ** -0.5

    def chunk(n):
        qc = lax.dynamic_slice_in_dim(q, n * CHUNK, CHUNK, axis=1)
        kc = lax.dynamic_slice_in_dim(kp, n * CHUNK, band_len, axis=1)
        vc = lax.dynamic_slice_in_dim(vp, n * CHUNK, band_len, axis=1)
        sc = jnp.einsum('bqhd,bshd->bhqs', qc, kc).astype(jnp.float32) * scale + bias
        valid = (n * CHUNK - pad + si) >= 0
        sc = jnp.where(valid[None, None, None, :], sc, NEG_INF)
        p = jax.nn.softmax(sc, axis=-1)
        return jnp.einsum('bhqs,bshd->bqhd', p.astype(vc.dtype), vc)

    out = lax.map(chunk, jnp.arange(nc))
    return out.transpose(1, 0, 2, 3, 4).reshape(b, s, h * dh)


def hybrid_mixer(h, w_in, b_forget, sinks, rel_table, w_branch, w_out):
    b, s, _ = h.shape
    proj = jnp.einsum('bsd,de->bse', h, w_in)
    split_points = [int(p) for p in np.cumsum(IN_SPLIT_SIZES)[:-1]]
    qa, ka, va, qb, kb, vb, fb, qc, kc, vc, gates = jnp.split(proj, split_points, axis=-1)
    heads = lambda t, n: t.reshape(b, s, n, HEAD_DIM)
    o_a = sliding_window_sink_attention(heads(qa, A_HEADS), heads(ka, A_KV_HEADS),
                                        heads(va, A_KV_HEADS), sinks)
    o_b = forgetting_attention(heads(qb, B_HEADS), heads(kb, B_HEADS), heads(vb, B_HEADS),
                               fb + b_forget)
    o_c = chunked_relpos_attention(heads(qc, C_HEADS), heads(kc, C_HEADS), heads(vc, C_HEADS),
                                   rel_table)
    branches = jnp.stack([o_a, o_b, o_c], axis=2)
    y = jnp.einsum('bskw,kwd->bskd', branches, w_branch)
    g = jax.nn.sigmoid(gates.reshape(b, s, N_BRANCH, D_MODEL))
    merged = jnp.sum(g * y, axis=2)
    return jnp.einsum('bsd,de->bse', merged, w_out)


def swiglu(h, w_ffn_in, w_ffn_out):
    u = jnp.einsum('bsd,df->bsf', h, w_ffn_in)
    gate, up = jnp.split(u, 2, axis=-1)
    return jnp.einsum('bsf,fd->bsd', jax.nn.silu(gate) * up, w_ffn_out)


def setup_inputs(seed: int = 0) -> dict:
    key = jax.random.key(seed)
    ks = jax.random.split(key, 16)
    f32 = jnp.float32
    nrm = lambda k, shape, sd: jax.random.normal(k, shape, f32) * sd
    return {
        "x": nrm(ks[0], (BATCH, SEQ, D_MODEL), 1.0),
        "c": nrm(ks[1], (BATCH, D_MODEL), 1.0),
        "norm_mix_g": 1.0 + nrm(ks[2], (DEPTH, D_MODEL), 0.02),
        "norm_ffn_g": 1.0 + nrm(ks[3], (DEPTH, D_MODEL), 0.02),
        "w_ada": nrm(ks[4], (DEPTH, D_MODEL, 6 * D_MODEL), 0.5 * D_MODEL ** -0.5),
        "b_ada": nrm(ks[5], (DEPTH, 6 * D_MODEL), 0.02),
        "w_in": nrm(ks[6], (DEPTH, D_MODEL, N_IN_COLS), D_MODEL ** -0.5),
        "b_forget": FORGET_BIAS_INIT + nrm(ks[7], (DEPTH, B_HEADS), 0.5),
        "sinks": nrm(ks[8], (DEPTH, A_HEADS), 0.5),
        "rel_bias": nrm(ks[9], (DEPTH, C_HEADS, N_REL), 0.1),
        "w_branch": nrm(ks[10], (DEPTH, N_BRANCH, BRANCH_WIDTH, D_MODEL), BRANCH_WIDTH ** -0.5),
        "w_out": nrm(ks[11], (DEPTH, D_MODEL, D_MODEL), D_MODEL ** -0.5),
        "w_ffn_in": nrm(ks[12], (DEPTH, D_MODEL, 2 * FFN_HIDDEN), D_MODEL ** -0.5),
        "w_ffn_out": nrm(ks[13], (DEPTH, FFN_HIDDEN, D_MODEL), FFN_HIDDEN ** -0.5),
        "final_norm_g": 1.0 + nrm(ks[14], (D_MODEL,), 0.02),
    }


def reference(x, c, norm_mix_g, norm_ffn_g, w_ada, b_ada, w_in, b_forget, sinks, rel_bias,
              w_branch, w_out, w_ffn_in, w_ffn_out, final_norm_g):
    cond = jax.nn.silu(c)
    for l in range(DEPTH):
        mod = jnp.einsum('bd,de->be', cond, w_ada[l]) + b_ada[l]
        sh_m, sc_m, g_m, sh_f, sc_f, g_f = jnp.split(mod, 6, axis=-1)
        h = modulate(rms_norm(x, norm_mix_g[l]), sh_m, sc_m)
        x = x + g_m[:, None, :] * hybrid_mixer(h, w_in[l], b_forget[l], sinks[l], rel_bias[l],
                                               w_branch[l], w_out[l])
        h = modulate(rms_norm(x, norm_ffn_g[l]), sh_f, sc_f)
        x = x + g_f[:, None, :] * swiglu(h, w_ffn_in[l], w_ffn_out[l])
    return rms_norm(x, final_norm_g)
```

```python
import numpy as np
import ml_dtypes
import concourse.bass as bass
import concourse.mybir as mybir
from concourse.bass_utils import run_bass_kernel_spmd

F32 = mybir.dt.float32
BF16 = mybir.dt.bfloat16
AF = mybir.ActivationFunctionType
ALU = mybir.AluOpType

D = 1024
S = 2048
DEPTH = 2
NCH = 8
NG = 4
NT = 16
GS = 512
EPS = 1e-6
NEG = -30000.0
NU_IN = 54

ENGS = ["pe", "act", "dve", "pool", "sp"]
SAME_ENGINE_SYNC = True


class Buf:
    __slots__ = ("name", "last_write", "reads", "dma_sem", "dma_count")

    def __init__(self, name=""):
        self.name = name
        self.last_write = None
        self.reads = []
        self.dma_sem = None
        self.dma_count = 0


class Event:
    __slots__ = ("kind", "eng", "op", "sem", "value")

    def __init__(self, kind, eng=None, op=None, sem=None, value=None):
        self.kind = kind
        self.eng = eng
        self.op = op
        self.sem = sem
        self.value = value


class Op:
    __slots__ = ("eng", "fn", "deps", "needed", "seq", "count", "is_dma", "dma_sem")

    def __init__(self, eng, fn):
        self.eng = eng
        self.fn = fn
        self.deps = []
        self.needed = False
        self.seq = None
        self.count = None
        self.is_dma = False
        self.dma_sem = None


class Prog:
    def __init__(self, nc):
        self.nc = nc
        self.ops = {e: [] for e in ENGS}
        self.sems = {e: nc.alloc_semaphore("s_" + e) for e in ENGS}
        self.waited = {e: {f: -1 for f in ENGS} for e in ENGS}
        self.dma_waited = {e: {} for e in ENGS}
        self.nsem = 0

    def _add_dep(self, op, ev):
        if ev is None:
            return
        if ev.kind == "eng":
            if ev.eng == op.eng and (not SAME_ENGINE_SYNC or op.eng == "pe"):
                return
            if ev.op is op:
                return
            if self.waited[op.eng][ev.eng] >= ev.op.seq:
                return
            self.waited[op.eng][ev.eng] = ev.op.seq
            ev.op.needed = True
            op.deps.append(ev)
        else:
            key = id(ev.sem)
            if self.dma_waited[op.eng].get(key, -1) >= ev.value:
                return
            self.dma_waited[op.eng][key] = ev.value
            op.deps.append(ev)

    def op(self, eng, fn, reads=(), writes=()):
        o = Op(eng, fn)
        o.seq = len(self.ops[eng])
        for b in reads:
            self._add_dep(o, b.last_write)
        for b in writes:
            self._add_dep(o, b.last_write)
            for r in b.reads:
                self._add_dep(o, r)
        ev = Event("eng", eng=eng, op=o)
        for b in reads:
            b.reads.append(ev)
        for b in writes:
            b.last_write = ev
            b.reads = []
        self.ops[eng].append(o)
        return o

    def dma(self, eng, fns, dst, reads=(), extra_writes=()):
        if not isinstance(fns, (list, tuple)):
            fns = [fns]
        if dst.dma_sem is None:
            dst.dma_sem = self.nc.alloc_semaphore("d%d" % self.nsem)
            self.nsem += 1
        for i, fn in enumerate(fns):
            o = Op(eng, fn)
            o.is_dma = True
            o.dma_sem = dst.dma_sem
            o.seq = len(self.ops[eng])
            if i == 0:
                for b in reads:
                    self._add_dep(o, b.last_write)
                for b in [dst] + list(extra_writes):
                    self._add_dep(o, b.last_write)
                    for r in b.reads:
                        self._add_dep(o, r)
            self.ops[eng].append(o)
        dst.dma_count += len(fns)
        ev = Event("dma", sem=dst.dma_sem, value=16 * dst.dma_count)
        for b in reads:
            b.reads.append(ev)
        for b in [dst] + list(extra_writes):
            b.last_write = ev
            b.reads = []
        return ev

    def emit(self, final_waits=()):
        nc = self.nc
        for b in final_waits:
            if b.last_write is not None and b.last_write.kind == "eng":
                b.last_write.op.needed = True
        for e in ENGS:
            c = 0
            for o in self.ops[e]:
                if o.needed:
                    c += 1
                    o.count = c
        sems = self.sems

        def run(e, engobj):
            for o in self.ops[e]:
                for ev in o.deps:
                    if ev.kind == "eng":
                        engobj.wait_ge(sems[ev.eng], ev.op.count)
                    else:
                        engobj.wait_ge(ev.sem, ev.value)
                inst = o.fn(engobj)
                if o.is_dma:
                    inst.then_inc(o.dma_sem, 16)
                elif o.needed:
                    inst.then_inc(sems[e], 1)
            if e == "sp":
                for b in final_waits:
                    ev = b.last_write
                    if ev is None:
                        continue
                    if ev.kind == "dma":
                        engobj.wait_ge(ev.sem, ev.value)
                    else:
                        engobj.wait_ge(sems[ev.eng], ev.op.count)

        with nc.Block() as block:
            @block.tensor
            def _(e):
                run("pe", e)

            @block.scalar
            def _(e):
                run("act", e)

            @block.vector
            def _(e):
                run("dve", e)

            @block.gpsimd
            def _(e):
                run("pool", e)

            @block.sync
            def _(e):
                run("sp", e)


def switch_bufs(old, new):
    evs = []
    for b in old:
        if b.last_write is not None:
            evs.append(b.last_write)
        evs.extend(b.reads)
    for b in new:
        b.last_write = None
        b.reads = list(evs)


def build(nl=DEPTH, stop=None):
    nc = bass.Bass("TRN2", target_bir_lowering=False)

    def din(name, shape, dt=F32):
        return nc.dram_tensor(name, list(shape), dt, kind="ExternalInput").ap()

    x_d = din("x", [S, D])
    cT_d = din("cT", [128, 8])
    gmix_d = din("gmix", [128, DEPTH * 8])
    gffn_d = din("gffn", [128, DEPTH * 8])
    gfin_d = din("gfin", [128, 8])
    wada_d = din("wada", [DEPTH * 24, 128, 2048])
    bada_d = din("bada", [128, DEPTH * 48])
    win_d = din("win", [DEPTH * NU_IN, 128, 1024])
    wf_d = din("wf", [DEPTH, 128, 64])
    bfor_d = din("bfor", [8, DEPTH])
    sinks_d = din("sinksb", [128, DEPTH * 8])
    biasc_d = din("biasc", [DEPTH * 4, 128, 2 * 5 * 128])
    wbr_d = din("wbr", [DEPTH * 24, 128, 512])
    wout_d = din("wout", [DEPTH * 8, 128, 1024])
    wfi_d = din("wfi", [DEPTH * 22, 128, 2048])
    wfo_d = din("wfo", [DEPTH * 16, 128, 1408])
    identf_d = din("identf", [128, 128])
    identb_d = din("identb", [128, 128], BF16)
    trimask_d = din("trimask", [128, 128], BF16)
    alibi_d = din("alibi", [128, 2 * 8 * 128], BF16)
    maskc_d = din("maskc", [128, 2 * 128], BF16)
    augc_d = din("augc", [6, S], BF16)
    out_d = nc.dram_tensor("out", [S, D], F32, kind="ExternalOutput").ap()

    P = Prog(nc)

    def sb(name, shape, dt):
        return nc.alloc_sbuf_tensor("sb_" + name, list(shape), dt)

    XT = sb("XT", [128, NCH, S], F32)
    HT = sb("HT", [128, NCH, S], BF16)
    W1 = sb("W1", [128, 24576], BF16)
    NS = 4
    WS = sb("WS", [128, NS, 2048], BF16)
    RS = sb("RS", [128, S], F32)
    PT = sb("PT", [128, 8, GS], BF16)
    TMPF = sb("TMPF", [128, 3, GS], F32)
    RD = sb("RD", [128, 2, GS], F32)
    CUM3 = sb("CUM3", [72, S], BF16)
    BIASC = sb("BIASC", [128, 2, 1280], BF16)
    identf = sb("identf", [128, 128], F32)
    identb = sb("identb", [128, 128], BF16)
    trimask = sb("trimask", [128, 128], BF16)
    alibi = sb("alibi", [128, 2, 8, 128], BF16)
    maskc = sb("maskc", [128, 2, 128], BF16)
    onesb = sb("onesb", [128, 128], BF16)
    ones8 = sb("ones8", [8, 1], F32)
    cT = sb("cTs", [128, 8], F32)
    condb = sb("condb", [128, 8], BF16)
    gmix = sb("gmixs", [128, DEPTH * 8], F32)
    gffn = sb("gffns", [128, DEPTH * 8], F32)
    gfin = sb("gfins", [128, 8], F32)
    bada = sb("badas", [128, DEPTH * 48], F32)
    modT = sb("modT", [128, DEPTH, 48], F32)
    gsc = sb("gsc", [128, DEPTH, 16], F32)
    bfor = sb("bfors", [8, DEPTH], F32)
    negb = sb("negb", [8, DEPTH], F32)
    sinkb = sb("sinkbs", [128, DEPTH * 8], F32)
    expsink = sb("expsink", [128, DEPTH * 8], F32)

    PS = [nc.alloc_psum_tensor("ps%d" % i, [128, GS], F32) for i in range(8)]
    psb = [Buf("ps%d" % i) for i in range(8)]

    xt = [[Buf() for _ in range(NG)] for _ in range(NCH)]
    ht = [[Buf() for _ in range(NG)] for _ in range(NCH)]
    wsb = [Buf("ws%d" % i) for i in range(NS)]
    rsb = [Buf() for _ in range(NG)]
    ptb = [Buf() for _ in range(8)]
    tmpb = [Buf() for _ in range(3)]
    rdb = [Buf() for _ in range(2)]
    cum3b = Buf()
    biascb = [Buf(), Buf()]
    cb = Buf("consts")
    modb = [Buf() for _ in range(DEPTH)]
    modb2 = [Buf() for _ in range(DEPTH)]
    outb = Buf("out")

    sl = lambda g: slice(g * GS, (g + 1) * GS)
    tl = lambda t: slice(t * 128, (t + 1) * 128)

    state = {"ws": 0, "pt": 0, "ptn": 0, "tmp": 0, "rd": 0, "pj": 0, "gu": 0}

    def wload(src_ap, n):
        i = state["ws"] % NS
        state["ws"] += 1
        dst = WS[:, i, 0:n]
        P.dma("pool", lambda e, d=dst, s_=src_ap: e.dma_start(out=d, in_=s_, max_dma_last_dim=4096), wsb[i])
        return dst, wsb[i]

    def pe_group(insts, reads, writes):
        def fn(e, insts=insts):
            r = None
            for it in insts:
                (o, l, rr, st, sp) = it[:5]
                if len(rr.shape) == 3 and len(o.shape) == 2:
                    o = o.rearrange("p (a b) -> p a b", a=rr.shape[1])
                if len(it) > 5 and it[5]:
                    r = e.matmul(o, lhsT=l, rhs=rr, start=st, stop=sp, skip_group_check=True)
                else:
                    r = e.matmul(o, lhsT=l, rhs=rr, start=st, stop=sp)
            return r
        return P.op("pe", fn, reads=reads, writes=writes)

    def act(out, in_, func, reads, writes, bias=None, scale=None):
        kw = {}
        if bias is not None:
            kw["bias"] = bias
        if scale is not None:
            kw["scale"] = scale
        return P.op("act", lambda e, o=out, i=in_, f=func, kw=kw: e.activation(out=o, in_=i, func=f, **kw),
                    reads=reads, writes=writes)

    def dve_tt(out, in0, in1, op, reads, writes):
        return P.op("dve", lambda e, o=out, a=in0, b=in1, op=op: e.tensor_tensor(out=o, in0=a, in1=b, op=op),
                    reads=reads, writes=writes)

    def dve_stt(out, in0, scalar, in1, op0, op1, reads, writes):
        return P.op("dve", lambda e, o=out, a=in0, s_=scalar, b=in1, p0=op0, p1=op1:
                    e.scalar_tensor_tensor(out=o, in0=a, scalar=s_, in1=b, op0=p0, op1=p1),
                    reads=reads, writes=writes)

    def dve_ts(out, in0, s1, s2, op0, op1, reads, writes, eng="dve"):
        if op1 is None:
            return P.op(eng, lambda e, o=out, a=in0, s1=s1, p0=op0: e.tensor_scalar(out=o, in0=a, scalar1=s1, scalar2=None, op0=p0),
                        reads=reads, writes=writes)
        return P.op(eng, lambda e, o=out, a=in0, s1=s1, s2=s2, p0=op0, p1=op1:
                    e.tensor_scalar(out=o, in0=a, scalar1=s1, scalar2=s2, op0=p0, op1=p1),
                    reads=reads, writes=writes)

    def copy(eng, out, in_, reads, writes):
        if eng == "act":
            return act(out, in_, AF.Copy, reads, writes)
        return P.op(eng, lambda e, o=out, i=in_: e.tensor_copy(out=o, in_=i), reads=reads, writes=writes)

    def next_rot(key, n):
        i = state[key] % n
        state[key] += 1
        return i

    small_loads = [
        (identf[:, :], identf_d[:, :]), (identb[:, :], identb_d[:, :]), (trimask[:, :], trimask_d[:, :]),
        (alibi[:, :, :, :].rearrange("p a h q -> p (a h q)"), alibi_d[:, :]),
        (maskc[:, :, :].rearrange("p a q -> p (a q)"), maskc_d[:, :]),
        (cT[:, :], cT_d[:, :]), (gmix[:, :], gmix_d[:, :]), (gffn[:, :], gffn_d[:, :]), (gfin[:, :], gfin_d[:, :]),
        (bada[:, :], bada_d[:, :]), (bfor[:, :], bfor_d[:, :]), (sinkb[:, :], sinks_d[:, :]),
    ]
    P.dma("sp", [lambda e, o=o, i=i: e.dma_start(out=o, in_=i) for (o, i) in small_loads], cb)
    onesbuf = Buf()
    P.op("pool", lambda e: e.memset(onesb[:, :], 1.0), writes=[onesbuf])
    P.op("pool", lambda e: e.memset(ones8[:, :], 1.0), writes=[onesbuf])
    cvb = Buf()
    act(condb[:, :], cT[:, :], AF.Silu, [cb], [cvb])
    act(expsink[:, :], sinkb[:, :], AF.Exp, [cb], [cvb])
    dve_ts(negb[:, :], bfor[:, :], -1.0, None, ALU.mult, None, [cb], [cvb])

    W1f = W1[:, :].bitcast(F32)
    xsb = [Buf() for _ in range(4)]
    w1_bufs = list(xsb)
    for t in range(NT):
        s_ = t % 4
        xs = W1f[:, s_ * 1024:(s_ + 1) * 1024]
        P.dma("sp", lambda e, o=xs, t=t: e.dma_start(out=o, in_=x_d[t * 128:(t + 1) * 128, :]), xsb[s_])
        for half in range(2):
            bi = (2 * t + half) % 4
            insts = [(PS[bi][:, q * 128:(q + 1) * 128], xs[:, (half * 4 + q) * 128:(half * 4 + q + 1) * 128]) for q in range(4)]

            def tfn(e, insts=insts):
                r = None
                for (o, i) in insts:
                    r = e.transpose(out=o, in_=i, identity=identf[:, :])
                return r
            P.op("pe", tfn, reads=[xsb[s_], cb], writes=[psb[bi]])
            copy("dve" if half == 0 else "act",
                 XT[:, half * 4:half * 4 + 4, tl(t)],
                 PS[bi][:, :].rearrange("p (c t) -> p c t", c=4),
                 [psb[bi]], [xt[c][t // 4] for c in range(half * 4, half * 4 + 4)])

    def ada_block(l, jb):
        modps = PS[7]
        wv, wb = wload(wada_d[l * 24 + jb], 2048)
        wv3 = wv.rearrange("p (k c) -> p k c", k=8)
        for jj in range(2):
            j = 2 * jb + jj
            insts = [(modps[:, j:j + 1], wv3[:, kc, jj * 128:(jj + 1) * 128], condb[:, kc:kc + 1], kc == 0, kc == 7)
                     for kc in range(8)]
            pe_group(insts, [wb, cvb], [psb[7]])

    def ada_finish(l, part):
        modps = PS[7]
        if part == 0:
            dve_tt(modT[:, l, 0:16], modps[:, 0:16], bada[:, l * 48:l * 48 + 16], ALU.add, [psb[7], cb], [modb[l]])
            dve_stt(gsc[:, l, 0:8], modT[:, l, 8:16], 1.0, gmix[:, l * 8:(l + 1) * 8], ALU.add, ALU.mult, [modb[l], cb], [modb[l]])
        else:
            dve_tt(modT[:, l, 16:48], modps[:, 16:48], bada[:, l * 48 + 16:(l + 1) * 48], ALU.add, [psb[7], cb], [modb2[l]])
            dve_stt(gsc[:, l, 8:16], modT[:, l, 32:40], 1.0, gffn[:, l * 8:(l + 1) * 8], ALU.add, ALU.mult, [modb2[l], cb], [modb2[l]])

    ada_rest = []

    def ada_first(l):
        for jb in range(8):
            ada_block(l, jb)
        ada_finish(l, 0)
        ada_rest.extend(range(8, 24))

    def ada_more(l, n):
        for _ in range(n):
            if ada_rest:
                ada_block(l, ada_rest.pop(0))
                if not ada_rest:
                    ada_finish(l, 1)

    def norm_group(g, gs_ap, sh_ap, vec_bufs, dst):
        bi = 5 + (g % 2)
        for c in range(NCH):
            i = next_rot("ptn", 4)
            if c % 2 == 0:
                act(PT[:, i, :], XT[:, c, sl(g)], AF.Square, [xt[c][g]], [ptb[i]])
            else:
                P.op("pool", lambda e, i=i, c=c, g=g: e.tensor_tensor(out=PT[:, i, :], in0=XT[:, c, sl(g)], in1=XT[:, c, sl(g)],
                                                                     op=ALU.mult), reads=[xt[c][g]], writes=[ptb[i]])
            pe_group([(PS[bi][:, :], onesb[:, :], PT[:, i, :], c == 0, c == NCH - 1)], [ptb[i], onesbuf], [psb[bi]])
        ti = next_rot("tmp", 3)
        act(TMPF[:, ti, :], PS[bi][:, :], AF.Ln, [psb[bi]], [tmpb[ti]], bias=EPS, scale=1.0 / D)
        act(RS[:, sl(g)], TMPF[:, ti, :], AF.Exp, [tmpb[ti]], [rsb[g]], scale=-0.5)
        for c in range(NCH):
            ti = next_rot("tmp", 3)
            dve_stt(TMPF[:, ti, :], XT[:, c, sl(g)], gs_ap[:, c:c + 1], RS[:, sl(g)], ALU.mult, ALU.mult,
                    [xt[c][g], rsb[g]] + vec_bufs, [tmpb[ti]])
            if dst is None:
                act(HT[:, c, sl(g)], TMPF[:, ti, :], AF.Identity, [tmpb[ti]] + vec_bufs, [ht[c][g]],
                    bias=sh_ap[:, c:c + 1], scale=1.0)
            else:
                dst(c, g, ti)

    def norm(gs_ap, sh_ap, vec_bufs, dst=None, after_group=None, skew=1):
        for g in range(NG):
            norm_group(g, gs_ap, sh_ap, vec_bufs, dst)
            if after_group is not None and g - skew >= 0:
                after_group(g - skew)
        if after_group is not None:
            for g in range(max(0, NG - skew), NG):
                after_group(g)

    def proj_fm(wv3, wb, g, nk=8, src=None, srcb=None):
        bi = 5 + next_rot("pj", 2 if ada_rest else 3)
        if src is None:
            src, srcb = HT, ht
        insts = [(PS[bi][:, :], wv3[:, kc, :], src[:, kc, sl(g)], kc == 0, kc == nk - 1) for kc in range(nk)]
        pe_group(insts, [wb] + [srcb[kc][g] for kc in range(nk)], [psb[bi]])
        return bi

    def proj_v(wv3, wb, t4, VB, vbufs):
        bi = 5 + next_rot("pj", 2 if ada_rest else 3)
        insts = []
        for q in range(4):
            t = t4 * 4 + q
            for kc in range(8):
                insts.append((PS[bi][:, q * 128:(q + 1) * 128], HT[:, kc, tl(t)], wv3[:, kc, :], kc == 0, kc == 7))
        pe_group(insts, [wb] + [ht[kc][t4] for kc in range(8)], [psb[bi]])
        pv = PS[bi][:, :].rearrange("p (q c) -> p q c", q=4)
        copy("dve", VB[:, t4 * 4:t4 * 4 + 4, 0:64], pv[:, :, 0:64], [psb[bi]], [vbufs[t4]])
        copy("dve", VB[:, t4 * 4:t4 * 4 + 4, 128:192], pv[:, :, 64:128], [psb[bi]], [vbufs[t4]])

    def attn_finish(obi, lo_num, out_ap, out_bufs, extra_reads, c0=0, sink_cols=None):
        num = slice(lo_num, lo_num + 64)
        den = slice(64 - lo_num, 128 - lo_num)
        ri = next_rot("rd", 2)
        if sink_cols is None:
            act(RD[num, ri, c0:], PS[obi][den, c0:], AF.Ln, [psb[obi]], [rdb[ri]])
        else:
            for hq in range(4):
                act(RD[num, ri, hq * 128:(hq + 1) * 128], PS[obi][den, hq * 128:(hq + 1) * 128], AF.Ln,
                    [psb[obi], cvb], [rdb[ri]], bias=expsink[num, sink_cols[hq]:sink_cols[hq] + 1], scale=1.0)
        act(RD[num, ri, c0:], RD[num, ri, c0:], AF.Exp, [rdb[ri]], [rdb[ri]], scale=-1.0)
        if sink_cols is None:
            dve_tt(out_ap, PS[obi][num, c0:], RD[num, ri, c0:], ALU.mult, [psb[obi], rdb[ri]] + extra_reads, out_bufs)
        else:
            dve_tt(out_ap, PS[obi][num, :].rearrange("p (h q) -> p h q", h=4),
                   RD[num, ri, :].rearrange("p (h q) -> p h q", h=4), ALU.mult,
                   [psb[obi], rdb[ri]] + extra_reads, out_bufs)

    OT = W1[:, 0:8192].rearrange("p (c t) -> p c t", c=4)
    MGR = W1[:, 8192:24576]
    MG = MGR.rearrange("p (c t) -> p c t", c=8)
    region = {"mg": list(w1_bufs), "ot": []}
    ot = [[Buf() for _ in range(NG)] for _ in range(4)]
    switch_bufs(w1_bufs, [b for r in ot for b in r])
    region["w1all"] = None

    def mg_switch(new):
        switch_bufs(region["mg"], new)
        region["mg"] = new

    def merge_branch(l, k):
        mg = [[Buf() for _ in range(NG)] for _ in range(NCH)]
        mg_switch([b for r in mg for b in r])
        for dc in range(NCH):
            wg, wgb = wload(win_d[l * NU_IN + 30 + k * 8 + dc], 1024)
            wg3 = wg.rearrange("p (k c) -> p k c", k=8)
            wbv, wbb = wload(wbr_d[l * 24 + k * 8 + dc], 512)
            wb3 = wbv.rearrange("p (k c) -> p k c", k=4)
            for g in range(NG):
                yb = g % 2
                gb = 2 + (g % 2)
                pe_group([(PS[yb][:, :], wb3[:, kc, :], OT[:, kc, sl(g)], kc == 0, kc == 3) for kc in range(4)],
                         [wbb] + [ot[kc][g] for kc in range(4)], [psb[yb]])
                pe_group([(PS[gb][:, :], wg3[:, kc, :], HT[:, kc, sl(g)], kc == 0, kc == 7) for kc in range(8)],
                         [wgb] + [ht[kc][g] for kc in range(8)], [psb[gb]])
                ti = next_rot("tmp", 3)
                act(TMPF[:, ti, :], PS[gb][:, :], AF.Sigmoid, [psb[gb]], [tmpb[ti]])
                dve_tt(MG[:, dc, sl(g)], PS[yb][:, :], TMPF[:, ti, :], ALU.mult, [psb[yb], tmpb[ti]], [mg[dc][g]])
        for dco in range(NCH):
            wo, wob = wload(wout_d[l * 8 + dco], 1024)
            wo3 = wo.rearrange("p (k c) -> p k c", k=8)
            for g in range(NG):
                bi = 4 + next_rot("pj", 4)
                pe_group([(PS[bi][:, :], wo3[:, kc, :], MG[:, kc, sl(g)], kc == 0, kc == 7) for kc in range(8)],
                         [wob] + [mg[kc][g] for kc in range(8)], [psb[bi]])
                dve_stt(XT[:, dco, sl(g)], PS[bi][:, :], modT[:, l, 16 + dco:17 + dco], XT[:, dco, sl(g)], ALU.mult, ALU.add,
                        [psb[bi], modb2[l], xt[dco][g]], [xt[dco][g]])

    SBANKS = [0, 1, 2, 5, 6, 7]
    LOOK = 4

    def s_exp_pv(k_insts, nk_reads, c0, ncols, obi, v_lhsT, v_reads, first, last, pending, skip=False):
        sbl = state["sbanks"]
        sbi = sbl[next_rot("sb", len(sbl))]
        insts = [(PS[sbi][:, c0:c0 + ncols] if o is None else o, l_, r_, st, sp) for (o, l_, r_, st, sp) in k_insts(sbi)]
        pe_group(insts, nk_reads, [psb[sbi]])
        pi = next_rot("pt", 8)
        act(PT[:, pi, c0:c0 + ncols], PS[sbi][:, c0:c0 + ncols], AF.Exp, [psb[sbi]], [ptb[pi]])
        pending.append(("pv", [(PS[obi][:, c0:c0 + ncols], v_lhsT, PT[:, pi, c0:c0 + ncols], first, last, skip)],
                        [ptb[pi]] + v_reads, [psb[obi]]))

    def push_fin(pending, fn):
        pending.append(("fin", fn))

    def flush(pending, keep):
        def npv():
            return sum(1 for it in pending if it[0] == "pv")
        while pending and (npv() > keep or pending[0][0] == "fin"):
            it = pending.pop(0)
            if it[0] == "pv":
                pe_group(it[1], it[2], it[3])
            else:
                it[1]()

    state["sb"] = 0
    state["ob"] = 0
    state["oba"] = 0
    state["sbanks"] = SBANKS

    def branch_A(l):
        QA = MGR[:, 0:8192].rearrange("p (c t) -> p c t", c=4)
        KA = [MGR[:, 8192:10240], MGR[:, 10240:12288]]
        VA = MGR[:, 12288:15360].rearrange("p (t c) -> p t c", t=16)
        qab = [[Buf() for _ in range(NG)] for _ in range(4)]
        kab = [[Buf() for _ in range(NG)] for _ in range(2)]
        vab = [Buf() for _ in range(4)]
        vones = Buf()
        mg_switch([b for r in qab for b in r] + [b for r in kab for b in r] + vab + [vones])
        P.op("pool", lambda e: e.memset(VA[:, :, 64:128], 1.0), writes=[vones] + vab)
        for kv in range(2):
            P.op("pool", lambda e, kv=kv: e.memset(KA[kv], 0.0), writes=kab[kv])
        base = l * NU_IN

        def qproj(c, g, wv3, wb):
            bi = proj_fm(wv3, wb, g)
            dve_ts(QA[:, c, sl(g)], PS[bi][:, :], 0.125, None, ALU.mult, None, [psb[bi]], [qab[c][g]])
        pro = []
        for c in range(3):
            wv, wb = wload(win_d[base + c], 1024)
            pro.append((c, wv.rearrange("p (k c) -> p k c", k=8), wb))

        def after_group(g):
            for (c, wv3, wb) in pro:
                qproj(c, g, wv3, wb)
        norm(gsc[:, l, 0:8], modT[:, l, 0:8], [modb[l]], after_group=after_group)
        ada_more(l, 7)
        wv, wb = wload(win_d[base + 3], 1024)
        wv3 = wv.rearrange("p (k c) -> p k c", k=8)
        for g in range(NG):
            qproj(3, g, wv3, wb)
        ada_more(l, 3)
        wv, wb = wload(win_d[base + 4], 1024)
        wv3 = wv.rearrange("p (k c) -> p k c", k=8)
        for g in range(NG):
            bi = proj_fm(wv3, wb, g)
            copy("dve", KA[0][0:64, sl(g)], PS[bi][0:64, :], [psb[bi]], [kab[0][g]])
            copy("dve", KA[1][64:128, sl(g)], PS[bi][64:128, :], [psb[bi]], [kab[1][g]])
        ada_more(l, 3)
        wv, wb = wload(win_d[base + 5], 1024)
        wv3 = wv.rearrange("p (k c) -> p k c", k=8)
        for t4 in range(4):
            proj_v(wv3, wb, t4, VA, vab)
        ada_more(l, 99)
        pending = []
        state["sbanks"] = [0, 1, 2, 5]
        obl = [3, 4, 6, 7]
        for m in range(NT):
            for kv in range(2):
                rows = slice(kv * 64, kv * 64 + 64)
                obi = obl[next_rot("oba", len(obl))]
                vsl = slice(0, 128) if kv == 0 else slice(64, 192)
                kts = [m] if m == 0 else [m - 1, m]
                for ji, j in enumerate(kts):
                    kind = m - j

                    def k_insts(sbi, j=j, kind=kind, rows=rows, kv=kv, m=m):
                        return [
                            (None, KA[kv][:, tl(j)], QA[:, :, tl(m)], True, False),
                            (None, identb[:, :], alibi[:, kind, kv * 4:kv * 4 + 4, :], False, True),
                        ]
                    s_exp_pv(k_insts, [kab[kv][j // 4], cb] + [qab[c][m // 4] for c in range(4)], 0, GS, obi,
                             VA[:, j, vsl], [vab[j // 4], vones], ji == 0, ji == len(kts) - 1, pending)
                    flush(pending, LOOK)
                sink_cols = [l * 8 + kv * 4 + hq for hq in range(4)]
                push_fin(pending, lambda obi=obi, kv=kv, rows=rows, m=m, sink_cols=sink_cols:
                         attn_finish(obi, kv * 64, OT[rows, :, tl(m)], [ot[c][m // 4] for c in range(4)], [], sink_cols=sink_cols))
        flush(pending, 0)
        state["sbanks"] = SBANKS

    def branch_BC(l, which):
        is_b = which == "B"
        base = l * NU_IN + (6 if is_b else 18)
        QK = [MGR[:, i * 2048:(i + 1) * 2048] for i in range(4)]
        VB = MGR[:, 8192:11264].rearrange("p (t c) -> p t c", t=16)
        qb = [[Buf() for _ in range(NG)] for _ in range(4)]
        augb = [Buf() for _ in range(4)]
        vbb = [Buf() for _ in range(4)]
        vones = Buf()
        mg_switch([b for r in qb for b in r] + augb + vbb + [vones])
        P.op("pool", lambda e: e.memset(VB[:, :, 64:128], 1.0), writes=[vones] + vbb)
        if is_b:
            wfv, wfb = wload(wf_d[l], 64)
            wf3 = wfv.rearrange("p (k c) -> p k c", k=8)
            A8 = RS[0:8, :]
            for g in range(NG):
                bi = 5 + next_rot("pj", 3)
                pe_group([(PS[bi][0:8, :], wf3[:, kc, :], HT[:, kc, sl(g)], kc == 0, kc == 7) for kc in range(8)],
                         [wfb] + [ht[kc][g] for kc in range(8)], [psb[bi]])
                act(A8[:, sl(g)], PS[bi][0:8, :], AF.Exp, [psb[bi], cvb], [rsb[g]], bias=negb[:, l:l + 1], scale=-1.0)
                act(A8[:, sl(g)], A8[:, sl(g)], AF.Ln, [rsb[g]], [rsb[g]], bias=1.0, scale=1.0)
            P.op("dve", lambda e: e.tensor_tensor_scan(out=A8, data0=ones8[:, 0:1].to_broadcast([8, S]), data1=A8,
                                                       initial=0.0, op0=ALU.mult, op1=ALU.subtract),
                 reads=rsb + [onesbuf], writes=rsb)
            TB = PT[0:8, 0:4, :].rearrange("p a b -> p (a b)")
            copy("dve", CUM3[0:8, :], A8, rsb, [cum3b])
            dve_tt(A8, A8, CUM3[0:8, :], ALU.subtract, rsb + [cum3b], rsb)
            copy("dve", TB, A8, rsb, ptb)
            dve_tt(A8, A8, TB, ALU.subtract, rsb + ptb[0:4], rsb)
            copy("act", CUM3[32:40, :], TB, ptb[0:4], [cum3b])
            copy("dve", TB, A8, rsb, ptb)
            copy("act", CUM3[64:72, :], TB, ptb[0:4], [cum3b])
        for i in range(4):
            P.op("pool", lambda e, i=i: e.memset(QK[i], 0.0), writes=qb[i] + [augb[i]])
        if is_b:
            cr = [(0, 67, 0), (1, 3, 0), (2, 64, 3), (3, 0, 3)]
            for (i, r0, a0) in cr:
                P.dma("sp", lambda e, i=i, r0=r0, a0=a0: e.dma_start(out=QK[i][r0:r0 + 3, :], in_=augc_d[a0:a0 + 3, :]), augb[i])
        for p in range(4):
            if not is_b:
                bs = p % 2
                P.dma("pool", lambda e, bs=bs, p=p: e.dma_start(out=BIASC[:, bs, :], in_=biasc_d[l * 4 + p], max_dma_last_dim=4096),
                      biascb[bs])
            if is_b:
                he, ho = 2 * p, 2 * p + 1
                mv = [(0, 64, he), (1, 0, ho), (2, 67, he), (3, 3, ho)]
                for (i, r0, h) in mv:
                    P.dma("sp", [lambda e, i=i, r0=r0, h=h, q=q: e.dma_start(out=QK[i][r0 + q:r0 + q + 1, :],
                                                                             in_=CUM3[32 * q + h:32 * q + h + 1, :])
                                 for q in range(3)], augb[i], reads=[cum3b])
            wq, wqb = wload(win_d[base + p * 3 + 0], 1024)
            wq3 = wq.rearrange("p (k c) -> p k c", k=8)
            for g in range(NG):
                bi = proj_fm(wq3, wqb, g)
                dve_ts(QK[0][0:64, sl(g)], PS[bi][0:64, :], 0.125, None, ALU.mult, None, [psb[bi]], [qb[0][g]])
                dve_ts(QK[1][64:128, sl(g)], PS[bi][64:128, :], 0.125, None, ALU.mult, None, [psb[bi]], [qb[1][g]])
            wk, wkb = wload(win_d[base + p * 3 + 1], 1024)
            wk3 = wk.rearrange("p (k c) -> p k c", k=8)
            for g in range(NG):
                bi = proj_fm(wk3, wkb, g)
                copy("dve", QK[2][0:64, sl(g)], PS[bi][0:64, :], [psb[bi]], [qb[2][g]])
                copy("dve", QK[3][64:128, sl(g)], PS[bi][64:128, :], [psb[bi]], [qb[3][g]])
            wv, wvb = wload(win_d[base + p * 3 + 2], 1024)
            wv3 = wv.rearrange("p (k c) -> p k c", k=8)
            for t4 in range(4):
                proj_v(wv3, wvb, t4, VB, vbb)
            pending = []
            for hh in range(2):
                rows = slice(hh * 64, hh * 64 + 64)
                vsl = slice(0, 128) if hh == 0 else slice(64, 192)
                for g in range(NG):
                    obi = 3 + next_rot("ob", 2)
                    if is_b:
                        kts = list(range(0, 4 * g + 4))
                    else:
                        kts = list(range(max(0, 4 * g - 4), 4 * g + 4))
                    for ji, j in enumerate(kts):
                        if is_b:
                            r = j - 4 * g
                            c0 = 128 * r if r > 0 else 0
                            ncols = GS - c0
                            Qt, Kt = QK[hh], QK[2 + hh]

                            def k_insts(sbi, j=j, r=r, c0=c0, ncols=ncols, Qt=Qt, Kt=Kt, g=g):
                                ins = [(None, Kt[:, tl(j)], Qt[:, g * GS + c0:(g + 1) * GS], True, r < 0)]
                                if r >= 0:
                                    ins.append((PS[sbi][:, c0:c0 + 128], identb[:, :], trimask[:, :], False, True))
                                return ins
                            kreads = [qb[hh][g], qb[2 + hh][j // 4], augb[hh], augb[2 + hh], cb]
                        else:
                            m0 = max(j, 4 * g)
                            m1 = min(j + 4, 4 * g + 3)
                            c0 = (m0 - 4 * g) * 128
                            ncols = (m1 - m0 + 1) * 128
                            k0 = m0 - j
                            k1 = m1 - j
                            bs = p % 2

                            def k_insts(sbi, j=j, c0=c0, ncols=ncols, k0=k0, k1=k1, rows=rows, g=g, hh=hh, bs=bs):
                                ins = [(None, QK[2 + hh][:, tl(j)], QK[hh][:, g * GS + c0:g * GS + c0 + ncols], True, False)]
                                has0 = (k0 == 0)
                                has4 = (k1 == 4)
                                ins.append((None, identb[:, :], BIASC[:, bs, hh * 640 + k0 * 128:hh * 640 + (k1 + 1) * 128],
                                            False, not (has0 or has4)))
                                if has0:
                                    ins.append((PS[sbi][:, c0:c0 + 128], identb[:, :], maskc[:, 0, :], False, not has4))
                                if has4:
                                    ins.append((PS[sbi][:, c0 + ncols - 128:c0 + ncols], identb[:, :], maskc[:, 1, :], False, True))
                                return ins
                            kreads = [qb[hh][g], qb[2 + hh][j // 4], biascb[bs], cb]
                        s_exp_pv(k_insts, kreads, c0, ncols, obi, VB[:, j, vsl], [vbb[j // 4], vones],
                                 ji == 0, ji == len(kts) - 1, pending, skip=not is_b)
                        flush(pending, LOOK)
                    push_fin(pending, lambda obi=obi, hh=hh, rows=rows, p=p, g=g:
                             attn_finish(obi, hh * 64, OT[rows, p, sl(g)], [ot[p][g]], []))
            flush(pending, 0)

    def ffn(l, next_ada):
        AT = W1[:, 0:22528].rearrange("p (j t) -> p j t", j=11)
        ada_todo = list(range(24)) if next_ada else []
        for J in range(2):
            at = [[Buf() for _ in range(NG)] for _ in range(11)]
            allb = [b for r in at for b in r]
            if J == 0:
                switch_bufs(region["mg"] + [b for r in ot for b in r], allb)
            else:
                switch_bufs(region["mg"], allb)
            region["mg"] = allb
            def gu_tile(jj, g, w4, wb, at=at):
                r2 = next_rot("gu", 2)
                gb = r2
                ub = 2 + r2
                pe_group([(PS[gb][:, :], w4[:, 0, kc, :], HT[:, kc, sl(g)], kc == 0, kc == 7) for kc in range(8)],
                         [wb] + [ht[kc][g] for kc in range(8)], [psb[gb]])
                pe_group([(PS[ub][:, :], w4[:, 1, kc, :], HT[:, kc, sl(g)], kc == 0, kc == 7) for kc in range(8)],
                         [wb] + [ht[kc][g] for kc in range(8)], [psb[ub]])
                ti = next_rot("tmp", 3)
                act(TMPF[:, ti, :], PS[gb][:, :], AF.Silu, [psb[gb]], [tmpb[ti]])
                dve_tt(AT[:, jj, sl(g)], PS[ub][:, :], TMPF[:, ti, :], ALU.mult, [psb[ub], tmpb[ti]], [at[jj][g]])

            def ada_step(j):
                for _ in range(2 if j < 2 else 1):
                    if ada_todo:
                        ada_block(l + 1, ada_todo.pop(0))
            jstart = 0
            if J == 0:
                pro = []
                for jj in range(3):
                    wv, wb = wload(wfi_d[l * 22 + jj], 2048)
                    pro.append((jj, wv.rearrange("p (a k c) -> p a k c", a=2, k=8), wb))

                def after_group(g, pro=pro):
                    for (jj, w4, wb) in pro:
                        gu_tile(jj, g, w4, wb)
                norm(gsc[:, l, 8:16], modT[:, l, 24:32], [modb2[l]], after_group=after_group)
                for jj in range(3):
                    ada_step(jj)
                jstart = 3
            for jj in range(jstart, 11):
                j = J * 11 + jj
                wv, wb = wload(wfi_d[l * 22 + j], 2048)
                w4 = wv.rearrange("p (a k c) -> p a k c", a=2, k=8)
                for g in range(NG):
                    gu_tile(jj, g, w4, wb)
                ada_step(j)
            for dco in range(NCH):
                wo, wob = wload(wfo_d[l * 16 + J * 8 + dco], 1408)
                wo3 = wo.rearrange("p (j c) -> p j c", j=11)
                for g in range(NG):
                    bi = 4 + next_rot("pj", 3)
                    pe_group([(PS[bi][:, :], wo3[:, jj, :], AT[:, jj, sl(g)], jj == 0, jj == 10) for jj in range(11)],
                             [wob] + [at[jj][g] for jj in range(11)], [psb[bi]])
                    dve_stt(XT[:, dco, sl(g)], PS[bi][:, :], modT[:, l, 40 + dco:41 + dco], XT[:, dco, sl(g)], ALU.mult, ALU.add,
                            [psb[bi], modb2[l], xt[dco][g]], [xt[dco][g]])
        if next_ada:
            assert not ada_todo
            ada_finish(l + 1, 0)
            ada_finish(l + 1, 1)
        newot = [b for r in ot for b in r]
        dummy = [Buf()]
        switch_bufs(region["mg"], newot + dummy)
        region["mg"] = dummy

    for l in range(nl):
        if l == 0:
            ada_first(l)
        branch_A(l)
        merge_branch(l, 0)
        if stop == "a%d" % l:
            break
        branch_BC(l, "B")
        merge_branch(l, 1)
        if stop == "b%d" % l:
            break
        branch_BC(l, "C")
        merge_branch(l, 2)
        if stop == "m%d" % l:
            break
        ffn(l, l + 1 < nl)

    fin_bufs = [Buf() for _ in range(3)]
    outbs = [Buf() for _ in range(3)]
    switch_bufs(region["mg"] + [b for r in ot for b in r], fin_bufs)
    YS = [W1f[:, i * 1024:(i + 1) * 1024] for i in range(3)]
    def out_tiles(g):
        for t in range(4 * g, 4 * g + 4):
            si = t % 3
            for half in range(2):
                bi = (2 * t + half) % 4
                insts = [(PS[bi][:, q * 128:(q + 1) * 128], XT[:, half * 4 + q, tl(t)]) for q in range(4)]

                def tfn(e, insts=insts):
                    r = None
                    for (o, i) in insts:
                        r = e.transpose(out=o, in_=i, identity=identf[:, :])
                    return r
                P.op("pe", tfn, reads=[xt[half * 4 + q][t // 4] for q in range(4)] + [cb], writes=[psb[bi]])
                copy("dve" if half == 0 else "act", YS[si][:, half * 512:(half + 1) * 512], PS[bi][:, :], [psb[bi]], [fin_bufs[si]])
            P.dma("sp", lambda e, t=t, si=si: e.dma_start(out=out_d[t * 128:(t + 1) * 128, :], in_=YS[si]), outbs[si], reads=[fin_bufs[si]])

    if stop is None:
        def dst(c, g, ti):
            copy("act", XT[:, c, sl(g)], TMPF[:, ti, :], [tmpb[ti]], [xt[c][g]])
        norm(gfin, None, [cb], dst=dst, after_group=out_tiles)
    else:
        for g in range(NG):
            out_tiles(g)
    P.emit(final_waits=outbs)
    return nc


def _win_cols():
    units = []
    for c in range(4):
        units.append(list(range(c * 64, c * 64 + 64)) + list(range((c + 4) * 64, (c + 4) * 64 + 64)))
    units.append(list(range(512, 640)))
    units.append(list(range(640, 768)))
    for p in range(4):
        units.append(list(range(768 + p * 128, 768 + (p + 1) * 128)))
        units.append(list(range(1280 + p * 128, 1280 + (p + 1) * 128)))
        units.append(list(range(1792 + p * 128, 1792 + (p + 1) * 128)))
    for p in range(4):
        units.append(list(range(2312 + p * 128, 2312 + (p + 1) * 128)))
        units.append(list(range(2824 + p * 128, 2824 + (p + 1) * 128)))
        units.append(list(range(3336 + p * 128, 3336 + (p + 1) * 128)))
    for k in range(3):
        for dc in range(8):
            units.append(list(range(3848 + k * 1024 + dc * 128, 3848 + k * 1024 + (dc + 1) * 128)))
    assert len(units) == NU_IN
    return np.array(units)


def _const_tables():
    bf = ml_dtypes.bfloat16
    identf = np.eye(128, dtype=np.float32)
    identb = np.eye(128).astype(bf)
    s = np.arange(128)[:, None]
    q = np.arange(128)[None, :]
    trimask = np.where(s > q, NEG, 0.0).astype(bf)
    slopes = 2.0 ** (-(np.arange(1, 9)))
    alibi = np.zeros((128, 2, 8, 128), np.float32)
    for kind in range(2):
        dist = 128 * kind + q - s
        msk = ((s >= 64) & (q < 64)) if kind == 0 else ((s < 64) & (q >= 64))
        for h in range(8):
            alibi[:, kind, h, :] = np.where(msk, NEG, -slopes[h] * np.abs(dist))
    maskc = np.zeros((128, 2, 128), np.float32)
    maskc[:, 0, :] = np.where((s >= 64) & (q < 64), NEG, 0.0)
    maskc[:, 1, :] = np.where((s < 64) & (q >= 64), NEG, 0.0)
    augc = np.concatenate([-np.ones((3, S), np.float32), np.ones((3, S), np.float32)], 0)
    return dict(identf=identf, identb=identb, trimask=trimask, alibi=alibi.reshape(128, -1).astype(bf),
                maskc=maskc.reshape(128, -1).astype(bf), augc=augc.astype(bf))


def _prep_shared(inp):
    f32 = np.float32
    vecT = lambda v: np.ascontiguousarray(v.reshape(-1, 8, 128).transpose(2, 0, 1).reshape(128, -1)).astype(f32)
    sh = {}
    sh["gmix"] = vecT(inp["norm_mix_g"])
    sh["gffn"] = vecT(inp["norm_ffn_g"])
    sh["gfin"] = vecT(inp["final_norm_g"][None])
    w_ada = inp["w_ada"]
    sh["wada"] = np.ascontiguousarray(
        w_ada.reshape(DEPTH, 8, 128, 24, 256).transpose(0, 3, 2, 1, 4).reshape(DEPTH * 24, 128, 2048))
    sh["bada"] = np.ascontiguousarray(inp["b_ada"].reshape(DEPTH, 48, 128).transpose(2, 0, 1).reshape(128, DEPTH * 48))
    cols = _win_cols()
    w_in = inp["w_in"]
    wu = w_in[:, :, cols]
    sh["win"] = np.ascontiguousarray(
        wu.reshape(DEPTH, 8, 128, NU_IN, 128).transpose(0, 3, 2, 1, 4).reshape(DEPTH * NU_IN, 128, 1024))
    wfc = w_in[:, :, 2304:2312]
    sh["wf"] = np.ascontiguousarray(wfc.reshape(DEPTH, 8, 128, 8).transpose(0, 2, 1, 3).reshape(DEPTH, 128, 64))
    sh["bfor"] = np.ascontiguousarray(inp["b_forget"].T)
    sh["sinksb"] = np.ascontiguousarray(np.broadcast_to(inp["sinks"].reshape(1, DEPTH * 8), (128, DEPTH * 8)))
    rb = inp["rel_bias"]
    s = np.arange(128)[:, None]
    q = np.arange(128)[None, :]
    idx = np.stack([np.clip(128 * kind + q - s, -128, 128) + 128 for kind in range(5)], 0)
    tb = rb[:, :, idx]
    tb = tb.reshape(DEPTH, 4, 2, 5, 128, 128).transpose(0, 1, 4, 2, 3, 5)
    sh["biasc"] = np.ascontiguousarray(tb.reshape(DEPTH * 4, 128, 1280))
    wbr = inp["w_branch"].copy()
    permA = np.concatenate([np.concatenate([np.arange(c * 64, c * 64 + 64), np.arange((c + 4) * 64, (c + 4) * 64 + 64)])
                            for c in range(4)])
    wbr[:, 0] = wbr[:, 0][:, permA, :]
    sh["wbr"] = np.ascontiguousarray(
        wbr.reshape(DEPTH, 3, 4, 128, 8, 128).transpose(0, 1, 4, 3, 2, 5).reshape(DEPTH * 24, 128, 512))
    sh["wout"] = np.ascontiguousarray(
        inp["w_out"].reshape(DEPTH, 8, 128, 8, 128).transpose(0, 3, 2, 1, 4).reshape(DEPTH * 8, 128, 1024))
    wfi = inp["w_ffn_in"].reshape(DEPTH, 8, 128, 2, 22, 128)
    sh["wfi"] = np.ascontiguousarray(wfi.transpose(0, 4, 2, 3, 1, 5).reshape(DEPTH * 22, 128, 2048))
    wfo = inp["w_ffn_out"].reshape(DEPTH, 2, 11, 128, 8, 128)
    sh["wfo"] = np.ascontiguousarray(wfo.transpose(0, 1, 4, 3, 2, 5).reshape(DEPTH * 16, 128, 1408))
    sh.update(_const_tables())
    return {k: (v if v.dtype != np.float64 else v.astype(f32)) for k, v in sh.items()}


_NC_CACHE = {}


def kernel(**inputs):
    inp = {k: np.asarray(v) for k, v in inputs.items()}
    sh = _prep_shared(inp)
    key = "full"
    if key not in _NC_CACHE:
        _NC_CACHE[key] = build()
    nc = _NC_CACHE[key]
    in_maps = []
    for b in range(8):
        m = dict(sh)
        m["x"] = np.ascontiguousarray(inp["x"][b]).astype(np.float32)
        m["cT"] = np.ascontiguousarray(inp["c"][b].reshape(8, 128).T).astype(np.float32)
        in_maps.append(m)
    res = run_bass_kernel_spmd(nc, in_maps, core_ids=list(range(8)))
    out = np.stack([np.asarray(res.results[b]["out"]) for b in range(8)], 0)
    return out.astype(np.float32)
```

```python
import numpy as np
import ml_dtypes
import concourse.bass as bass
import concourse.mybir as mybir
from concourse.bass_utils import run_bass_kernel_spmd

F32 = mybir.dt.float32
BF16 = mybir.dt.bfloat16
AF = mybir.ActivationFunctionType
ALU = mybir.AluOpType

D = 1024
S = 2048
DEPTH = 2
NCH = 8
NG = 4
NT = 16
GS = 512
EPS = 1e-6
NEG = -30000.0
NU_IN = 54

ENGS = ["pe", "act", "dve", "pool", "sp"]
SAME_ENGINE_SYNC = True


class Buf:
    __slots__ = ("name", "last_write", "reads", "dma_sem", "dma_count")

    def __init__(self, name=""):
        self.name = name
        self.last_write = None
        self.reads = []
        self.dma_sem = None
        self.dma_count = 0


class Event:
    __slots__ = ("kind", "eng", "op", "sem", "value")

    def __init__(self, kind, eng=None, op=None, sem=None, value=None):
        self.kind = kind
        self.eng = eng
        self.op = op
        self.sem = sem
        self.value = value


class Op:
    __slots__ = ("eng", "fn", "deps", "needed", "seq", "count", "is_dma", "dma_sem")

    def __init__(self, eng, fn):
        self.eng = eng
        self.fn = fn
        self.deps = []
        self.needed = False
        self.seq = None
        self.count = None
        self.is_dma = False
        self.dma_sem = None


class Prog:
    def __init__(self, nc):
        self.nc = nc
        self.ops = {e: [] for e in ENGS}
        self.sems = {e: nc.alloc_semaphore("s_" + e) for e in ENGS}
        self.waited = {e: {f: -1 for f in ENGS} for e in ENGS}
        self.dma_waited = {e: {} for e in ENGS}
        self.nsem = 0

    def _add_dep(self, op, ev):
        if ev is None:
            return
        if ev.kind == "eng":
            if ev.eng == op.eng and (not SAME_ENGINE_SYNC or op.eng == "pe"):
                return
            if ev.op is op:
                return
            if self.waited[op.eng][ev.eng] >= ev.op.seq:
                return
            self.waited[op.eng][ev.eng] = ev.op.seq
            ev.op.needed = True
            op.deps.append(ev)
        else:
            key = id(ev.sem)
            if self.dma_waited[op.eng].get(key, -1) >= ev.value:
                return
            self.dma_waited[op.eng][key] = ev.value
            op.deps.append(ev)

    def op(self, eng, fn, reads=(), writes=()):
        o = Op(eng, fn)
        o.seq = len(self.ops[eng])
        for b in reads:
            self._add_dep(o, b.last_write)
        for b in writes:
            self._add_dep(o, b.last_write)
            for r in b.reads:
                self._add_dep(o, r)
        ev = Event("eng", eng=eng, op=o)
        for b in reads:
            b.reads.append(ev)
        for b in writes:
            b.last_write = ev
            b.reads = []
        self.ops[eng].append(o)
        return o

    def dma(self, eng, fns, dst, reads=(), extra_writes=()):
        if not isinstance(fns, (list, tuple)):
            fns = [fns]
        if dst.dma_sem is None:
            dst.dma_sem = self.nc.alloc_semaphore("d%d" % self.nsem)
            self.nsem += 1
        for i, fn in enumerate(fns):
            o = Op(eng, fn)
            o.is_dma = True
            o.dma_sem = dst.dma_sem
            o.seq = len(self.ops[eng])
            if i == 0:
                for b in reads:
                    self._add_dep(o, b.last_write)
                for b in [dst] + list(extra_writes):
                    self._add_dep(o, b.last_write)
                    for r in b.reads:
                        self._add_dep(o, r)
            self.ops[eng].append(o)
        dst.dma_count += len(fns)
        ev = Event("dma", sem=dst.dma_sem, value=16 * dst.dma_count)
        for b in reads:
            b.reads.append(ev)
        for b in [dst] + list(extra_writes):
            b.last_write = ev
            b.reads = []
        return ev

    def emit(self, final_waits=()):
        nc = self.nc
        for b in final_waits:
            if b.last_write is not None and b.last_write.kind == "eng":
                b.last_write.op.needed = True
        for e in ENGS:
            c = 0
            for o in self.ops[e]:
                if o.needed:
                    c += 1
                    o.count = c
        sems = self.sems

        def run(e, engobj):
            for o in self.ops[e]:
                for ev in o.deps:
                    if ev.kind == "eng":
                        engobj.wait_ge(sems[ev.eng], ev.op.count)
                    else:
                        engobj.wait_ge(ev.sem, ev.value)
                inst = o.fn(engobj)
                if o.is_dma:
                    inst.then_inc(o.dma_sem, 16)
                elif o.needed:
                    inst.then_inc(sems[e], 1)
            if e == "sp":
                for b in final_waits:
                    ev = b.last_write
                    if ev is None:
                        continue
                    if ev.kind == "dma":
                        engobj.wait_ge(ev.sem, ev.value)
                    else:
                        engobj.wait_ge(sems[ev.eng], ev.op.count)

        with nc.Block() as block:
            @block.tensor
            def _(e):
                run("pe", e)

            @block.scalar
            def _(e):
                run("act", e)

            @block.vector
            def _(e):
                run("dve", e)

            @block.gpsimd
            def _(e):
                run("pool", e)

            @block.sync
            def _(e):
                run("sp", e)


def switch_bufs(old, new):
    evs = []
    for b in old:
        if b.last_write is not None:
            evs.append(b.last_write)
        evs.extend(b.reads)
    for b in new:
        b.last_write = None
        b.reads = list(evs)


def build(nl=DEPTH, stop=None):
    nc = bass.Bass("TRN2", target_bir_lowering=False)

    def din(name, shape, dt=F32):
        return nc.dram_tensor(name, list(shape), dt, kind="ExternalInput").ap()

    x_d = din("x", [S, D])
    cT_d = din("cT", [128, 8])
    gmix_d = din("gmix", [128, DEPTH * 8])
    gffn_d = din("gffn", [128, DEPTH * 8])
    gfin_d = din("gfin", [128, 8])
    wada_d = din("wada", [DEPTH * 24, 128, 2048])
    bada_d = din("bada", [128, DEPTH * 48])
    win_d = din("win", [DEPTH * NU_IN, 128, 1024])
    wf_d = din("wf", [DEPTH, 128, 64])
    bfor_d = din("bfor", [8, DEPTH])
    sinks_d = din("sinksb", [128, DEPTH * 8])
    biasc_d = din("biasc", [DEPTH * 4, 128, 2 * 5 * 128])
    wbr_d = din("wbr", [DEPTH * 24, 128, 512])
    wout_d = din("wout", [DEPTH * 8, 128, 1024])
    wfi_d = din("wfi", [DEPTH * 22, 128, 2048])
    wfo_d = din("wfo", [DEPTH * 16, 128, 1408])
    identf_d = din("identf", [128, 128])
    identb_d = din("identb", [128, 128], BF16)
    trimask_d = din("trimask", [128, 128], BF16)
    alibi_d = din("alibi", [128, 2 * 8 * 128], BF16)
    maskc_d = din("maskc", [128, 2 * 128], BF16)
    augc_d = din("augc", [6, S], BF16)
    out_d = nc.dram_tensor("out", [S, D], F32, kind="ExternalOutput").ap()

    P = Prog(nc)

    def sb(name, shape, dt):
        return nc.alloc_sbuf_tensor("sb_" + name, list(shape), dt)

    XT = sb("XT", [128, NCH, S], F32)
    HT = sb("HT", [128, NCH, S], BF16)
    W1 = sb("W1", [128, 24576], BF16)
    NS = 4
    WS = sb("WS", [128, NS, 2048], BF16)
    RS = sb("RS", [128, S], F32)
    PT = sb("PT", [128, 8, GS], BF16)
    TMPF = sb("TMPF", [128, 3, GS], F32)
    RD = sb("RD", [128, 2, GS], F32)
    CUM3 = sb("CUM3", [72, S], BF16)
    BIASC = sb("BIASC", [128, 2, 1280], BF16)
    identf = sb("identf", [128, 128], F32)
    identb = sb("identb", [128, 128], BF16)
    trimask = sb("trimask", [128, 128], BF16)
    alibi = sb("alibi", [128, 2, 8, 128], BF16)
    maskc = sb("maskc", [128, 2, 128], BF16)
    onesb = sb("onesb", [128, 128], BF16)
    ones8 = sb("ones8", [8, 1], F32)
    cT = sb("cTs", [128, 8], F32)
    condb = sb("condb", [128, 8], BF16)
    gmix = sb("gmixs", [128, DEPTH * 8], F32)
    gffn = sb("gffns", [128, DEPTH * 8], F32)
    gfin = sb("gfins", [128, 8], F32)
    bada = sb("badas", [128, DEPTH * 48], F32)
    modT = sb("modT", [128, DEPTH, 48], F32)
    gsc = sb("gsc", [128, DEPTH, 16], F32)
    bfor = sb("bfors", [8, DEPTH], F32)
    negb = sb("negb", [8, DEPTH], F32)
    sinkb = sb("sinkbs", [128, DEPTH * 8], F32)
    expsink = sb("expsink", [128, DEPTH * 8], F32)

    PS = [nc.alloc_psum_tensor("ps%d" % i, [128, GS], F32) for i in range(8)]
    psb = [Buf("ps%d" % i) for i in range(8)]

    xt = [[Buf() for _ in range(NG)] for _ in range(NCH)]
    ht = [[Buf() for _ in range(NG)] for _ in range(NCH)]
    wsb = [Buf("ws%d" % i) for i in range(NS)]
    rsb = [Buf() for _ in range(NG)]
    ptb = [Buf() for _ in range(8)]
    tmpb = [Buf() for _ in range(3)]
    rdb = [Buf() for _ in range(2)]
    cum3b = Buf()
    biascb = [Buf(), Buf()]
    cb = Buf("consts")
    modb = [Buf() for _ in range(DEPTH)]
    modb2 = [Buf() for _ in range(DEPTH)]
    outb = Buf("out")

    sl = lambda g: slice(g * GS, (g + 1) * GS)
    tl = lambda t: slice(t * 128, (t + 1) * 128)

    state = {"ws": 0, "pt": 0, "ptn": 0, "tmp": 0, "rd": 0, "pj": 0, "gu": 0}

    def wload(src_ap, n):
        i = state["ws"] % NS
        state["ws"] += 1
        dst = WS[:, i, 0:n]
        P.dma("pool", lambda e, d=dst, s_=src_ap: e.dma_start(out=d, in_=s_, max_dma_last_dim=4096), wsb[i])
        return dst, wsb[i]

    def pe_group(insts, reads, writes):
        def fn(e, insts=insts):
            r = None
            for it in insts:
                (o, l, rr, st, sp) = it[:5]
                if len(rr.shape) == 3 and len(o.shape) == 2:
                    o = o.rearrange("p (a b) -> p a b", a=rr.shape[1])
                if len(it) > 5 and it[5]:
                    r = e.matmul(o, lhsT=l, rhs=rr, start=st, stop=sp, skip_group_check=True)
                else:
                    r = e.matmul(o, lhsT=l, rhs=rr, start=st, stop=sp)
            return r
        return P.op("pe", fn, reads=reads, writes=writes)

    def act(out, in_, func, reads, writes, bias=None, scale=None):
        kw = {}
        if bias is not None:
            kw["bias"] = bias
        if scale is not None:
            kw["scale"] = scale
        return P.op("act", lambda e, o=out, i=in_, f=func, kw=kw: e.activation(out=o, in_=i, func=f, **kw),
                    reads=reads, writes=writes)

    def dve_tt(out, in0, in1, op, reads, writes):
        return P.op("dve", lambda e, o=out, a=in0, b=in1, op=op: e.tensor_tensor(out=o, in0=a, in1=b, op=op),
                    reads=reads, writes=writes)

    def dve_stt(out, in0, scalar, in1, op0, op1, reads, writes):
        return P.op("dve", lambda e, o=out, a=in0, s_=scalar, b=in1, p0=op0, p1=op1:
                    e.scalar_tensor_tensor(out=o, in0=a, scalar=s_, in1=b, op0=p0, op1=p1),
                    reads=reads, writes=writes)

    def dve_ts(out, in0, s1, s2, op0, op1, reads, writes, eng="dve"):
        if op1 is None:
            return P.op(eng, lambda e, o=out, a=in0, s1=s1, p0=op0: e.tensor_scalar(out=o, in0=a, scalar1=s1, scalar2=None, op0=p0),
                        reads=reads, writes=writes)
        return P.op(eng, lambda e, o=out, a=in0, s1=s1, s2=s2, p0=op0, p1=op1:
                    e.tensor_scalar(out=o, in0=a, scalar1=s1, scalar2=s2, op0=p0, op1=p1),
                    reads=reads, writes=writes)

    def copy(eng, out, in_, reads, writes):
        if eng == "act":
            return act(out, in_, AF.Copy, reads, writes)
        return P.op(eng, lambda e, o=out, i=in_: e.tensor_copy(out=o, in_=i), reads=reads, writes=writes)

    def next_rot(key, n):
        i = state[key] % n
        state[key] += 1
        return i

    small_loads = [
        (identf[:, :], identf_d[:, :]), (identb[:, :], identb_d[:, :]), (trimask[:, :], trimask_d[:, :]),
        (alibi[:, :, :, :].rearrange("p a h q -> p (a h q)"), alibi_d[:, :]),
        (maskc[:, :, :].rearrange("p a q -> p (a q)"), maskc_d[:, :]),
        (cT[:, :], cT_d[:, :]), (gmix[:, :], gmix_d[:, :]), (gffn[:, :], gffn_d[:, :]), (gfin[:, :], gfin_d[:, :]),
        (bada[:, :], bada_d[:, :]), (bfor[:, :], bfor_d[:, :]), (sinkb[:, :], sinks_d[:, :]),
    ]
    P.dma("sp", [lambda e, o=o, i=i: e.dma_start(out=o, in_=i) for (o, i) in small_loads], cb)
    onesbuf = Buf()
    P.op("pool", lambda e: e.memset(onesb[:, :], 1.0), writes=[onesbuf])
    P.op("pool", lambda e: e.memset(ones8[:, :], 1.0), writes=[onesbuf])
    cvb = Buf()
    act(condb[:, :], cT[:, :], AF.Silu, [cb], [cvb])
    act(expsink[:, :], sinkb[:, :], AF.Exp, [cb], [cvb])
    dve_ts(negb[:, :], bfor[:, :], -1.0, None, ALU.mult, None, [cb], [cvb])

    W1f = W1[:, :].bitcast(F32)
    xsb = [Buf() for _ in range(4)]
    w1_bufs = list(xsb)
    for t in range(NT):
        s_ = t % 4
        xs = W1f[:, s_ * 1024:(s_ + 1) * 1024]
        P.dma("sp", lambda e, o=xs, t=t: e.dma_start(out=o, in_=x_d[t * 128:(t + 1) * 128, :]), xsb[s_])
        for half in range(2):
            bi = (2 * t + half) % 4
            insts = [(PS[bi][:, q * 128:(q + 1) * 128], xs[:, (half * 4 + q) * 128:(half * 4 + q + 1) * 128]) for q in range(4)]

            def tfn(e, insts=insts):
                r = None
                for (o, i) in insts:
                    r = e.transpose(out=o, in_=i, identity=identf[:, :])
                return r
            P.op("pe", tfn, reads=[xsb[s_], cb], writes=[psb[bi]])
            copy("dve" if half == 0 else "act",
                 XT[:, half * 4:half * 4 + 4, tl(t)],
                 PS[bi][:, :].rearrange("p (c t) -> p c t", c=4),
                 [psb[bi]], [xt[c][t // 4] for c in range(half * 4, half * 4 + 4)])

    def ada_block(l, jb):
        modps = PS[7]
        wv, wb = wload(wada_d[l * 24 + jb], 2048)
        wv3 = wv.rearrange("p (k c) -> p k c", k=8)
        for jj in range(2):
            j = 2 * jb + jj
            insts = [(modps[:, j:j + 1], wv3[:, kc, jj * 128:(jj + 1) * 128], condb[:, kc:kc + 1], kc == 0, kc == 7)
                     for kc in range(8)]
            pe_group(insts, [wb, cvb], [psb[7]])

    def ada_finish(l, part):
        modps = PS[7]
        if part == 0:
            dve_tt(modT[:, l, 0:16], modps[:, 0:16], bada[:, l * 48:l * 48 + 16], ALU.add, [psb[7], cb], [modb[l]])
            dve_stt(gsc[:, l, 0:8], modT[:, l, 8:16], 1.0, gmix[:, l * 8:(l + 1) * 8], ALU.add, ALU.mult, [modb[l], cb], [modb[l]])
        else:
            dve_tt(modT[:, l, 16:48], modps[:, 16:48], bada[:, l * 48 + 16:(l + 1) * 48], ALU.add, [psb[7], cb], [modb2[l]])
            dve_stt(gsc[:, l, 8:16], modT[:, l, 32:40], 1.0, gffn[:, l * 8:(l + 1) * 8], ALU.add, ALU.mult, [modb2[l], cb], [modb2[l]])

    ada_rest = []

    def ada_first(l):
        for jb in range(8):
            ada_block(l, jb)
        ada_finish(l, 0)
        ada_rest.extend(range(8, 24))

    def ada_more(l, n):
        for _ in range(n):
            if ada_rest:
                ada_block(l, ada_rest.pop(0))
                if not ada_rest:
                    ada_finish(l, 1)

    def norm_group(g, gs_ap, sh_ap, vec_bufs, dst):
        bi = 5 + (g % 2)
        for c in range(NCH):
            i = next_rot("ptn", 4)
            if c % 2 == 0:
                act(PT[:, i, :], XT[:, c, sl(g)], AF.Square, [xt[c][g]], [ptb[i]])
            else:
                P.op("pool", lambda e, i=i, c=c, g=g: e.tensor_tensor(out=PT[:, i, :], in0=XT[:, c, sl(g)], in1=XT[:, c, sl(g)],
                                                                     op=ALU.mult), reads=[xt[c][g]], writes=[ptb[i]])
            pe_group([(PS[bi][:, :], onesb[:, :], PT[:, i, :], c == 0, c == NCH - 1)], [ptb[i], onesbuf], [psb[bi]])
        ti = next_rot("tmp", 3)
        act(TMPF[:, ti, :], PS[bi][:, :], AF.Ln, [psb[bi]], [tmpb[ti]], bias=EPS, scale=1.0 / D)
        act(RS[:, sl(g)], TMPF[:, ti, :], AF.Exp, [tmpb[ti]], [rsb[g]], scale=-0.5)
        for c in range(NCH):
            ti = next_rot("tmp", 3)
            dve_stt(TMPF[:, ti, :], XT[:, c, sl(g)], gs_ap[:, c:c + 1], RS[:, sl(g)], ALU.mult, ALU.mult,
                    [xt[c][g], rsb[g]] + vec_bufs, [tmpb[ti]])
            if dst is None:
                act(HT[:, c, sl(g)], TMPF[:, ti, :], AF.Identity, [tmpb[ti]] + vec_bufs, [ht[c][g]],
                    bias=sh_ap[:, c:c + 1], scale=1.0)
            else:
                dst(c, g, ti)

    def norm(gs_ap, sh_ap, vec_bufs, dst=None, after_group=None, skew=1):
        for g in range(NG):
            norm_group(g, gs_ap, sh_ap, vec_bufs, dst)
            if after_group is not None and g - skew >= 0:
                after_group(g - skew)
        if after_group is not None:
            for g in range(max(0, NG - skew), NG):
                after_group(g)

    def proj_fm(wv3, wb, g, nk=8, src=None, srcb=None):
        bi = 5 + next_rot("pj", 2 if ada_rest else 3)
        if src is None:
            src, srcb = HT, ht
        insts = [(PS[bi][:, :], wv3[:, kc, :], src[:, kc, sl(g)], kc == 0, kc == nk - 1) for kc in range(nk)]
        pe_group(insts, [wb] + [srcb[kc][g] for kc in range(nk)], [psb[bi]])
        return bi

    def proj_v(wv3, wb, t4, VB, vbufs):
        bi = 5 + next_rot("pj", 2 if ada_rest else 3)
        insts = []
        for q in range(4):
            t = t4 * 4 + q
            for kc in range(8):
                insts.append((PS[bi][:, q * 128:(q + 1) * 128], HT[:, kc, tl(t)], wv3[:, kc, :], kc == 0, kc == 7))
        pe_group(insts, [wb] + [ht[kc][t4] for kc in range(8)], [psb[bi]])
        pv = PS[bi][:, :].rearrange("p (q c) -> p q c", q=4)
        copy("dve", VB[:, t4 * 4:t4 * 4 + 4, 0:64], pv[:, :, 0:64], [psb[bi]], [vbufs[t4]])
        copy("dve", VB[:, t4 * 4:t4 * 4 + 4, 128:192], pv[:, :, 64:128], [psb[bi]], [vbufs[t4]])

    def attn_finish(obi, lo_num, out_ap, out_bufs, extra_reads, c0=0, sink_cols=None):
        num = slice(lo_num, lo_num + 64)
        den = slice(64 - lo_num, 128 - lo_num)
        ri = next_rot("rd", 2)
        if sink_cols is None:
            act(RD[num, ri, c0:], PS[obi][den, c0:], AF.Ln, [psb[obi]], [rdb[ri]])
        else:
            dve_tt(RD[den, ri, :].rearrange("p (h q) -> p h q", h=4), PS[obi][den, :].rearrange("p (h q) -> p h q", h=4),
                   expsink[den, sink_cols[0]:sink_cols[0] + 4].unsqueeze(2).to_broadcast([64, 4, 128]), ALU.add,
                   [psb[obi], cvb], [rdb[ri]])
            act(RD[num, ri, :], RD[den, ri, :], AF.Ln, [rdb[ri]], [rdb[ri]])
        act(RD[num, ri, c0:], RD[num, ri, c0:], AF.Exp, [rdb[ri]], [rdb[ri]], scale=-1.0)
        if sink_cols is None:
            dve_tt(out_ap, PS[obi][num, c0:], RD[num, ri, c0:], ALU.mult, [psb[obi], rdb[ri]] + extra_reads, out_bufs)
        else:
            dve_tt(out_ap, PS[obi][num, :].rearrange("p (h q) -> p h q", h=4),
                   RD[num, ri, :].rearrange("p (h q) -> p h q", h=4), ALU.mult,
                   [psb[obi], rdb[ri]] + extra_reads, out_bufs)

    OT = W1[:, 0:8192].rearrange("p (c t) -> p c t", c=4)
    MGR = W1[:, 8192:24576]
    MG = MGR.rearrange("p (c t) -> p c t", c=8)
    region = {"mg": list(w1_bufs), "ot": []}
    ot = [[Buf() for _ in range(NG)] for _ in range(4)]
    switch_bufs(w1_bufs, [b for r in ot for b in r])
    region["w1all"] = None

    def mg_switch(new):
        switch_bufs(region["mg"], new)
        region["mg"] = new

    def merge_branch(l, k):
        mg = [[Buf() for _ in range(NG)] for _ in range(NCH)]
        mg_switch([b for r in mg for b in r])
        for dc in range(NCH):
            wg, wgb = wload(win_d[l * NU_IN + 30 + k * 8 + dc], 1024)
            wg3 = wg.rearrange("p (k c) -> p k c", k=8)
            wbv, wbb = wload(wbr_d[l * 24 + k * 8 + dc], 512)
            wb3 = wbv.rearrange("p (k c) -> p k c", k=4)
            for g in range(NG):
                yb = g % 2
                gb = 2 + (g % 2)
                pe_group([(PS[yb][:, :], wb3[:, kc, :], OT[:, kc, sl(g)], kc == 0, kc == 3) for kc in range(4)],
                         [wbb] + [ot[kc][g] for kc in range(4)], [psb[yb]])
                pe_group([(PS[gb][:, :], wg3[:, kc, :], HT[:, kc, sl(g)], kc == 0, kc == 7) for kc in range(8)],
                         [wgb] + [ht[kc][g] for kc in range(8)], [psb[gb]])
                ti = next_rot("tmp", 3)
                act(TMPF[:, ti, :], PS[gb][:, :], AF.Sigmoid, [psb[gb]], [tmpb[ti]])
                dve_tt(MG[:, dc, sl(g)], PS[yb][:, :], TMPF[:, ti, :], ALU.mult, [psb[yb], tmpb[ti]], [mg[dc][g]])
        for dco in range(NCH):
            wo, wob = wload(wout_d[l * 8 + dco], 1024)
            wo3 = wo.rearrange("p (k c) -> p k c", k=8)
            for g in range(NG):
                bi = 4 + next_rot("pj", 4)
                pe_group([(PS[bi][:, :], wo3[:, kc, :], MG[:, kc, sl(g)], kc == 0, kc == 7) for kc in range(8)],
                         [wob] + [mg[kc][g] for kc in range(8)], [psb[bi]])
                dve_stt(XT[:, dco, sl(g)], PS[bi][:, :], modT[:, l, 16 + dco:17 + dco], XT[:, dco, sl(g)], ALU.mult, ALU.add,
                        [psb[bi], modb2[l], xt[dco][g]], [xt[dco][g]])

    SBANKS = [0, 1, 2, 5, 6, 7]
    LOOK = 4

    def s_exp_pv(k_insts, nk_reads, c0, ncols, obi, v_lhsT, v_reads, first, last, pending, skip=False):
        sbl = state["sbanks"]
        sbi = sbl[next_rot("sb", len(sbl))]
        insts = [(PS[sbi][:, c0:c0 + ncols] if o is None else o, l_, r_, st, sp) for (o, l_, r_, st, sp) in k_insts(sbi)]
        pe_group(insts, nk_reads, [psb[sbi]])
        pi = next_rot("pt", 8)
        act(PT[:, pi, c0:c0 + ncols], PS[sbi][:, c0:c0 + ncols], AF.Exp, [psb[sbi]], [ptb[pi]])
        pending.append(("pv", [(PS[obi][:, c0:c0 + ncols], v_lhsT, PT[:, pi, c0:c0 + ncols], first, last, skip)],
                        [ptb[pi]] + v_reads, [psb[obi]]))

    def push_fin(pending, fn):
        pending.append(("fin", fn))

    def flush(pending, keep):
        def npv():
            return sum(1 for it in pending if it[0] == "pv")
        while pending and (npv() > keep or pending[0][0] == "fin"):
            it = pending.pop(0)
            if it[0] == "pv":
                pe_group(it[1], it[2], it[3])
            else:
                it[1]()

    state["sb"] = 0
    state["ob"] = 0
    state["oba"] = 0
    state["sbanks"] = SBANKS

    def branch_A(l):
        QA = MGR[:, 0:8192].rearrange("p (c t) -> p c t", c=4)
        KA = [MGR[:, 8192:10240], MGR[:, 10240:12288]]
        VA = MGR[:, 12288:15360].rearrange("p (t c) -> p t c", t=16)
        qab = [[Buf() for _ in range(NG)] for _ in range(4)]
        kab = [[Buf() for _ in range(NG)] for _ in range(2)]
        vab = [Buf() for _ in range(4)]
        vones = Buf()
        mg_switch([b for r in qab for b in r] + [b for r in kab for b in r] + vab + [vones])
        P.op("pool", lambda e: e.memset(VA[:, :, 64:128], 1.0), writes=[vones] + vab)
        for kv in range(2):
            P.op("pool", lambda e, kv=kv: e.memset(KA[kv], 0.0), writes=kab[kv])
        base = l * NU_IN

        def qproj(c, g, wv3, wb):
            bi = proj_fm(wv3, wb, g)
            dve_ts(QA[:, c, sl(g)], PS[bi][:, :], 0.125, None, ALU.mult, None, [psb[bi]], [qab[c][g]])
        pro = []
        for c in range(3):
            wv, wb = wload(win_d[base + c], 1024)
            pro.append((c, wv.rearrange("p (k c) -> p k c", k=8), wb))

        def after_group(g):
            for (c, wv3, wb) in pro:
                qproj(c, g, wv3, wb)
        norm(gsc[:, l, 0:8], modT[:, l, 0:8], [modb[l]], after_group=after_group)
        ada_more(l, 7)
        wv, wb = wload(win_d[base + 3], 1024)
        wv3 = wv.rearrange("p (k c) -> p k c", k=8)
        for g in range(NG):
            qproj(3, g, wv3, wb)
        ada_more(l, 3)
        wv, wb = wload(win_d[base + 4], 1024)
        wv3 = wv.rearrange("p (k c) -> p k c", k=8)
        for g in range(NG):
            bi = proj_fm(wv3, wb, g)
            copy("dve", KA[0][0:64, sl(g)], PS[bi][0:64, :], [psb[bi]], [kab[0][g]])
            copy("dve", KA[1][64:128, sl(g)], PS[bi][64:128, :], [psb[bi]], [kab[1][g]])
        ada_more(l, 3)
        wv, wb = wload(win_d[base + 5], 1024)
        wv3 = wv.rearrange("p (k c) -> p k c", k=8)
        for t4 in range(4):
            proj_v(wv3, wb, t4, VA, vab)
        ada_more(l, 99)
        pending = []
        state["sbanks"] = [0, 1, 2, 5]
        obl = [3, 4, 6, 7]
        for m in range(NT):
            for kv in range(2):
                rows = slice(kv * 64, kv * 64 + 64)
                obi = obl[next_rot("oba", len(obl))]
                vsl = slice(0, 128) if kv == 0 else slice(64, 192)
                kts = [m] if m == 0 else [m - 1, m]
                for ji, j in enumerate(kts):
                    kind = m - j

                    def k_insts(sbi, j=j, kind=kind, rows=rows, kv=kv, m=m):
                        return [
                            (None, KA[kv][:, tl(j)], QA[:, :, tl(m)], True, False),
                            (None, identb[:, :], alibi[:, kind, kv * 4:kv * 4 + 4, :], False, True),
                        ]
                    s_exp_pv(k_insts, [kab[kv][j // 4], cb] + [qab[c][m // 4] for c in range(4)], 0, GS, obi,
                             VA[:, j, vsl], [vab[j // 4], vones], ji == 0, ji == len(kts) - 1, pending)
                    flush(pending, LOOK)
                sink_cols = [l * 8 + kv * 4 + hq for hq in range(4)]
                push_fin(pending, lambda obi=obi, kv=kv, rows=rows, m=m, sink_cols=sink_cols:
                         attn_finish(obi, kv * 64, OT[rows, :, tl(m)], [ot[c][m // 4] for c in range(4)], [], sink_cols=sink_cols))
        flush(pending, 0)
        state["sbanks"] = SBANKS

    def branch_BC(l, which):
        is_b = which == "B"
        base = l * NU_IN + (6 if is_b else 18)
        QK = [MGR[:, i * 2048:(i + 1) * 2048] for i in range(4)]
        VB = MGR[:, 8192:11264].rearrange("p (t c) -> p t c", t=16)
        qb = [[Buf() for _ in range(NG)] for _ in range(4)]
        augb = [Buf() for _ in range(4)]
        vbb = [Buf() for _ in range(4)]
        vones = Buf()
        mg_switch([b for r in qb for b in r] + augb + vbb + [vones])
        P.op("pool", lambda e: e.memset(VB[:, :, 64:128], 1.0), writes=[vones] + vbb)
        if is_b:
            wfv, wfb = wload(wf_d[l], 64)
            wf3 = wfv.rearrange("p (k c) -> p k c", k=8)
            A8 = RS[0:8, :]
            for g in range(NG):
                bi = 5 + next_rot("pj", 3)
                pe_group([(PS[bi][0:8, :], wf3[:, kc, :], HT[:, kc, sl(g)], kc == 0, kc == 7) for kc in range(8)],
                         [wfb] + [ht[kc][g] for kc in range(8)], [psb[bi]])
                act(A8[:, sl(g)], PS[bi][0:8, :], AF.Exp, [psb[bi], cvb], [rsb[g]], bias=negb[:, l:l + 1], scale=-1.0)
                act(A8[:, sl(g)], A8[:, sl(g)], AF.Ln, [rsb[g]], [rsb[g]], bias=1.0, scale=1.0)
            P.op("dve", lambda e: e.tensor_tensor_scan(out=A8, data0=ones8[:, 0:1].to_broadcast([8, S]), data1=A8,
                                                       initial=0.0, op0=ALU.mult, op1=ALU.subtract),
                 reads=rsb + [onesbuf], writes=rsb)
            TB = PT[0:8, 0:4, :].rearrange("p a b -> p (a b)")
            copy("dve", CUM3[0:8, :], A8, rsb, [cum3b])
            dve_tt(A8, A8, CUM3[0:8, :], ALU.subtract, rsb + [cum3b], rsb)
            copy("dve", TB, A8, rsb, ptb)
            dve_tt(A8, A8, TB, ALU.subtract, rsb + ptb[0:4], rsb)
            copy("act", CUM3[32:40, :], TB, ptb[0:4], [cum3b])
            copy("dve", TB, A8, rsb, ptb)
            copy("act", CUM3[64:72, :], TB, ptb[0:4], [cum3b])
        for i in range(4):
            P.op("pool", lambda e, i=i: e.memset(QK[i], 0.0), writes=qb[i] + [augb[i]])
        if is_b:
            cr = [(0, 67, 0), (1, 3, 0), (2, 64, 3), (3, 0, 3)]
            for (i, r0, a0) in cr:
                P.dma("sp", lambda e, i=i, r0=r0, a0=a0: e.dma_start(out=QK[i][r0:r0 + 3, :], in_=augc_d[a0:a0 + 3, :]), augb[i])
        for p in range(4):
            if not is_b:
                bs = p % 2
                P.dma("pool", lambda e, bs=bs, p=p: e.dma_start(out=BIASC[:, bs, :], in_=biasc_d[l * 4 + p], max_dma_last_dim=4096),
                      biascb[bs])
            if is_b:
                he, ho = 2 * p, 2 * p + 1
                mv = [(0, 64, he), (1, 0, ho), (2, 67, he), (3, 3, ho)]
                for (i, r0, h) in mv:
                    P.dma("sp", [lambda e, i=i, r0=r0, h=h, q=q: e.dma_start(out=QK[i][r0 + q:r0 + q + 1, :],
                                                                             in_=CUM3[32 * q + h:32 * q + h + 1, :])
                                 for q in range(3)], augb[i], reads=[cum3b])
            wq, wqb = wload(win_d[base + p * 3 + 0], 1024)
            wq3 = wq.rearrange("p (k c) -> p k c", k=8)
            for g in range(NG):
                bi = proj_fm(wq3, wqb, g)
                dve_ts(QK[0][0:64, sl(g)], PS[bi][0:64, :], 0.125, None, ALU.mult, None, [psb[bi]], [qb[0][g]])
                dve_ts(QK[1][64:128, sl(g)], PS[bi][64:128, :], 0.125, None, ALU.mult, None, [psb[bi]], [qb[1][g]])
            wk, wkb = wload(win_d[base + p * 3 + 1], 1024)
            wk3 = wk.rearrange("p (k c) -> p k c", k=8)
            for g in range(NG):
                bi = proj_fm(wk3, wkb, g)
                copy("dve", QK[2][0:64, sl(g)], PS[bi][0:64, :], [psb[bi]], [qb[2][g]])
                copy("dve", QK[3][64:128, sl(g)], PS[bi][64:128, :], [psb[bi]], [qb[3][g]])
            wv, wvb = wload(win_d[base + p * 3 + 2], 1024)
            wv3 = wv.rearrange("p (k c) -> p k c", k=8)
            for t4 in range(4):
                proj_v(wv3, wvb, t4, VB, vbb)
            pending = []
            for hh in range(2):
                rows = slice(hh * 64, hh * 64 + 64)
                vsl = slice(0, 128) if hh == 0 else slice(64, 192)
                for g in range(NG):
                    obi = 3 + next_rot("ob", 2)
                    if is_b:
                        kts = list(range(0, 4 * g + 4))
                    else:
                        kts = list(range(max(0, 4 * g - 4), 4 * g + 4))
                    for ji, j in enumerate(kts):
                        if is_b:
                            r = j - 4 * g
                            c0 = 128 * r if r > 0 else 0
                            ncols = GS - c0
                            Qt, Kt = QK[hh], QK[2 + hh]

                            def k_insts(sbi, j=j, r=r, c0=c0, ncols=ncols, Qt=Qt, Kt=Kt, g=g):
                                ins = [(None, Kt[:, tl(j)], Qt[:, g * GS + c0:(g + 1) * GS], True, r < 0)]
                                if r >= 0:
                                    ins.append((PS[sbi][:, c0:c0 + 128], identb[:, :], trimask[:, :], False, True))
                                return ins
                            kreads = [qb[hh][g], qb[2 + hh][j // 4], augb[hh], augb[2 + hh], cb]
                        else:
                            m0 = max(j, 4 * g)
                            m1 = min(j + 4, 4 * g + 3)
                            c0 = (m0 - 4 * g) * 128
                            ncols = (m1 - m0 + 1) * 128
                            k0 = m0 - j
                            k1 = m1 - j
                            bs = p % 2

                            def k_insts(sbi, j=j, c0=c0, ncols=ncols, k0=k0, k1=k1, rows=rows, g=g, hh=hh, bs=bs):
                                ins = [(None, QK[2 + hh][:, tl(j)], QK[hh][:, g * GS + c0:g * GS + c0 + ncols], True, False)]
                                has0 = (k0 == 0)
                                has4 = (k1 == 4)
                                ins.append((None, identb[:, :], BIASC[:, bs, hh * 640 + k0 * 128:hh * 640 + (k1 + 1) * 128],
                                            False, not (has0 or has4)))
                                if has0:
                                    ins.append((PS[sbi][:, c0:c0 + 128], identb[:, :], maskc[:, 0, :], False, not has4))
                                if has4:
                                    ins.append((PS[sbi][:, c0 + ncols - 128:c0 + ncols], identb[:, :], maskc[:, 1, :], False, True))
                                return ins
                            kreads = [qb[hh][g], qb[2 + hh][j // 4], biascb[bs], cb]
                        s_exp_pv(k_insts, kreads, c0, ncols, obi, VB[:, j, vsl], [vbb[j // 4], vones],
                                 ji == 0, ji == len(kts) - 1, pending, skip=not is_b)
                        flush(pending, LOOK)
                    push_fin(pending, lambda obi=obi, hh=hh, rows=rows, p=p, g=g:
                             attn_finish(obi, hh * 64, OT[rows, p, sl(g)], [ot[p][g]], []))
            flush(pending, 0)

    def ffn(l, next_ada):
        AT = W1[:, 0:22528].rearrange("p (j t) -> p j t", j=11)
        ada_todo = list(range(24)) if next_ada else []
        for J in range(2):
            at = [[Buf() for _ in range(NG)] for _ in range(11)]
            allb = [b for r in at for b in r]
            if J == 0:
                switch_bufs(region["mg"] + [b for r in ot for b in r], allb)
            else:
                switch_bufs(region["mg"], allb)
            region["mg"] = allb
            def gu_tile(jj, g, w4, wb, at=at):
                r2 = next_rot("gu", 2)
                gb = r2
                ub = 2 + r2
                pe_group([(PS[gb][:, :], w4[:, 0, kc, :], HT[:, kc, sl(g)], kc == 0, kc == 7) for kc in range(8)],
                         [wb] + [ht[kc][g] for kc in range(8)], [psb[gb]])
                pe_group([(PS[ub][:, :], w4[:, 1, kc, :], HT[:, kc, sl(g)], kc == 0, kc == 7) for kc in range(8)],
                         [wb] + [ht[kc][g] for kc in range(8)], [psb[ub]])
                ti = next_rot("tmp", 3)
                act(TMPF[:, ti, :], PS[gb][:, :], AF.Silu, [psb[gb]], [tmpb[ti]])
                dve_tt(AT[:, jj, sl(g)], PS[ub][:, :], TMPF[:, ti, :], ALU.mult, [psb[ub], tmpb[ti]], [at[jj][g]])

            def ada_step(j):
                for _ in range(2 if j < 2 else 1):
                    if ada_todo:
                        ada_block(l + 1, ada_todo.pop(0))
            jstart = 0
            if J == 0:
                pro = []
                for jj in range(3):
                    wv, wb = wload(wfi_d[l * 22 + jj], 2048)
                    pro.append((jj, wv.rearrange("p (a k c) -> p a k c", a=2, k=8), wb))

                def after_group(g, pro=pro):
                    for (jj, w4, wb) in pro:
                        gu_tile(jj, g, w4, wb)
                norm(gsc[:, l, 8:16], modT[:, l, 24:32], [modb2[l]], after_group=after_group)
                for jj in range(3):
                    ada_step(jj)
                jstart = 3
            for jj in range(jstart, 11):
                j = J * 11 + jj
                wv, wb = wload(wfi_d[l * 22 + j], 2048)
                w4 = wv.rearrange("p (a k c) -> p a k c", a=2, k=8)
                for g in range(NG):
                    gu_tile(jj, g, w4, wb)
                ada_step(j)
            for dco in range(NCH):
                wo, wob = wload(wfo_d[l * 16 + J * 8 + dco], 1408)
                wo3 = wo.rearrange("p (j c) -> p j c", j=11)
                for g in range(NG):
                    bi = 4 + next_rot("pj", 3)
                    pe_group([(PS[bi][:, :], wo3[:, jj, :], AT[:, jj, sl(g)], jj == 0, jj == 10) for jj in range(11)],
                             [wob] + [at[jj][g] for jj in range(11)], [psb[bi]])
                    dve_stt(XT[:, dco, sl(g)], PS[bi][:, :], modT[:, l, 40 + dco:41 + dco], XT[:, dco, sl(g)], ALU.mult, ALU.add,
                            [psb[bi], modb2[l], xt[dco][g]], [xt[dco][g]])
        if next_ada:
            assert not ada_todo
            ada_finish(l + 1, 0)
            ada_finish(l + 1, 1)
        newot = [b for r in ot for b in r]
        dummy = [Buf()]
        switch_bufs(region["mg"], newot + dummy)
        region["mg"] = dummy

    for l in range(nl):
        if l == 0:
            ada_first(l)
        branch_A(l)
        merge_branch(l, 0)
        if stop == "a%d" % l:
            break
        branch_BC(l, "B")
        merge_branch(l, 1)
        if stop == "b%d" % l:
            break
        branch_BC(l, "C")
        merge_branch(l, 2)
        if stop == "m%d" % l:
            break
        ffn(l, l + 1 < nl)

    fin_bufs = [Buf() for _ in range(3)]
    outbs = [Buf() for _ in range(3)]
    switch_bufs(region["mg"] + [b for r in ot for b in r], fin_bufs)
    YS = [W1f[:, i * 1024:(i + 1) * 1024] for i in range(3)]
    def out_tiles(g):
        for t in range(4 * g, 4 * g + 4):
            si = t % 3
            for half in range(2):
                bi = (2 * t + half) % 4
                insts = [(PS[bi][:, q * 128:(q + 1) * 128], XT[:, half * 4 + q, tl(t)]) for q in range(4)]

                def tfn(e, insts=insts):
                    r = None
                    for (o, i) in insts:
                        r = e.transpose(out=o, in_=i, identity=identf[:, :])
                    return r
                P.op("pe", tfn, reads=[xt[half * 4 + q][t // 4] for q in range(4)] + [cb], writes=[psb[bi]])
                copy("dve" if half == 0 else "act", YS[si][:, half * 512:(half + 1) * 512], PS[bi][:, :], [psb[bi]], [fin_bufs[si]])
            P.dma("sp", lambda e, t=t, si=si: e.dma_start(out=out_d[t * 128:(t + 1) * 128, :], in_=YS[si]), outbs[si], reads=[fin_bufs[si]])

    if stop is None:
        def dst(c, g, ti):
            copy("act", XT[:, c, sl(g)], TMPF[:, ti, :], [tmpb[ti]], [xt[c][g]])
        norm(gfin, None, [cb], dst=dst, after_group=out_tiles)
    else:
        for g in range(NG):
            out_tiles(g)
    P.emit(final_waits=outbs)
    return nc


def _win_cols():
    units = []
    for c in range(4):
        units.append(list(range(c * 64, c * 64 + 64)) + list(range((c + 4) * 64, (c + 4) * 64 + 64)))
    units.append(list(range(512, 640)))
    units.append(list(range(640, 768)))
    for p in range(4):
        units.append(list(range(768 + p * 128, 768 + (p + 1) * 128)))
        units.append(list(range(1280 + p * 128, 1280 + (p + 1) * 128)))
        units.append(list(range(1792 + p * 128, 1792 + (p + 1) * 128)))
    for p in range(4):
        units.append(list(range(2312 + p * 128, 2312 + (p + 1) * 128)))
        units.append(list(range(2824 + p * 128, 2824 + (p + 1) * 128)))
        units.append(list(range(3336 + p * 128, 3336 + (p + 1) * 128)))
    for k in range(3):
        for dc in range(8):
            units.append(list(range(3848 + k * 1024 + dc * 128, 3848 + k * 1024 + (dc + 1) * 128)))
    assert len(units) == NU_IN
    return np.array(units)


def _const_tables():
    bf = ml_dtypes.bfloat16
    identf = np.eye(128, dtype=np.float32)
    identb = np.eye(128).astype(bf)
    s = np.arange(128)[:, None]
    q = np.arange(128)[None, :]
    trimask = np.where(s > q, NEG, 0.0).astype(bf)
    slopes = 2.0 ** (-(np.arange(1, 9)))
    alibi = np.zeros((128, 2, 8, 128), np.float32)
    for kind in range(2):
        dist = 128 * kind + q - s
        msk = ((s >= 64) & (q < 64)) if kind == 0 else ((s < 64) & (q >= 64))
        for h in range(8):
            alibi[:, kind, h, :] = np.where(msk, NEG, -slopes[h] * np.abs(dist))
    maskc = np.zeros((128, 2, 128), np.float32)
    maskc[:, 0, :] = np.where((s >= 64) & (q < 64), NEG, 0.0)
    maskc[:, 1, :] = np.where((s < 64) & (q >= 64), NEG, 0.0)
    augc = np.concatenate([-np.ones((3, S), np.float32), np.ones((3, S), np.float32)], 0)
    return dict(identf=identf, identb=identb, trimask=trimask, alibi=alibi.reshape(128, -1).astype(bf),
                maskc=maskc.reshape(128, -1).astype(bf), augc=augc.astype(bf))


def _prep_shared(inp):
    f32 = np.float32
    vecT = lambda v: np.ascontiguousarray(v.reshape(-1, 8, 128).transpose(2, 0, 1).reshape(128, -1)).astype(f32)
    sh = {}
    sh["gmix"] = vecT(inp["norm_mix_g"])
    sh["gffn"] = vecT(inp["norm_ffn_g"])
    sh["gfin"] = vecT(inp["final_norm_g"][None])
    w_ada = inp["w_ada"]
    sh["wada"] = np.ascontiguousarray(
        w_ada.reshape(DEPTH, 8, 128, 24, 256).transpose(0, 3, 2, 1, 4).reshape(DEPTH * 24, 128, 2048))
    sh["bada"] = np.ascontiguousarray(inp["b_ada"].reshape(DEPTH, 48, 128).transpose(2, 0, 1).reshape(128, DEPTH * 48))
    cols = _win_cols()
    w_in = inp["w_in"]
    wu = w_in[:, :, cols]
    sh["win"] = np.ascontiguousarray(
        wu.reshape(DEPTH, 8, 128, NU_IN, 128).transpose(0, 3, 2, 1, 4).reshape(DEPTH * NU_IN, 128, 1024))
    wfc = w_in[:, :, 2304:2312]
    sh["wf"] = np.ascontiguousarray(wfc.reshape(DEPTH, 8, 128, 8).transpose(0, 2, 1, 3).reshape(DEPTH, 128, 64))
    sh["bfor"] = np.ascontiguousarray(inp["b_forget"].T)
    sh["sinksb"] = np.ascontiguousarray(np.broadcast_to(inp["sinks"].reshape(1, DEPTH * 8), (128, DEPTH * 8)))
    rb = inp["rel_bias"]
    s = np.arange(128)[:, None]
    q = np.arange(128)[None, :]
    idx = np.stack([np.clip(128 * kind + q - s, -128, 128) + 128 for kind in range(5)], 0)
    tb = rb[:, :, idx]
    tb = tb.reshape(DEPTH, 4, 2, 5, 128, 128).transpose(0, 1, 4, 2, 3, 5)
    sh["biasc"] = np.ascontiguousarray(tb.reshape(DEPTH * 4, 128, 1280))
    wbr = inp["w_branch"].copy()
    permA = np.concatenate([np.concatenate([np.arange(c * 64, c * 64 + 64), np.arange((c + 4) * 64, (c + 4) * 64 + 64)])
                            for c in range(4)])
    wbr[:, 0] = wbr[:, 0][:, permA, :]
    sh["wbr"] = np.ascontiguousarray(
        wbr.reshape(DEPTH, 3, 4, 128, 8, 128).transpose(0, 1, 4, 3, 2, 5).reshape(DEPTH * 24, 128, 512))
    sh["wout"] = np.ascontiguousarray(
        inp["w_out"].reshape(DEPTH, 8, 128, 8, 128).transpose(0, 3, 2, 1, 4).reshape(DEPTH * 8, 128, 1024))
    wfi = inp["w_ffn_in"].reshape(DEPTH, 8, 128, 2, 22, 128)
    sh["wfi"] = np.ascontiguousarray(wfi.transpose(0, 4, 2, 3, 1, 5).reshape(DEPTH * 22, 128, 2048))
    wfo = inp["w_ffn_out"].reshape(DEPTH, 2, 11, 128, 8, 128)
    sh["wfo"] = np.ascontiguousarray(wfo.transpose(0, 1, 4, 3, 2, 5).reshape(DEPTH * 16, 128, 1408))
    sh.update(_const_tables())
    return {k: (v if v.dtype != np.float64 else v.astype(f32)) for k, v in sh.items()}


_NC_CACHE = {}


def kernel(**inputs):
    inp = {k: np.asarray(v) for k, v in inputs.items()}
    sh = _prep_shared(inp)
    key = "full"
    if key not in _NC_CACHE:
        _NC_CACHE[key] = build()
    nc = _NC_CACHE[key]
    in_maps = []
    for b in range(8):
        m = dict(sh)
        m["x"] = np.ascontiguousarray(inp["x"][b]).astype(np.float32)
        m["cT"] = np.ascontiguousarray(inp["c"][b].reshape(8, 128).T).astype(np.float32)
        in_maps.append(m)
    res = run_bass_kernel_spmd(nc, in_maps, core_ids=list(range(8)))
    out = np.stack([np.asarray(res.results[b]["out"]) for b in range(8)], 0)
    return out.astype(np.float32)
```

```python
import numpy as np
import ml_dtypes
import concourse.bass as bass
import concourse.mybir as mybir
from concourse.bass_utils import run_bass_kernel_spmd

F32 = mybir.dt.float32
BF16 = mybir.dt.bfloat16
AF = mybir.ActivationFunctionType
ALU = mybir.AluOpType

D = 1024
S = 2048
DEPTH = 2
NCH = 8
NG = 4
NT = 16
GS = 512
EPS = 1e-6
NEG = -30000.0
NU_IN = 54

ENGS = ["pe", "act", "dve", "pool", "sp"]
SAME_ENGINE_SYNC = True


class Buf:
    __slots__ = ("name", "last_write", "reads", "dma_sem", "dma_count")

    def __init__(self, name=""):
        self.name = name
        self.last_write = None
        self.reads = []
        self.dma_sem = None
        self.dma_count = 0


class Event:
    __slots__ = ("kind", "eng", "op", "sem", "value")

    def __init__(self, kind, eng=None, op=None, sem=None, value=None):
        self.kind = kind
        self.eng = eng
        self.op = op
        self.sem = sem
        self.value = value


class Op:
    __slots__ = ("eng", "fn", "deps", "needed", "seq", "count", "is_dma", "dma_sem")

    def __init__(self, eng, fn):
        self.eng = eng
        self.fn = fn
        self.deps = []
        self.needed = False
        self.seq = None
        self.count = None
        self.is_dma = False
        self.dma_sem = None


class Prog:
    def __init__(self, nc):
        self.nc = nc
        self.ops = {e: [] for e in ENGS}
        self.sems = {e: nc.alloc_semaphore("s_" + e) for e in ENGS}
        self.waited = {e: {f: -1 for f in ENGS} for e in ENGS}
        self.dma_waited = {e: {} for e in ENGS}
        self.nsem = 0

    def _add_dep(self, op, ev):
        if ev is None:
            return
        if ev.kind == "eng":
            if ev.eng == op.eng and (not SAME_ENGINE_SYNC or op.eng == "pe"):
                return
            if ev.op is op:
                return
            if self.waited[op.eng][ev.eng] >= ev.op.seq:
                return
            self.waited[op.eng][ev.eng] = ev.op.seq
            ev.op.needed = True
            op.deps.append(ev)
        else:
            key = id(ev.sem)
            if self.dma_waited[op.eng].get(key, -1) >= ev.value:
                return
            self.dma_waited[op.eng][key] = ev.value
            op.deps.append(ev)

    def op(self, eng, fn, reads=(), writes=()):
        o = Op(eng, fn)
        o.seq = len(self.ops[eng])
        for b in reads:
            self._add_dep(o, b.last_write)
        for b in writes:
            self._add_dep(o, b.last_write)
            for r in b.reads:
                self._add_dep(o, r)
        ev = Event("eng", eng=eng, op=o)
        for b in reads:
            b.reads.append(ev)
        for b in writes:
            b.last_write = ev
            b.reads = []
        self.ops[eng].append(o)
        return o

    def dma(self, eng, fns, dst, reads=(), extra_writes=()):
        if not isinstance(fns, (list, tuple)):
            fns = [fns]
        if dst.dma_sem is None:
            dst.dma_sem = self.nc.alloc_semaphore("d%d" % self.nsem)
            self.nsem += 1
        for i, fn in enumerate(fns):
            o = Op(eng, fn)
            o.is_dma = True
            o.dma_sem = dst.dma_sem
            o.seq = len(self.ops[eng])
            if i == 0:
                for b in reads:
                    self._add_dep(o, b.last_write)
                for b in [dst] + list(extra_writes):
                    self._add_dep(o, b.last_write)
                    for r in b.reads:
                        self._add_dep(o, r)
            self.ops[eng].append(o)
        dst.dma_count += len(fns)
        ev = Event("dma", sem=dst.dma_sem, value=16 * dst.dma_count)
        for b in reads:
            b.reads.append(ev)
        for b in [dst] + list(extra_writes):
            b.last_write = ev
            b.reads = []
        return ev

    def emit(self, final_waits=()):
        nc = self.nc
        for b in final_waits:
            if b.last_write is not None and b.last_write.kind == "eng":
                b.last_write.op.needed = True
        for e in ENGS:
            c = 0
            for o in self.ops[e]:
                if o.needed:
                    c += 1
                    o.count = c
        sems = self.sems

        def run(e, engobj):
            for o in self.ops[e]:
                for ev in o.deps:
                    if ev.kind == "eng":
                        engobj.wait_ge(sems[ev.eng], ev.op.count)
                    else:
                        engobj.wait_ge(ev.sem, ev.value)
                inst = o.fn(engobj)
                if o.is_dma:
                    inst.then_inc(o.dma_sem, 16)
                elif o.needed:
                    inst.then_inc(sems[e], 1)
            if e == "sp":
                for b in final_waits:
                    ev = b.last_write
                    if ev is None:
                        continue
                    if ev.kind == "dma":
                        engobj.wait_ge(ev.sem, ev.value)
                    else:
                        engobj.wait_ge(sems[ev.eng], ev.op.count)

        with nc.Block() as block:
            @block.tensor
            def _(e):
                run("pe", e)

            @block.scalar
            def _(e):
                run("act", e)

            @block.vector
            def _(e):
                run("dve", e)

            @block.gpsimd
            def _(e):
                run("pool", e)

            @block.sync
            def _(e):
                run("sp", e)


def switch_bufs(old, new):
    evs = []
    for b in old:
        if b.last_write is not None:
            evs.append(b.last_write)
        evs.extend(b.reads)
    for b in new:
        b.last_write = None
        b.reads = list(evs)


def build(nl=DEPTH, stop=None):
    nc = bass.Bass("TRN2", target_bir_lowering=False)

    def din(name, shape, dt=F32):
        return nc.dram_tensor(name, list(shape), dt, kind="ExternalInput").ap()

    x_d = din("x", [S, D])
    cT_d = din("cT", [128, 8])
    gmix_d = din("gmix", [128, DEPTH * 8])
    gffn_d = din("gffn", [128, DEPTH * 8])
    gfin_d = din("gfin", [128, 8])
    wada_d = din("wada", [DEPTH * 24, 128, 2048])
    bada_d = din("bada", [128, DEPTH * 48])
    win_d = din("win", [DEPTH * NU_IN, 128, 1024])
    wf_d = din("wf", [DEPTH, 128, 64])
    bfor_d = din("bfor", [8, DEPTH])
    sinks_d = din("sinksb", [128, DEPTH * 8])
    biasc_d = din("biasc", [DEPTH * 4, 128, 2 * 5 * 128])
    wbr_d = din("wbr", [DEPTH * 24, 128, 512])
    wout_d = din("wout", [DEPTH * 8, 128, 1024])
    wfi_d = din("wfi", [DEPTH * 22, 128, 2048])
    wfo_d = din("wfo", [DEPTH * 16, 128, 1408])
    identf_d = din("identf", [128, 128])
    identb_d = din("identb", [128, 128], BF16)
    trimask_d = din("trimask", [128, 128], BF16)
    alibi_d = din("alibi", [128, 2 * 8 * 128], BF16)
    maskc_d = din("maskc", [128, 2 * 128], BF16)
    augc_d = din("augc", [6, S], BF16)
    out_d = nc.dram_tensor("out", [S, D], F32, kind="ExternalOutput").ap()

    P = Prog(nc)

    def sb(name, shape, dt):
        return nc.alloc_sbuf_tensor("sb_" + name, list(shape), dt)

    XT = sb("XT", [128, NCH, S], F32)
    HT = sb("HT", [128, NCH, S], BF16)
    W1 = sb("W1", [128, 24576], BF16)
    NS = 4
    WS = sb("WS", [128, NS, 2048], BF16)
    RS = sb("RS", [128, S], F32)
    PT = sb("PT", [128, 8, GS], BF16)
    TMPF = sb("TMPF", [128, 3, GS], F32)
    RD = sb("RD", [128, 2, GS], F32)
    CUM3 = sb("CUM3", [72, S], BF16)
    BIASC = sb("BIASC", [128, 2, 1280], BF16)
    identf = sb("identf", [128, 128], F32)
    identb = sb("identb", [128, 128], BF16)
    trimask = sb("trimask", [128, 128], BF16)
    alibi = sb("alibi", [128, 2, 8, 128], BF16)
    maskc = sb("maskc", [128, 2, 128], BF16)
    onesb = sb("onesb", [128, 128], BF16)
    ones8 = sb("ones8", [8, 1], F32)
    cT = sb("cTs", [128, 8], F32)
    condb = sb("condb", [128, 8], BF16)
    gmix = sb("gmixs", [128, DEPTH * 8], F32)
    gffn = sb("gffns", [128, DEPTH * 8], F32)
    gfin = sb("gfins", [128, 8], F32)
    bada = sb("badas", [128, DEPTH * 48], F32)
    modT = sb("modT", [128, DEPTH, 48], F32)
    gsc = sb("gsc", [128, DEPTH, 16], F32)
    bfor = sb("bfors", [8, DEPTH], F32)
    negb = sb("negb", [8, DEPTH], F32)
    sinkb = sb("sinkbs", [128, DEPTH * 8], F32)
    expsink = sb("expsink", [128, DEPTH * 8], F32)

    PS = [nc.alloc_psum_tensor("ps%d" % i, [128, GS], F32) for i in range(8)]
    psb = [Buf("ps%d" % i) for i in range(8)]

    xt = [[Buf() for _ in range(NG)] for _ in range(NCH)]
    ht = [[Buf() for _ in range(NG)] for _ in range(NCH)]
    wsb = [Buf("ws%d" % i) for i in range(NS)]
    rsb = [Buf() for _ in range(NG)]
    ptb = [Buf() for _ in range(8)]
    tmpb = [Buf() for _ in range(3)]
    rdb = [Buf() for _ in range(2)]
    cum3b = Buf()
    biascb = [Buf(), Buf()]
    cb = Buf("consts")
    modb = [Buf() for _ in range(DEPTH)]
    modb2 = [Buf() for _ in range(DEPTH)]
    outb = Buf("out")

    sl = lambda g: slice(g * GS, (g + 1) * GS)
    tl = lambda t: slice(t * 128, (t + 1) * 128)

    state = {"wo": 0, "ws": 0, "pt": 0, "ptn": 0, "tmp": 0, "rd": 0, "pj": 0, "gu": 0}

    def wload(src_ap, n):
        i = state["ws"] % NS
        state["ws"] += 1
        dst = WS[:, i, 0:n]
        P.dma("pool", lambda e, d=dst, s_=src_ap: e.dma_start(out=d, in_=s_, max_dma_last_dim=4096), wsb[i])
        return dst, wsb[i]

    def pe_group(insts, reads, writes):
        def fn(e, insts=insts):
            r = None
            for it in insts:
                (o, l, rr, st, sp) = it[:5]
                if len(rr.shape) == 3 and len(o.shape) == 2:
                    o = o.rearrange("p (a b) -> p a b", a=rr.shape[1])
                if len(it) > 5 and it[5]:
                    r = e.matmul(o, lhsT=l, rhs=rr, start=st, stop=sp, skip_group_check=True)
                else:
                    r = e.matmul(o, lhsT=l, rhs=rr, start=st, stop=sp)
            return r
        return P.op("pe", fn, reads=reads, writes=writes)

    def act(out, in_, func, reads, writes, bias=None, scale=None):
        kw = {}
        if bias is not None:
            kw["bias"] = bias
        if scale is not None:
            kw["scale"] = scale
        return P.op("act", lambda e, o=out, i=in_, f=func, kw=kw: e.activation(out=o, in_=i, func=f, **kw),
                    reads=reads, writes=writes)

    def dve_tt(out, in0, in1, op, reads, writes):
        return P.op("dve", lambda e, o=out, a=in0, b=in1, op=op: e.tensor_tensor(out=o, in0=a, in1=b, op=op),
                    reads=reads, writes=writes)

    def dve_stt(out, in0, scalar, in1, op0, op1, reads, writes):
        return P.op("dve", lambda e, o=out, a=in0, s_=scalar, b=in1, p0=op0, p1=op1:
                    e.scalar_tensor_tensor(out=o, in0=a, scalar=s_, in1=b, op0=p0, op1=p1),
                    reads=reads, writes=writes)

    def dve_ts(out, in0, s1, s2, op0, op1, reads, writes, eng="dve"):
        if op1 is None:
            return P.op(eng, lambda e, o=out, a=in0, s1=s1, p0=op0: e.tensor_scalar(out=o, in0=a, scalar1=s1, scalar2=None, op0=p0),
                        reads=reads, writes=writes)
        return P.op(eng, lambda e, o=out, a=in0, s1=s1, s2=s2, p0=op0, p1=op1:
                    e.tensor_scalar(out=o, in0=a, scalar1=s1, scalar2=s2, op0=p0, op1=p1),
                    reads=reads, writes=writes)

    def copy(eng, out, in_, reads, writes):
        if eng == "act":
            return act(out, in_, AF.Copy, reads, writes)
        return P.op(eng, lambda e, o=out, i=in_: e.tensor_copy(out=o, in_=i), reads=reads, writes=writes)

    def next_rot(key, n):
        i = state[key] % n
        state[key] += 1
        return i

    small_loads = [
        (identf[:, :], identf_d[:, :]), (identb[:, :], identb_d[:, :]), (trimask[:, :], trimask_d[:, :]),
        (alibi[:, :, :, :].rearrange("p a h q -> p (a h q)"), alibi_d[:, :]),
        (maskc[:, :, :].rearrange("p a q -> p (a q)"), maskc_d[:, :]),
        (cT[:, :], cT_d[:, :]), (gmix[:, :], gmix_d[:, :]), (gffn[:, :], gffn_d[:, :]), (gfin[:, :], gfin_d[:, :]),
        (bada[:, :], bada_d[:, :]), (bfor[:, :], bfor_d[:, :]), (sinkb[:, :], sinks_d[:, :]),
    ]
    P.dma("sp", [lambda e, o=o, i=i: e.dma_start(out=o, in_=i) for (o, i) in small_loads], cb)
    onesbuf = Buf()
    P.op("pool", lambda e: e.memset(onesb[:, :], 1.0), writes=[onesbuf])
    P.op("pool", lambda e: e.memset(ones8[:, :], 1.0), writes=[onesbuf])
    cvb = Buf()
    act(condb[:, :], cT[:, :], AF.Silu, [cb], [cvb])
    act(expsink[:, :], sinkb[:, :], AF.Exp, [cb], [cvb])
    dve_ts(negb[:, :], bfor[:, :], -1.0, None, ALU.mult, None, [cb], [cvb])

    W1f = W1[:, :].bitcast(F32)
    xsb = [Buf() for _ in range(4)]
    w1_bufs = list(xsb)
    for t in range(NT):
        s_ = t % 4
        xs = W1f[:, s_ * 1024:(s_ + 1) * 1024]
        P.dma("sp", lambda e, o=xs, t=t: e.dma_start(out=o, in_=x_d[t * 128:(t + 1) * 128, :]), xsb[s_])
        for half in range(2):
            bi = (2 * t + half) % 4
            insts = [(PS[bi][:, q * 128:(q + 1) * 128], xs[:, (half * 4 + q) * 128:(half * 4 + q + 1) * 128]) for q in range(4)]

            def tfn(e, insts=insts):
                r = None
                for (o, i) in insts:
                    r = e.transpose(out=o, in_=i, identity=identf[:, :])
                return r
            P.op("pe", tfn, reads=[xsb[s_], cb], writes=[psb[bi]])
            copy("dve" if half == 0 else "act",
                 XT[:, half * 4:half * 4 + 4, tl(t)],
                 PS[bi][:, :].rearrange("p (c t) -> p c t", c=4),
                 [psb[bi]], [xt[c][t // 4] for c in range(half * 4, half * 4 + 4)])

    def ada_block(l, jb):
        modps = PS[7]
        wv, wb = wload(wada_d[l * 24 + jb], 2048)
        wv3 = wv.rearrange("p (k c) -> p k c", k=8)
        for jj in range(2):
            j = 2 * jb + jj
            insts = [(modps[:, j:j + 1], wv3[:, kc, jj * 128:(jj + 1) * 128], condb[:, kc:kc + 1], kc == 0, kc == 7)
                     for kc in range(8)]
            pe_group(insts, [wb, cvb], [psb[7]])

    def ada_finish(l, part):
        modps = PS[7]
        if part == 0:
            dve_tt(modT[:, l, 0:16], modps[:, 0:16], bada[:, l * 48:l * 48 + 16], ALU.add, [psb[7], cb], [modb[l]])
            dve_stt(gsc[:, l, 0:8], modT[:, l, 8:16], 1.0, gmix[:, l * 8:(l + 1) * 8], ALU.add, ALU.mult, [modb[l], cb], [modb[l]])
        else:
            dve_tt(modT[:, l, 16:48], modps[:, 16:48], bada[:, l * 48 + 16:(l + 1) * 48], ALU.add, [psb[7], cb], [modb2[l]])
            dve_stt(gsc[:, l, 8:16], modT[:, l, 32:40], 1.0, gffn[:, l * 8:(l + 1) * 8], ALU.add, ALU.mult, [modb2[l], cb], [modb2[l]])

    ada_rest = []

    def ada_first(l):
        for jb in range(8):
            ada_block(l, jb)
        ada_finish(l, 0)
        ada_rest.extend(range(8, 24))

    def ada_more(l, n):
        for _ in range(n):
            if ada_rest:
                ada_block(l, ada_rest.pop(0))
                if not ada_rest:
                    ada_finish(l, 1)

    def norm_group(g, gs_ap, sh_ap, vec_bufs, dst):
        bi = 5 + (g % 2)
        for c in range(NCH):
            i = next_rot("ptn", 4)
            if c % 2 == 0:
                act(PT[:, i, :], XT[:, c, sl(g)], AF.Square, [xt[c][g]], [ptb[i]])
            else:
                P.op("pool", lambda e, i=i, c=c, g=g: e.tensor_tensor(out=PT[:, i, :], in0=XT[:, c, sl(g)], in1=XT[:, c, sl(g)],
                                                                     op=ALU.mult), reads=[xt[c][g]], writes=[ptb[i]])
            pe_group([(PS[bi][:, :], onesb[:, :], PT[:, i, :], c == 0, c == NCH - 1)], [ptb[i], onesbuf], [psb[bi]])
        ti = next_rot("tmp", 3)
        act(TMPF[:, ti, :], PS[bi][:, :], AF.Ln, [psb[bi]], [tmpb[ti]], bias=EPS, scale=1.0 / D)
        act(RS[:, sl(g)], TMPF[:, ti, :], AF.Exp, [tmpb[ti]], [rsb[g]], scale=-0.5)
        for c in range(NCH):
            ti = next_rot("tmp", 3)
            dve_stt(TMPF[:, ti, :], XT[:, c, sl(g)], gs_ap[:, c:c + 1], RS[:, sl(g)], ALU.mult, ALU.mult,
                    [xt[c][g], rsb[g]] + vec_bufs, [tmpb[ti]])
            if dst is None:
                act(HT[:, c, sl(g)], TMPF[:, ti, :], AF.Identity, [tmpb[ti]] + vec_bufs, [ht[c][g]],
                    bias=sh_ap[:, c:c + 1], scale=1.0)
            else:
                dst(c, g, ti)

    def norm(gs_ap, sh_ap, vec_bufs, dst=None, after_group=None, skew=1):
        for g in range(NG):
            norm_group(g, gs_ap, sh_ap, vec_bufs, dst)
            if after_group is not None and g - skew >= 0:
                after_group(g - skew)
        if after_group is not None:
            for g in range(max(0, NG - skew), NG):
                after_group(g)

    def proj_fm(wv3, wb, g, nk=8, src=None, srcb=None):
        bi = 5 + next_rot("pj", 2 if ada_rest else 3)
        if src is None:
            src, srcb = HT, ht
        insts = [(PS[bi][:, :], wv3[:, kc, :], src[:, kc, sl(g)], kc == 0, kc == nk - 1) for kc in range(nk)]
        pe_group(insts, [wb] + [srcb[kc][g] for kc in range(nk)], [psb[bi]])
        return bi

    def proj_v(wv3, wb, t4, VB, vbufs):
        bi = 5 + next_rot("pj", 2 if ada_rest else 3)
        insts = []
        for q in range(4):
            t = t4 * 4 + q
            for kc in range(8):
                insts.append((PS[bi][:, q * 128:(q + 1) * 128], HT[:, kc, tl(t)], wv3[:, kc, :], kc == 0, kc == 7))
        pe_group(insts, [wb] + [ht[kc][t4] for kc in range(8)], [psb[bi]])
        pv = PS[bi][:, :].rearrange("p (q c) -> p q c", q=4)
        copy("dve", VB[:, t4 * 4:t4 * 4 + 4, 0:64], pv[:, :, 0:64], [psb[bi]], [vbufs[t4]])
        copy("dve", VB[:, t4 * 4:t4 * 4 + 4, 128:192], pv[:, :, 64:128], [psb[bi]], [vbufs[t4]])

    def attn_finish(obi, lo_num, out_ap, out_bufs, extra_reads, c0=0, sink_cols=None):
        num = slice(lo_num, lo_num + 64)
        den = slice(64 - lo_num, 128 - lo_num)
        ri = next_rot("rd", 2)
        if sink_cols is None:
            act(RD[num, ri, c0:], PS[obi][den, c0:], AF.Ln, [psb[obi]], [rdb[ri]])
        else:
            for hq in range(4):
                act(RD[num, ri, hq * 128:(hq + 1) * 128], PS[obi][den, hq * 128:(hq + 1) * 128], AF.Ln,
                    [psb[obi], cvb], [rdb[ri]], bias=expsink[num, sink_cols[hq]:sink_cols[hq] + 1], scale=1.0)
        act(RD[num, ri, c0:], RD[num, ri, c0:], AF.Exp, [rdb[ri]], [rdb[ri]], scale=-1.0)
        if sink_cols is None:
            dve_tt(out_ap, PS[obi][num, c0:], RD[num, ri, c0:], ALU.mult, [psb[obi], rdb[ri]] + extra_reads, out_bufs)
        else:
            dve_tt(out_ap, PS[obi][num, :].rearrange("p (h q) -> p h q", h=4),
                   RD[num, ri, :].rearrange("p (h q) -> p h q", h=4), ALU.mult,
                   [psb[obi], rdb[ri]] + extra_reads, out_bufs)

    OT = W1[:, 0:8192].rearrange("p (c t) -> p c t", c=4)
    MGR = W1[:, 8192:24576]
    MG = MGR.rearrange("p (c t) -> p c t", c=8)
    region = {"mg": list(w1_bufs), "ot": []}
    ot = [[Buf() for _ in range(NG)] for _ in range(4)]
    switch_bufs(w1_bufs, [b for r in ot for b in r])
    region["w1all"] = None

    def mg_switch(new):
        switch_bufs(region["mg"], new)
        region["mg"] = new

    def merge_branch(l, k, post_group=None):
        mg = [[Buf() for _ in range(NG)] for _ in range(NCH)]
        mg_switch([b for r in mg for b in r])
        for dc in range(NCH):
            wg, wgb = wload(win_d[l * NU_IN + 30 + k * 8 + dc], 1024)
            wg3 = wg.rearrange("p (k c) -> p k c", k=8)
            wbv, wbb = wload(wbr_d[l * 24 + k * 8 + dc], 512)
            wb3 = wbv.rearrange("p (k c) -> p k c", k=4)
            for g in range(NG):
                yb = g % 2
                gb = 2 + (g % 2)
                pe_group([(PS[yb][:, :], wb3[:, kc, :], OT[:, kc, sl(g)], kc == 0, kc == 3) for kc in range(4)],
                         [wbb] + [ot[kc][g] for kc in range(4)], [psb[yb]])
                pe_group([(PS[gb][:, :], wg3[:, kc, :], HT[:, kc, sl(g)], kc == 0, kc == 7) for kc in range(8)],
                         [wgb] + [ht[kc][g] for kc in range(8)], [psb[gb]])
                ti = next_rot("tmp", 3)
                act(TMPF[:, ti, :], PS[gb][:, :], AF.Sigmoid, [psb[gb]], [tmpb[ti]])
                dve_tt(MG[:, dc, sl(g)], PS[yb][:, :], TMPF[:, ti, :], ALU.mult, [psb[yb], tmpb[ti]], [mg[dc][g]])
        if post_group is None:
            for dco in range(NCH):
                wo, wob = wload(wout_d[l * 8 + dco], 1024)
                wo3 = wo.rearrange("p (k c) -> p k c", k=8)
                for g in range(NG):
                    bi = 4 + next_rot("pj", 4)
                    pe_group([(PS[bi][:, :], wo3[:, kc, :], MG[:, kc, sl(g)], kc == 0, kc == 7) for kc in range(8)],
                             [wob] + [mg[kc][g] for kc in range(8)], [psb[bi]])
                    dve_stt(XT[:, dco, sl(g)], PS[bi][:, :], modT[:, l, 16 + dco:17 + dco], XT[:, dco, sl(g)], ALU.mult, ALU.add,
                            [psb[bi], modb2[l], xt[dco][g]], [xt[dco][g]])
        else:
            wos = []
            for pr in range(4):
                i = state["ws"] % NS
                state["ws"] += 1
                P.dma("pool", [lambda e, i=i, h=h, pr=pr: e.dma_start(out=WS[:, i, h * 1024:(h + 1) * 1024],
                                                                      in_=wout_d[l * 8 + 2 * pr + h], max_dma_last_dim=4096)
                               for h in range(2)], wsb[i])
                for h in range(2):
                    wos.append((WS[:, i, h * 1024:(h + 1) * 1024].rearrange("p (k c) -> p k c", k=8), wsb[i]))
            obanks = [0, 1, 2, 3, 4, 7]
            for g in range(NG):
                for dco in range(NCH):
                    wo3, wob = wos[dco]
                    bi = obanks[next_rot("wo", len(obanks))]
                    pe_group([(PS[bi][:, :], wo3[:, kc, :], MG[:, kc, sl(g)], kc == 0, kc == 7) for kc in range(8)],
                             [wob] + [mg[kc][g] for kc in range(8)], [psb[bi]])
                    dve_stt(XT[:, dco, sl(g)], PS[bi][:, :], modT[:, l, 16 + dco:17 + dco], XT[:, dco, sl(g)], ALU.mult, ALU.add,
                            [psb[bi], modb2[l], xt[dco][g]], [xt[dco][g]])
                post_group(g)

    SBANKS = [0, 1, 2, 5, 6, 7]
    LOOK = 4

    def s_exp_pv(k_insts, nk_reads, c0, ncols, obi, v_lhsT, v_reads, first, last, pending, skip=False):
        sbl = state["sbanks"]
        sbi = sbl[next_rot("sb", len(sbl))]
        insts = [(PS[sbi][:, c0:c0 + ncols] if o is None else o, l_, r_, st, sp) for (o, l_, r_, st, sp) in k_insts(sbi)]
        pe_group(insts, nk_reads, [psb[sbi]])
        pi = next_rot("pt", 8)
        act(PT[:, pi, c0:c0 + ncols], PS[sbi][:, c0:c0 + ncols], AF.Exp, [psb[sbi]], [ptb[pi]])
        pending.append(("pv", [(PS[obi][:, c0:c0 + ncols], v_lhsT, PT[:, pi, c0:c0 + ncols], first, last, skip)],
                        [ptb[pi]] + v_reads, [psb[obi]]))

    def push_fin(pending, fn):
        pending.append(("fin", fn))

    def flush(pending, keep):
        def npv():
            return sum(1 for it in pending if it[0] == "pv")
        while pending and (npv() > keep or pending[0][0] == "fin"):
            it = pending.pop(0)
            if it[0] == "pv":
                pe_group(it[1], it[2], it[3])
            else:
                it[1]()

    state["sb"] = 0
    state["ob"] = 0
    state["oba"] = 0
    state["sbanks"] = SBANKS

    def branch_A(l):
        QA = MGR[:, 0:8192].rearrange("p (c t) -> p c t", c=4)
        KA = [MGR[:, 8192:10240], MGR[:, 10240:12288]]
        VA = MGR[:, 12288:15360].rearrange("p (t c) -> p t c", t=16)
        qab = [[Buf() for _ in range(NG)] for _ in range(4)]
        kab = [[Buf() for _ in range(NG)] for _ in range(2)]
        vab = [Buf() for _ in range(4)]
        vones = Buf()
        mg_switch([b for r in qab for b in r] + [b for r in kab for b in r] + vab + [vones])
        P.op("pool", lambda e: e.memset(VA[:, :, 64:128], 1.0), writes=[vones] + vab)
        for kv in range(2):
            P.op("pool", lambda e, kv=kv: e.memset(KA[kv], 0.0), writes=kab[kv])
        base = l * NU_IN

        def qproj(c, g, wv3, wb):
            bi = proj_fm(wv3, wb, g)
            dve_ts(QA[:, c, sl(g)], PS[bi][:, :], 0.125, None, ALU.mult, None, [psb[bi]], [qab[c][g]])
        pro = []
        for c in range(3):
            wv, wb = wload(win_d[base + c], 1024)
            pro.append((c, wv.rearrange("p (k c) -> p k c", k=8), wb))

        def after_group(g):
            for (c, wv3, wb) in pro:
                qproj(c, g, wv3, wb)
        norm(gsc[:, l, 0:8], modT[:, l, 0:8], [modb[l]], after_group=after_group)
        ada_more(l, 7)
        wv, wb = wload(win_d[base + 3], 1024)
        wv3 = wv.rearrange("p (k c) -> p k c", k=8)
        for g in range(NG):
            qproj(3, g, wv3, wb)
        ada_more(l, 3)
        wv, wb = wload(win_d[base + 4], 1024)
        wv3 = wv.rearrange("p (k c) -> p k c", k=8)
        for g in range(NG):
            bi = proj_fm(wv3, wb, g)
            copy("dve", KA[0][0:64, sl(g)], PS[bi][0:64, :], [psb[bi]], [kab[0][g]])
            copy("dve", KA[1][64:128, sl(g)], PS[bi][64:128, :], [psb[bi]], [kab[1][g]])
        ada_more(l, 3)
        wv, wb = wload(win_d[base + 5], 1024)
        wv3 = wv.rearrange("p (k c) -> p k c", k=8)
        for t4 in range(4):
            proj_v(wv3, wb, t4, VA, vab)
        ada_more(l, 99)
        pending = []
        state["sbanks"] = [0, 1, 2, 5]
        obl = [3, 4, 6, 7]
        for m in range(NT):
            for kv in range(2):
                rows = slice(kv * 64, kv * 64 + 64)
                obi = obl[next_rot("oba", len(obl))]
                vsl = slice(0, 128) if kv == 0 else slice(64, 192)
                kts = [m] if m == 0 else [m - 1, m]
                for ji, j in enumerate(kts):
                    kind = m - j

                    def k_insts(sbi, j=j, kind=kind, rows=rows, kv=kv, m=m):
                        return [
                            (None, KA[kv][:, tl(j)], QA[:, :, tl(m)], True, False),
                            (None, identb[:, :], alibi[:, kind, kv * 4:kv * 4 + 4, :], False, True),
                        ]
                    s_exp_pv(k_insts, [kab[kv][j // 4], cb] + [qab[c][m // 4] for c in range(4)], 0, GS, obi,
                             VA[:, j, vsl], [vab[j // 4], vones], ji == 0, ji == len(kts) - 1, pending)
                    flush(pending, LOOK)
                sink_cols = [l * 8 + kv * 4 + hq for hq in range(4)]
                push_fin(pending, lambda obi=obi, kv=kv, rows=rows, m=m, sink_cols=sink_cols:
                         attn_finish(obi, kv * 64, OT[rows, :, tl(m)], [ot[c][m // 4] for c in range(4)], [], sink_cols=sink_cols))
        flush(pending, 0)
        state["sbanks"] = SBANKS

    def branch_BC(l, which):
        is_b = which == "B"
        base = l * NU_IN + (6 if is_b else 18)
        QK = [MGR[:, i * 2048:(i + 1) * 2048] for i in range(4)]
        VB = MGR[:, 8192:11264].rearrange("p (t c) -> p t c", t=16)
        qb = [[Buf() for _ in range(NG)] for _ in range(4)]
        augb = [Buf() for _ in range(4)]
        vbb = [Buf() for _ in range(4)]
        vones = Buf()
        mg_switch([b for r in qb for b in r] + augb + vbb + [vones])
        P.op("pool", lambda e: e.memset(VB[:, :, 64:128], 1.0), writes=[vones] + vbb)
        if is_b:
            wfv, wfb = wload(wf_d[l], 64)
            wf3 = wfv.rearrange("p (k c) -> p k c", k=8)
            A8 = RS[0:8, :]
            for g in range(NG):
                bi = 5 + next_rot("pj", 3)
                pe_group([(PS[bi][0:8, :], wf3[:, kc, :], HT[:, kc, sl(g)], kc == 0, kc == 7) for kc in range(8)],
                         [wfb] + [ht[kc][g] for kc in range(8)], [psb[bi]])
                act(A8[:, sl(g)], PS[bi][0:8, :], AF.Exp, [psb[bi], cvb], [rsb[g]], bias=negb[:, l:l + 1], scale=-1.0)
                act(A8[:, sl(g)], A8[:, sl(g)], AF.Ln, [rsb[g]], [rsb[g]], bias=1.0, scale=1.0)
            P.op("dve", lambda e: e.tensor_tensor_scan(out=A8, data0=ones8[:, 0:1].to_broadcast([8, S]), data1=A8,
                                                       initial=0.0, op0=ALU.mult, op1=ALU.subtract),
                 reads=rsb + [onesbuf], writes=rsb)
            TB = PT[0:8, 0:4, :].rearrange("p a b -> p (a b)")
            copy("dve", CUM3[0:8, :], A8, rsb, [cum3b])
            dve_tt(A8, A8, CUM3[0:8, :], ALU.subtract, rsb + [cum3b], rsb)
            copy("dve", TB, A8, rsb, ptb)
            dve_tt(A8, A8, TB, ALU.subtract, rsb + ptb[0:4], rsb)
            copy("act", CUM3[32:40, :], TB, ptb[0:4], [cum3b])
            copy("dve", TB, A8, rsb, ptb)
            copy("act", CUM3[64:72, :], TB, ptb[0:4], [cum3b])
        for i in range(4):
            P.op("pool", lambda e, i=i: e.memset(QK[i], 0.0), writes=qb[i] + [augb[i]])
        if is_b:
            cr = [(0, 67, 0), (1, 3, 0), (2, 64, 3), (3, 0, 3)]
            for (i, r0, a0) in cr:
                P.dma("sp", lambda e, i=i, r0=r0, a0=a0: e.dma_start(out=QK[i][r0:r0 + 3, :], in_=augc_d[a0:a0 + 3, :]), augb[i])
        for p in range(4):
            if not is_b:
                bs = p % 2
                P.dma("pool", lambda e, bs=bs, p=p: e.dma_start(out=BIASC[:, bs, :], in_=biasc_d[l * 4 + p], max_dma_last_dim=4096),
                      biascb[bs])
            if is_b:
                he, ho = 2 * p, 2 * p + 1
                mv = [(0, 64, he), (1, 0, ho), (2, 67, he), (3, 3, ho)]
                for (i, r0, h) in mv:
                    P.dma("sp", [lambda e, i=i, r0=r0, h=h, q=q: e.dma_start(out=QK[i][r0 + q:r0 + q + 1, :],
                                                                             in_=CUM3[32 * q + h:32 * q + h + 1, :])
                                 for q in range(3)], augb[i], reads=[cum3b])
            wq, wqb = wload(win_d[base + p * 3 + 0], 1024)
            wq3 = wq.rearrange("p (k c) -> p k c", k=8)
            for g in range(NG):
                bi = proj_fm(wq3, wqb, g)
                dve_ts(QK[0][0:64, sl(g)], PS[bi][0:64, :], 0.125, None, ALU.mult, None, [psb[bi]], [qb[0][g]])
                dve_ts(QK[1][64:128, sl(g)], PS[bi][64:128, :], 0.125, None, ALU.mult, None, [psb[bi]], [qb[1][g]])
            wk, wkb = wload(win_d[base + p * 3 + 1], 1024)
            wk3 = wk.rearrange("p (k c) -> p k c", k=8)
            for g in range(NG):
                bi = proj_fm(wk3, wkb, g)
                copy("dve", QK[2][0:64, sl(g)], PS[bi][0:64, :], [psb[bi]], [qb[2][g]])
                copy("dve", QK[3][64:128, sl(g)], PS[bi][64:128, :], [psb[bi]], [qb[3][g]])
            wv, wvb = wload(win_d[base + p * 3 + 2], 1024)
            wv3 = wv.rearrange("p (k c) -> p k c", k=8)
            for t4 in range(4):
                proj_v(wv3, wvb, t4, VB, vbb)
            pending = []
            for hh in range(2):
                rows = slice(hh * 64, hh * 64 + 64)
                vsl = slice(0, 128) if hh == 0 else slice(64, 192)
                for g in range(NG):
                    obi = 3 + next_rot("ob", 2)
                    if is_b:
                        kts = list(range(0, 4 * g + 4))
                    else:
                        kts = list(range(max(0, 4 * g - 4), 4 * g + 4))
                    for ji, j in enumerate(kts):
                        if is_b:
                            r = j - 4 * g
                            c0 = 128 * r if r > 0 else 0
                            ncols = GS - c0
                            Qt, Kt = QK[hh], QK[2 + hh]

                            def k_insts(sbi, j=j, r=r, c0=c0, ncols=ncols, Qt=Qt, Kt=Kt, g=g):
                                ins = [(None, Kt[:, tl(j)], Qt[:, g * GS + c0:(g + 1) * GS], True, r < 0)]
                                if r >= 0:
                                    ins.append((PS[sbi][:, c0:c0 + 128], identb[:, :], trimask[:, :], False, True))
                                return ins
                            kreads = [qb[hh][g], qb[2 + hh][j // 4], augb[hh], augb[2 + hh], cb]
                        else:
                            m0 = max(j, 4 * g)
                            m1 = min(j + 4, 4 * g + 3)
                            c0 = (m0 - 4 * g) * 128
                            ncols = (m1 - m0 + 1) * 128
                            k0 = m0 - j
                            k1 = m1 - j
                            bs = p % 2

                            def k_insts(sbi, j=j, c0=c0, ncols=ncols, k0=k0, k1=k1, rows=rows, g=g, hh=hh, bs=bs):
                                ins = [(None, QK[2 + hh][:, tl(j)], QK[hh][:, g * GS + c0:g * GS + c0 + ncols], True, False)]
                                has0 = (k0 == 0)
                                has4 = (k1 == 4)
                                ins.append((None, identb[:, :], BIASC[:, bs, hh * 640 + k0 * 128:hh * 640 + (k1 + 1) * 128],
                                            False, not (has0 or has4)))
                                if has0:
                                    ins.append((PS[sbi][:, c0:c0 + 128], identb[:, :], maskc[:, 0, :], False, not has4))
                                if has4:
                                    ins.append((PS[sbi][:, c0 + ncols - 128:c0 + ncols], identb[:, :], maskc[:, 1, :], False, True))
                                return ins
                            kreads = [qb[hh][g], qb[2 + hh][j // 4], biascb[bs], cb]
                        s_exp_pv(k_insts, kreads, c0, ncols, obi, VB[:, j, vsl], [vbb[j // 4], vones],
                                 ji == 0, ji == len(kts) - 1, pending, skip=not is_b)
                        flush(pending, LOOK)
                    push_fin(pending, lambda obi=obi, hh=hh, rows=rows, p=p, g=g:
                             attn_finish(obi, hh * 64, OT[rows, p, sl(g)], [ot[p][g]], []))
            flush(pending, 0)

    def ffn(l, next_ada, norm_done=False):
        AT = W1[:, 0:22528].rearrange("p (j t) -> p j t", j=11)
        ada_todo = list(range(24)) if next_ada else []
        for J in range(2):
            at = [[Buf() for _ in range(NG)] for _ in range(11)]
            allb = [b for r in at for b in r]
            if J == 0:
                switch_bufs(region["mg"] + [b for r in ot for b in r], allb)
            else:
                switch_bufs(region["mg"], allb)
            region["mg"] = allb
            def gu_tile(jj, g, w4, wb, at=at):
                r2 = next_rot("gu", 2)
                gb = r2
                ub = 2 + r2
                pe_group([(PS[gb][:, :], w4[:, 0, kc, :], HT[:, kc, sl(g)], kc == 0, kc == 7) for kc in range(8)],
                         [wb] + [ht[kc][g] for kc in range(8)], [psb[gb]])
                pe_group([(PS[ub][:, :], w4[:, 1, kc, :], HT[:, kc, sl(g)], kc == 0, kc == 7) for kc in range(8)],
                         [wb] + [ht[kc][g] for kc in range(8)], [psb[ub]])
                ti = next_rot("tmp", 3)
                act(TMPF[:, ti, :], PS[gb][:, :], AF.Silu, [psb[gb]], [tmpb[ti]])
                dve_tt(AT[:, jj, sl(g)], PS[ub][:, :], TMPF[:, ti, :], ALU.mult, [psb[ub], tmpb[ti]], [at[jj][g]])

            def ada_step(j):
                for _ in range(2 if j < 2 else 1):
                    if ada_todo:
                        ada_block(l + 1, ada_todo.pop(0))
            jstart = 0
            if J == 0 and not norm_done:
                pro = []
                for jj in range(3):
                    wv, wb = wload(wfi_d[l * 22 + jj], 2048)
                    pro.append((jj, wv.rearrange("p (a k c) -> p a k c", a=2, k=8), wb))

                def after_group(g, pro=pro):
                    for (jj, w4, wb) in pro:
                        gu_tile(jj, g, w4, wb)
                norm(gsc[:, l, 8:16], modT[:, l, 24:32], [modb2[l]], after_group=after_group)
                for jj in range(3):
                    ada_step(jj)
                jstart = 3
            for jj in range(jstart, 11):
                j = J * 11 + jj
                wv, wb = wload(wfi_d[l * 22 + j], 2048)
                w4 = wv.rearrange("p (a k c) -> p a k c", a=2, k=8)
                for g in range(NG):
                    gu_tile(jj, g, w4, wb)
                ada_step(j)
            for dco in range(NCH):
                wo, wob = wload(wfo_d[l * 16 + J * 8 + dco], 1408)
                wo3 = wo.rearrange("p (j c) -> p j c", j=11)
                for g in range(NG):
                    bi = 4 + next_rot("pj", 3)
                    pe_group([(PS[bi][:, :], wo3[:, jj, :], AT[:, jj, sl(g)], jj == 0, jj == 10) for jj in range(11)],
                             [wob] + [at[jj][g] for jj in range(11)], [psb[bi]])
                    dve_stt(XT[:, dco, sl(g)], PS[bi][:, :], modT[:, l, 40 + dco:41 + dco], XT[:, dco, sl(g)], ALU.mult, ALU.add,
                            [psb[bi], modb2[l], xt[dco][g]], [xt[dco][g]])
        if next_ada:
            assert not ada_todo
            ada_finish(l + 1, 0)
            ada_finish(l + 1, 1)
        newot = [b for r in ot for b in r]
        dummy = [Buf()]
        switch_bufs(region["mg"], newot + dummy)
        region["mg"] = dummy

    for l in range(nl):
        if l == 0:
            ada_first(l)
        branch_A(l)
        merge_branch(l, 0)
        if stop == "a%d" % l:
            break
        branch_BC(l, "B")
        merge_branch(l, 1)
        if stop == "b%d" % l:
            break
        branch_BC(l, "C")
        if stop == "m%d" % l:
            merge_branch(l, 2)
            break
        merge_branch(l, 2, post_group=lambda g, l=l: norm_group(g, gsc[:, l, 8:16], modT[:, l, 24:32], [modb2[l]], None))
        ffn(l, l + 1 < nl, norm_done=True)

    fin_bufs = [Buf() for _ in range(3)]
    outbs = [Buf() for _ in range(3)]
    switch_bufs(region["mg"] + [b for r in ot for b in r], fin_bufs)
    YS = [W1f[:, i * 1024:(i + 1) * 1024] for i in range(3)]
    def out_tiles(g):
        for t in range(4 * g, 4 * g + 4):
            si = t % 3
            for half in range(2):
                bi = (2 * t + half) % 4
                insts = [(PS[bi][:, q * 128:(q + 1) * 128], XT[:, half * 4 + q, tl(t)]) for q in range(4)]

                def tfn(e, insts=insts):
                    r = None
                    for (o, i) in insts:
                        r = e.transpose(out=o, in_=i, identity=identf[:, :])
                    return r
                P.op("pe", tfn, reads=[xt[half * 4 + q][t // 4] for q in range(4)] + [cb], writes=[psb[bi]])
                copy("dve" if half == 0 else "act", YS[si][:, half * 512:(half + 1) * 512], PS[bi][:, :], [psb[bi]], [fin_bufs[si]])
            P.dma("sp", lambda e, t=t, si=si: e.dma_start(out=out_d[t * 128:(t + 1) * 128, :], in_=YS[si]), outbs[si], reads=[fin_bufs[si]])

    if stop is None:
        def dst(c, g, ti):
            copy("act", XT[:, c, sl(g)], TMPF[:, ti, :], [tmpb[ti]], [xt[c][g]])
        norm(gfin, None, [cb], dst=dst, after_group=out_tiles)
    else:
        for g in range(NG):
            out_tiles(g)
    P.emit(final_waits=outbs)
    return nc


def _win_cols():
    units = []
    for c in range(4):
        units.append(list(range(c * 64, c * 64 + 64)) + list(range((c + 4) * 64, (c + 4) * 64 + 64)))
    units.append(list(range(512, 640)))
    units.append(list(range(640, 768)))
    for p in range(4):
        units.append(list(range(768 + p * 128, 768 + (p + 1) * 128)))
        units.append(list(range(1280 + p * 128, 1280 + (p + 1) * 128)))
        units.append(list(range(1792 + p * 128, 1792 + (p + 1) * 128)))
    for p in range(4):
        units.append(list(range(2312 + p * 128, 2312 + (p + 1) * 128)))
        units.append(list(range(2824 + p * 128, 2824 + (p + 1) * 128)))
        units.append(list(range(3336 + p * 128, 3336 + (p + 1) * 128)))
    for k in range(3):
        for dc in range(8):
            units.append(list(range(3848 + k * 1024 + dc * 128, 3848 + k * 1024 + (dc + 1) * 128)))
    assert len(units) == NU_IN
    return np.array(units)


def _const_tables():
    bf = ml_dtypes.bfloat16
    identf = np.eye(128, dtype=np.float32)
    identb = np.eye(128).astype(bf)
    s = np.arange(128)[:, None]
    q = np.arange(128)[None, :]
    trimask = np.where(s > q, NEG, 0.0).astype(bf)
    slopes = 2.0 ** (-(np.arange(1, 9)))
    alibi = np.zeros((128, 2, 8, 128), np.float32)
    for kind in range(2):
        dist = 128 * kind + q - s
        msk = ((s >= 64) & (q < 64)) if kind == 0 else ((s < 64) & (q >= 64))
        for h in range(8):
            alibi[:, kind, h, :] = np.where(msk, NEG, -slopes[h] * np.abs(dist))
    maskc = np.zeros((128, 2, 128), np.float32)
    maskc[:, 0, :] = np.where((s >= 64) & (q < 64), NEG, 0.0)
    maskc[:, 1, :] = np.where((s < 64) & (q >= 64), NEG, 0.0)
    augc = np.concatenate([-np.ones((3, S), np.float32), np.ones((3, S), np.float32)], 0)
    return dict(identf=identf, identb=identb, trimask=trimask, alibi=alibi.reshape(128, -1).astype(bf),
                maskc=maskc.reshape(128, -1).astype(bf), augc=augc.astype(bf))


def _prep_shared(inp):
    f32 = np.float32
    vecT = lambda v: np.ascontiguousarray(v.reshape(-1, 8, 128).transpose(2, 0, 1).reshape(128, -1)).astype(f32)
    sh = {}
    sh["gmix"] = vecT(inp["norm_mix_g"])
    sh["gffn"] = vecT(inp["norm_ffn_g"])
    sh["gfin"] = vecT(inp["final_norm_g"][None])
    w_ada = inp["w_ada"]
    sh["wada"] = np.ascontiguousarray(
        w_ada.reshape(DEPTH, 8, 128, 24, 256).transpose(0, 3, 2, 1, 4).reshape(DEPTH * 24, 128, 2048))
    sh["bada"] = np.ascontiguousarray(inp["b_ada"].reshape(DEPTH, 48, 128).transpose(2, 0, 1).reshape(128, DEPTH * 48))
    cols = _win_cols()
    w_in = inp["w_in"]
    wu = w_in[:, :, cols]
    sh["win"] = np.ascontiguousarray(
        wu.reshape(DEPTH, 8, 128, NU_IN, 128).transpose(0, 3, 2, 1, 4).reshape(DEPTH * NU_IN, 128, 1024))
    wfc = w_in[:, :, 2304:2312]
    sh["wf"] = np.ascontiguousarray(wfc.reshape(DEPTH, 8, 128, 8).transpose(0, 2, 1, 3).reshape(DEPTH, 128, 64))
    sh["bfor"] = np.ascontiguousarray(inp["b_forget"].T)
    sh["sinksb"] = np.ascontiguousarray(np.broadcast_to(inp["sinks"].reshape(1, DEPTH * 8), (128, DEPTH * 8)))
    rb = inp["rel_bias"]
    s = np.arange(128)[:, None]
    q = np.arange(128)[None, :]
    idx = np.stack([np.clip(128 * kind + q - s, -128, 128) + 128 for kind in range(5)], 0)
    tb = rb[:, :, idx]
    tb = tb.reshape(DEPTH, 4, 2, 5, 128, 128).transpose(0, 1, 4, 2, 3, 5)
    sh["biasc"] = np.ascontiguousarray(tb.reshape(DEPTH * 4, 128, 1280))
    wbr = inp["w_branch"].copy()
    permA = np.concatenate([np.concatenate([np.arange(c * 64, c * 64 + 64), np.arange((c + 4) * 64, (c + 4) * 64 + 64)])
                            for c in range(4)])
    wbr[:, 0] = wbr[:, 0][:, permA, :]
    sh["wbr"] = np.ascontiguousarray(
        wbr.reshape(DEPTH, 3, 4, 128, 8, 128).transpose(0, 1, 4, 3, 2, 5).reshape(DEPTH * 24, 128, 512))
    sh["wout"] = np.ascontiguousarray(
        inp["w_out"].reshape(DEPTH, 8, 128, 8, 128).transpose(0, 3, 2, 1, 4).reshape(DEPTH * 8, 128, 1024))
    wfi = inp["w_ffn_in"].reshape(DEPTH, 8, 128, 2, 22, 128)
    sh["wfi"] = np.ascontiguousarray(wfi.transpose(0, 4, 2, 3, 1, 5).reshape(DEPTH * 22, 128, 2048))
    wfo = inp["w_ffn_out"].reshape(DEPTH, 2, 11, 128, 8, 128)
    sh["wfo"] = np.ascontiguousarray(wfo.transpose(0, 1, 4, 3, 2, 5).reshape(DEPTH * 16, 128, 1408))
    sh.update(_const_tables())
    return {k: (v if v.dtype != np.float64 else v.astype(f32)) for k, v in sh.items()}


_NC_CACHE = {}


def kernel(**inputs):
    inp = {k: np.asarray(v) for k, v in inputs.items()}
    sh = _prep_shared(inp)
    key = "full"
    if key not in _NC_CACHE:
        _NC_CACHE[key] = build()
    nc = _NC_CACHE[key]
    in_maps = []
    for b in range(8):
        m = dict(sh)
        m["x"] = np.ascontiguousarray(inp["x"][b]).astype(np.float32)
        m["cT"] = np.ascontiguousarray(inp["c"][b].reshape(8, 128).T).astype(np.float32)
        in_maps.append(m)
    res = run_bass_kernel_spmd(nc, in_maps, core_ids=list(range(8)))
    out = np.stack([np.asarray(res.results[b]["out"]) for b in range(8)], 0)
    return out.astype(np.float32)
```

```python
import numpy as np
import ml_dtypes
import concourse.bass as bass
import concourse.mybir as mybir
from concourse.bass_utils import run_bass_kernel_spmd

F32 = mybir.dt.float32
BF16 = mybir.dt.bfloat16
AF = mybir.ActivationFunctionType
ALU = mybir.AluOpType

D = 1024
S = 2048
DEPTH = 2
NCH = 8
NG = 4
NT = 16
GS = 512
EPS = 1e-6
NEG = -30000.0
NU_IN = 54
FFN_JS = [8, 7, 7]
FFN_OFF = [0, 8, 15]

ENGS = ["pe", "act", "dve", "pool", "sp"]
SAME_ENGINE_SYNC = True


class Buf:
    __slots__ = ("name", "last_write", "reads", "dma_sem", "dma_count")

    def __init__(self, name=""):
        self.name = name
        self.last_write = None
        self.reads = []
        self.dma_sem = None
        self.dma_count = 0


class Event:
    __slots__ = ("kind", "eng", "op", "sem", "value")

    def __init__(self, kind, eng=None, op=None, sem=None, value=None):
        self.kind = kind
        self.eng = eng
        self.op = op
        self.sem = sem
        self.value = value


class Op:
    __slots__ = ("eng", "fn", "deps", "needed", "seq", "count", "is_dma", "dma_sem")

    def __init__(self, eng, fn):
        self.eng = eng
        self.fn = fn
        self.deps = []
        self.needed = False
        self.seq = None
        self.count = None
        self.is_dma = False
        self.dma_sem = None


class Prog:
    def __init__(self, nc):
        self.nc = nc
        self.ops = {e: [] for e in ENGS}
        self.sems = {e: nc.alloc_semaphore("s_" + e) for e in ENGS}
        self.waited = {e: {f: -1 for f in ENGS} for e in ENGS}
        self.dma_waited = {e: {} for e in ENGS}
        self.nsem = 0

    def _add_dep(self, op, ev):
        if ev is None:
            return
        if ev.kind == "eng":
            if ev.eng == op.eng and (not SAME_ENGINE_SYNC or op.eng == "pe"):
                return
            if ev.op is op:
                return
            if self.waited[op.eng][ev.eng] >= ev.op.seq:
                return
            self.waited[op.eng][ev.eng] = ev.op.seq
            ev.op.needed = True
            op.deps.append(ev)
        else:
            key = id(ev.sem)
            if self.dma_waited[op.eng].get(key, -1) >= ev.value:
                return
            self.dma_waited[op.eng][key] = ev.value
            op.deps.append(ev)

    def op(self, eng, fn, reads=(), writes=()):
        o = Op(eng, fn)
        o.seq = len(self.ops[eng])
        for b in reads:
            self._add_dep(o, b.last_write)
        for b in writes:
            self._add_dep(o, b.last_write)
            for r in b.reads:
                self._add_dep(o, r)
        ev = Event("eng", eng=eng, op=o)
        for b in reads:
            b.reads.append(ev)
        for b in writes:
            b.last_write = ev
            b.reads = []
        self.ops[eng].append(o)
        return o

    def dma(self, eng, fns, dst, reads=(), extra_writes=()):
        if not isinstance(fns, (list, tuple)):
            fns = [fns]
        if dst.dma_sem is None:
            dst.dma_sem = self.nc.alloc_semaphore("d%d" % self.nsem)
            self.nsem += 1
        for i, fn in enumerate(fns):
            o = Op(eng, fn)
            o.is_dma = True
            o.dma_sem = dst.dma_sem
            o.seq = len(self.ops[eng])
            if i == 0:
                for b in reads:
                    self._add_dep(o, b.last_write)
                for b in [dst] + list(extra_writes):
                    self._add_dep(o, b.last_write)
                    for r in b.reads:
                        self._add_dep(o, r)
            self.ops[eng].append(o)
        dst.dma_count += len(fns)
        ev = Event("dma", sem=dst.dma_sem, value=16 * dst.dma_count)
        for b in reads:
            b.reads.append(ev)
        for b in [dst] + list(extra_writes):
            b.last_write = ev
            b.reads = []
        return ev

    def emit(self, final_waits=()):
        nc = self.nc
        for b in final_waits:
            if b.last_write is not None and b.last_write.kind == "eng":
                b.last_write.op.needed = True
        for e in ENGS:
            c = 0
            for o in self.ops[e]:
                if o.needed:
                    c += 1
                    o.count = c
        sems = self.sems

        def run(e, engobj):
            for o in self.ops[e]:
                for ev in o.deps:
                    if ev.kind == "eng":
                        engobj.wait_ge(sems[ev.eng], ev.op.count)
                    else:
                        engobj.wait_ge(ev.sem, ev.value)
                inst = o.fn(engobj)
                if o.is_dma:
                    inst.then_inc(o.dma_sem, 16)
                elif o.needed:
                    inst.then_inc(sems[e], 1)
            if e == "sp":
                for b in final_waits:
                    ev = b.last_write
                    if ev is None:
                        continue
                    if ev.kind == "dma":
                        engobj.wait_ge(ev.sem, ev.value)
                    else:
                        engobj.wait_ge(sems[ev.eng], ev.op.count)

        with nc.Block() as block:
            @block.tensor
            def _(e):
                run("pe", e)

            @block.scalar
            def _(e):
                run("act", e)

            @block.vector
            def _(e):
                run("dve", e)

            @block.gpsimd
            def _(e):
                run("pool", e)

            @block.sync
            def _(e):
                run("sp", e)


def switch_bufs(old, new):
    evs = []
    for b in old:
        if b.last_write is not None:
            evs.append(b.last_write)
        evs.extend(b.reads)
    for b in new:
        b.last_write = None
        b.reads = list(evs)


def build(nl=DEPTH, stop=None):
    nc = bass.Bass("TRN2", target_bir_lowering=False)

    def din(name, shape, dt=F32):
        return nc.dram_tensor(name, list(shape), dt, kind="ExternalInput").ap()

    x_d = din("x", [S, D])
    cT_d = din("cT", [128, 8])
    gmix_d = din("gmix", [128, DEPTH * 8])
    gffn_d = din("gffn", [128, DEPTH * 8])
    gfin_d = din("gfin", [128, 8])
    wada_d = din("wada", [DEPTH * 24, 128, 2048])
    bada_d = din("bada", [128, DEPTH * 48])
    win_d = din("win", [DEPTH * NU_IN, 128, 1024])
    wf_d = din("wf", [DEPTH, 128, 64])
    bfor_d = din("bfor", [8, DEPTH])
    sinks_d = din("sinksb", [128, DEPTH * 8])
    biasc_d = din("biasc", [DEPTH * 4, 128, 2 * 5 * 128])
    wbr_d = din("wbr", [DEPTH * 24, 128, 512])
    wout_d = din("wout", [DEPTH * 8, 128, 1024])
    wfi_d = din("wfi", [DEPTH * 22, 128, 2048])
    wfo_d = din("wfo", [DEPTH * 24, 128, 1024])
    identf_d = din("identf", [128, 128])
    identb_d = din("identb", [128, 128], BF16)
    trimask_d = din("trimask", [128, 128], BF16)
    alibi_d = din("alibi", [128, 2 * 8 * 128], BF16)
    maskc_d = din("maskc", [128, 2 * 128], BF16)
    augc_d = din("augc", [6, S], BF16)
    out_d = nc.dram_tensor("out", [S, D], F32, kind="ExternalOutput").ap()

    P = Prog(nc)

    def sb(name, shape, dt):
        return nc.alloc_sbuf_tensor("sb_" + name, list(shape), dt)

    XT = sb("XT", [128, NCH, S], F32)
    HT = sb("HT", [128, NCH, S], BF16)
    W1 = sb("W1", [128, 24576], BF16)
    NS = 4
    WS = sb("WS", [128, NS, 2048], BF16)
    RS = sb("RS", [128, S], F32)
    PT = sb("PT", [128, 8, GS], BF16)
    TMPF = sb("TMPF", [128, 3, GS], F32)
    RD = sb("RD", [128, 2, GS], F32)
    CUM3 = sb("CUM3", [72, S], BF16)
    BIASC = sb("BIASC", [128, 2, 1280], BF16)
    identf = sb("identf", [128, 128], F32)
    identb = sb("identb", [128, 128], BF16)
    trimask = sb("trimask", [128, 128], BF16)
    alibi = sb("alibi", [128, 2, 8, 128], BF16)
    maskc = sb("maskc", [128, 2, 128], BF16)
    onesb = sb("onesb", [128, 128], BF16)
    ones8 = sb("ones8", [8, 1], F32)
    cT = sb("cTs", [128, 8], F32)
    condb = sb("condb", [128, 8], BF16)
    gmix = sb("gmixs", [128, DEPTH * 8], F32)
    gffn = sb("gffns", [128, DEPTH * 8], F32)
    gfin = sb("gfins", [128, 8], F32)
    bada = sb("badas", [128, DEPTH * 48], F32)
    modT = sb("modT", [128, DEPTH, 48], F32)
    gsc = sb("gsc", [128, DEPTH, 16], F32)
    bfor = sb("bfors", [8, DEPTH], F32)
    negb = sb("negb", [8, DEPTH], F32)
    sinkb = sb("sinkbs", [128, DEPTH * 8], F32)
    expsink = sb("expsink", [128, DEPTH * 8], F32)

    PS = [nc.alloc_psum_tensor("ps%d" % i, [128, GS], F32) for i in range(8)]
    psb = [Buf("ps%d" % i) for i in range(8)]

    xt = [[Buf() for _ in range(NG)] for _ in range(NCH)]
    ht = [[Buf() for _ in range(NG)] for _ in range(NCH)]
    wsb = [Buf("ws%d" % i) for i in range(NS)]
    rsb = [Buf() for _ in range(NG)]
    ptb = [Buf() for _ in range(8)]
    tmpb = [Buf() for _ in range(3)]
    rdb = [Buf() for _ in range(2)]
    cum3b = Buf()
    biascb = [Buf(), Buf()]
    cb = Buf("consts")
    modb = [Buf() for _ in range(DEPTH)]
    modb2 = [Buf() for _ in range(DEPTH)]
    outb = Buf("out")

    sl = lambda g: slice(g * GS, (g + 1) * GS)
    tl = lambda t: slice(t * 128, (t + 1) * 128)

    state = {"wo": 0, "ws": 0, "pt": 0, "ptn": 0, "tmp": 0, "rd": 0, "pj": 0, "gu": 0}

    def wload(src_ap, n):
        i = state["ws"] % NS
        state["ws"] += 1
        dst = WS[:, i, 0:n]
        P.dma("pool", lambda e, d=dst, s_=src_ap: e.dma_start(out=d, in_=s_, max_dma_last_dim=4096), wsb[i])
        return dst, wsb[i]

    def pe_group(insts, reads, writes):
        def fn(e, insts=insts):
            r = None
            for it in insts:
                (o, l, rr, st, sp) = it[:5]
                if len(rr.shape) == 3 and len(o.shape) == 2:
                    o = o.rearrange("p (a b) -> p a b", a=rr.shape[1])
                if len(it) > 5 and it[5]:
                    r = e.matmul(o, lhsT=l, rhs=rr, start=st, stop=sp, skip_group_check=True)
                else:
                    r = e.matmul(o, lhsT=l, rhs=rr, start=st, stop=sp)
            return r
        return P.op("pe", fn, reads=reads, writes=writes)

    def act(out, in_, func, reads, writes, bias=None, scale=None):
        kw = {}
        if bias is not None:
            kw["bias"] = bias
        if scale is not None:
            kw["scale"] = scale
        return P.op("act", lambda e, o=out, i=in_, f=func, kw=kw: e.activation(out=o, in_=i, func=f, **kw),
                    reads=reads, writes=writes)

    def dve_tt(out, in0, in1, op, reads, writes):
        return P.op("dve", lambda e, o=out, a=in0, b=in1, op=op: e.tensor_tensor(out=o, in0=a, in1=b, op=op),
                    reads=reads, writes=writes)

    def dve_stt(out, in0, scalar, in1, op0, op1, reads, writes):
        return P.op("dve", lambda e, o=out, a=in0, s_=scalar, b=in1, p0=op0, p1=op1:
                    e.scalar_tensor_tensor(out=o, in0=a, scalar=s_, in1=b, op0=p0, op1=p1),
                    reads=reads, writes=writes)

    def dve_ts(out, in0, s1, s2, op0, op1, reads, writes, eng="dve"):
        if op1 is None:
            return P.op(eng, lambda e, o=out, a=in0, s1=s1, p0=op0: e.tensor_scalar(out=o, in0=a, scalar1=s1, scalar2=None, op0=p0),
                        reads=reads, writes=writes)
        return P.op(eng, lambda e, o=out, a=in0, s1=s1, s2=s2, p0=op0, p1=op1:
                    e.tensor_scalar(out=o, in0=a, scalar1=s1, scalar2=s2, op0=p0, op1=p1),
                    reads=reads, writes=writes)

    def copy(eng, out, in_, reads, writes):
        if eng == "act":
            return act(out, in_, AF.Copy, reads, writes)
        return P.op(eng, lambda e, o=out, i=in_: e.tensor_copy(out=o, in_=i), reads=reads, writes=writes)

    def next_rot(key, n):
        i = state[key] % n
        state[key] += 1
        return i

    small_loads = [
        (identf[:, :], identf_d[:, :]), (identb[:, :], identb_d[:, :]), (trimask[:, :], trimask_d[:, :]),
        (alibi[:, :, :, :].rearrange("p a h q -> p (a h q)"), alibi_d[:, :]),
        (maskc[:, :, :].rearrange("p a q -> p (a q)"), maskc_d[:, :]),
        (cT[:, :], cT_d[:, :]), (gmix[:, :], gmix_d[:, :]), (gffn[:, :], gffn_d[:, :]), (gfin[:, :], gfin_d[:, :]),
        (bada[:, :], bada_d[:, :]), (bfor[:, :], bfor_d[:, :]), (sinkb[:, :], sinks_d[:, :]),
    ]
    P.dma("sp", [lambda e, o=o, i=i: e.dma_start(out=o, in_=i) for (o, i) in small_loads], cb)
    onesbuf = Buf()
    P.op("pool", lambda e: e.memset(onesb[:, :], 1.0), writes=[onesbuf])
    P.op("pool", lambda e: e.memset(ones8[:, :], 1.0), writes=[onesbuf])
    cvb = Buf()
    act(condb[:, :], cT[:, :], AF.Silu, [cb], [cvb])
    act(expsink[:, :], sinkb[:, :], AF.Exp, [cb], [cvb])
    dve_ts(negb[:, :], bfor[:, :], -1.0, None, ALU.mult, None, [cb], [cvb])

    W1f = W1[:, :].bitcast(F32)
    xsb = [Buf() for _ in range(4)]
    w1_bufs = list(xsb)
    for t in range(NT):
        s_ = t % 4
        xs = W1f[:, s_ * 1024:(s_ + 1) * 1024]
        P.dma("sp", lambda e, o=xs, t=t: e.dma_start(out=o, in_=x_d[t * 128:(t + 1) * 128, :]), xsb[s_])
        for half in range(2):
            bi = (2 * t + half) % 4
            insts = [(PS[bi][:, q * 128:(q + 1) * 128], xs[:, (half * 4 + q) * 128:(half * 4 + q + 1) * 128]) for q in range(4)]

            def tfn(e, insts=insts):
                r = None
                for (o, i) in insts:
                    r = e.transpose(out=o, in_=i, identity=identf[:, :])
                return r
            P.op("pe", tfn, reads=[xsb[s_], cb], writes=[psb[bi]])
            copy("dve" if half == 0 else "act",
                 XT[:, half * 4:half * 4 + 4, tl(t)],
                 PS[bi][:, :].rearrange("p (c t) -> p c t", c=4),
                 [psb[bi]], [xt[c][t // 4] for c in range(half * 4, half * 4 + 4)])

    def ada_block(l, jb):
        modps = PS[7]
        wv, wb = wload(wada_d[l * 24 + jb], 2048)
        wv3 = wv.rearrange("p (k c) -> p k c", k=8)
        for jj in range(2):
            j = 2 * jb + jj
            insts = [(modps[:, j:j + 1], wv3[:, kc, jj * 128:(jj + 1) * 128], condb[:, kc:kc + 1], kc == 0, kc == 7)
                     for kc in range(8)]
            pe_group(insts, [wb, cvb], [psb[7]])

    def ada_finish(l, part):
        modps = PS[7]
        if part == 0:
            dve_tt(modT[:, l, 0:16], modps[:, 0:16], bada[:, l * 48:l * 48 + 16], ALU.add, [psb[7], cb], [modb[l]])
            dve_stt(gsc[:, l, 0:8], modT[:, l, 8:16], 1.0, gmix[:, l * 8:(l + 1) * 8], ALU.add, ALU.mult, [modb[l], cb], [modb[l]])
        else:
            dve_tt(modT[:, l, 16:48], modps[:, 16:48], bada[:, l * 48 + 16:(l + 1) * 48], ALU.add, [psb[7], cb], [modb2[l]])
            dve_stt(gsc[:, l, 8:16], modT[:, l, 32:40], 1.0, gffn[:, l * 8:(l + 1) * 8], ALU.add, ALU.mult, [modb2[l], cb], [modb2[l]])

    ada_rest = []

    def ada_first(l):
        for jb in range(8):
            ada_block(l, jb)
        ada_finish(l, 0)
        ada_rest.extend(range(8, 24))

    def ada_more(l, n):
        for _ in range(n):
            if ada_rest:
                ada_block(l, ada_rest.pop(0))
                if not ada_rest:
                    ada_finish(l, 1)

    def norm_group(g, gs_ap, sh_ap, vec_bufs, dst):
        bi = 5 + (g % 2)
        for c in range(NCH):
            i = next_rot("ptn", 4)
            if c % 2 == 0:
                act(PT[:, i, :], XT[:, c, sl(g)], AF.Square, [xt[c][g]], [ptb[i]])
            else:
                P.op("pool", lambda e, i=i, c=c, g=g: e.tensor_tensor(out=PT[:, i, :], in0=XT[:, c, sl(g)], in1=XT[:, c, sl(g)],
                                                                     op=ALU.mult), reads=[xt[c][g]], writes=[ptb[i]])
            pe_group([(PS[bi][:, :], onesb[:, :], PT[:, i, :], c == 0, c == NCH - 1)], [ptb[i], onesbuf], [psb[bi]])
        ti = next_rot("tmp", 3)
        act(TMPF[:, ti, :], PS[bi][:, :], AF.Ln, [psb[bi]], [tmpb[ti]], bias=EPS, scale=1.0 / D)
        act(RS[:, sl(g)], TMPF[:, ti, :], AF.Exp, [tmpb[ti]], [rsb[g]], scale=-0.5)
        for c in range(NCH):
            ti = next_rot("tmp", 3)
            dve_stt(TMPF[:, ti, :], XT[:, c, sl(g)], gs_ap[:, c:c + 1], RS[:, sl(g)], ALU.mult, ALU.mult,
                    [xt[c][g], rsb[g]] + vec_bufs, [tmpb[ti]])
            if dst is None:
                act(HT[:, c, sl(g)], TMPF[:, ti, :], AF.Identity, [tmpb[ti]] + vec_bufs, [ht[c][g]],
                    bias=sh_ap[:, c:c + 1], scale=1.0)
            else:
                dst(c, g, ti)

    def norm(gs_ap, sh_ap, vec_bufs, dst=None, after_group=None, skew=1):
        for g in range(NG):
            norm_group(g, gs_ap, sh_ap, vec_bufs, dst)
            if after_group is not None and g - skew >= 0:
                after_group(g - skew)
        if after_group is not None:
            for g in range(max(0, NG - skew), NG):
                after_group(g)

    def proj_fm(wv3, wb, g, nk=8, src=None, srcb=None):
        bi = 5 + next_rot("pj", 2 if ada_rest else 3)
        if src is None:
            src, srcb = HT, ht
        insts = [(PS[bi][:, :], wv3[:, kc, :], src[:, kc, sl(g)], kc == 0, kc == nk - 1) for kc in range(nk)]
        pe_group(insts, [wb] + [srcb[kc][g] for kc in range(nk)], [psb[bi]])
        return bi

    def proj_v(wv3, wb, t4, VB, vbufs):
        bi = 5 + next_rot("pj", 2 if ada_rest else 3)
        insts = []
        for q in range(4):
            t = t4 * 4 + q
            for kc in range(8):
                insts.append((PS[bi][:, q * 128:(q + 1) * 128], HT[:, kc, tl(t)], wv3[:, kc, :], kc == 0, kc == 7))
        pe_group(insts, [wb] + [ht[kc][t4] for kc in range(8)], [psb[bi]])
        pv = PS[bi][:, :].rearrange("p (q c) -> p q c", q=4)
        copy("dve", VB[:, t4 * 4:t4 * 4 + 4, 0:64], pv[:, :, 0:64], [psb[bi]], [vbufs[t4]])
        copy("dve", VB[:, t4 * 4:t4 * 4 + 4, 128:192], pv[:, :, 64:128], [psb[bi]], [vbufs[t4]])

    def attn_finish(obi, lo_num, out_ap, out_bufs, extra_reads, c0=0, sink_cols=None):
        num = slice(lo_num, lo_num + 64)
        den = slice(64 - lo_num, 128 - lo_num)
        ri = next_rot("rd", 2)
        if sink_cols is None:
            act(RD[num, ri, c0:], PS[obi][den, c0:], AF.Ln, [psb[obi]], [rdb[ri]])
        else:
            for hq in range(4):
                act(RD[num, ri, hq * 128:(hq + 1) * 128], PS[obi][den, hq * 128:(hq + 1) * 128], AF.Ln,
                    [psb[obi], cvb], [rdb[ri]], bias=expsink[num, sink_cols[hq]:sink_cols[hq] + 1], scale=1.0)
        act(RD[num, ri, c0:], RD[num, ri, c0:], AF.Exp, [rdb[ri]], [rdb[ri]], scale=-1.0)
        if sink_cols is None:
            dve_tt(out_ap, PS[obi][num, c0:], RD[num, ri, c0:], ALU.mult, [psb[obi], rdb[ri]] + extra_reads, out_bufs)
        else:
            dve_tt(out_ap, PS[obi][num, :].rearrange("p (h q) -> p h q", h=4),
                   RD[num, ri, :].rearrange("p (h q) -> p h q", h=4), ALU.mult,
                   [psb[obi], rdb[ri]] + extra_reads, out_bufs)

    OT = W1[:, 0:8192].rearrange("p (c t) -> p c t", c=4)
    MGR = W1[:, 8192:24576]
    MG = MGR.rearrange("p (c t) -> p c t", c=8)
    region = {"mg": list(w1_bufs), "ot": []}
    ot = [[Buf() for _ in range(NG)] for _ in range(4)]
    switch_bufs(w1_bufs, [b for r in ot for b in r])
    region["w1all"] = None

    def mg_switch(new):
        switch_bufs(region["mg"], new)
        region["mg"] = new

    def merge_branch(l, k, post_group=None):
        mg = [[Buf() for _ in range(NG)] for _ in range(NCH)]
        mg_switch([b for r in mg for b in r])
        for dc in range(NCH):
            wg, wgb = wload(win_d[l * NU_IN + 30 + k * 8 + dc], 1024)
            wg3 = wg.rearrange("p (k c) -> p k c", k=8)
            wbv, wbb = wload(wbr_d[l * 24 + k * 8 + dc], 512)
            wb3 = wbv.rearrange("p (k c) -> p k c", k=4)
            for g in range(NG):
                yb = g % 2
                gb = 2 + (g % 2)
                pe_group([(PS[yb][:, :], wb3[:, kc, :], OT[:, kc, sl(g)], kc == 0, kc == 3) for kc in range(4)],
                         [wbb] + [ot[kc][g] for kc in range(4)], [psb[yb]])
                pe_group([(PS[gb][:, :], wg3[:, kc, :], HT[:, kc, sl(g)], kc == 0, kc == 7) for kc in range(8)],
                         [wgb] + [ht[kc][g] for kc in range(8)], [psb[gb]])
                ti = next_rot("tmp", 3)
                act(TMPF[:, ti, :], PS[gb][:, :], AF.Sigmoid, [psb[gb]], [tmpb[ti]])
                dve_tt(MG[:, dc, sl(g)], PS[yb][:, :], TMPF[:, ti, :], ALU.mult, [psb[yb], tmpb[ti]], [mg[dc][g]])
        if post_group is None:
            for dco in range(NCH):
                wo, wob = wload(wout_d[l * 8 + dco], 1024)
                wo3 = wo.rearrange("p (k c) -> p k c", k=8)
                for g in range(NG):
                    bi = 4 + next_rot("pj", 4)
                    pe_group([(PS[bi][:, :], wo3[:, kc, :], MG[:, kc, sl(g)], kc == 0, kc == 7) for kc in range(8)],
                             [wob] + [mg[kc][g] for kc in range(8)], [psb[bi]])
                    dve_stt(XT[:, dco, sl(g)], PS[bi][:, :], modT[:, l, 16 + dco:17 + dco], XT[:, dco, sl(g)], ALU.mult, ALU.add,
                            [psb[bi], modb2[l], xt[dco][g]], [xt[dco][g]])
        else:
            wos = []
            for pr in range(4):
                i = state["ws"] % NS
                state["ws"] += 1
                P.dma("pool", [lambda e, i=i, h=h, pr=pr: e.dma_start(out=WS[:, i, h * 1024:(h + 1) * 1024],
                                                                      in_=wout_d[l * 8 + 2 * pr + h], max_dma_last_dim=4096)
                               for h in range(2)], wsb[i])
                for h in range(2):
                    wos.append((WS[:, i, h * 1024:(h + 1) * 1024].rearrange("p (k c) -> p k c", k=8), wsb[i]))
            obanks = [0, 1, 2, 3, 4, 7]
            for g in range(NG):
                for dco in range(NCH):
                    wo3, wob = wos[dco]
                    bi = obanks[next_rot("wo", len(obanks))]
                    pe_group([(PS[bi][:, :], wo3[:, kc, :], MG[:, kc, sl(g)], kc == 0, kc == 7) for kc in range(8)],
                             [wob] + [mg[kc][g] for kc in range(8)], [psb[bi]])
                    dve_stt(XT[:, dco, sl(g)], PS[bi][:, :], modT[:, l, 16 + dco:17 + dco], XT[:, dco, sl(g)], ALU.mult, ALU.add,
                            [psb[bi], modb2[l], xt[dco][g]], [xt[dco][g]])
                post_group(g)

    SBANKS = [0, 1, 2, 5, 6, 7]
    LOOK = 4

    def s_exp_pv(k_insts, nk_reads, c0, ncols, obi, v_lhsT, v_reads, first, last, pending, skip=False):
        sbl = state["sbanks"]
        sbi = sbl[next_rot("sb", len(sbl))]
        insts = [(PS[sbi][:, c0:c0 + ncols] if o is None else o, l_, r_, st, sp) for (o, l_, r_, st, sp) in k_insts(sbi)]
        pe_group(insts, nk_reads, [psb[sbi]])
        pi = next_rot("pt", 8)
        act(PT[:, pi, c0:c0 + ncols], PS[sbi][:, c0:c0 + ncols], AF.Exp, [psb[sbi]], [ptb[pi]])
        pending.append(("pv", [(PS[obi][:, c0:c0 + ncols], v_lhsT, PT[:, pi, c0:c0 + ncols], first, last, skip)],
                        [ptb[pi]] + v_reads, [psb[obi]]))

    def push_fin(pending, fn):
        pending.append(("fin", fn))

    def flush(pending, keep):
        def npv():
            return sum(1 for it in pending if it[0] == "pv")
        while pending and (npv() > keep or pending[0][0] == "fin"):
            it = pending.pop(0)
            if it[0] == "pv":
                pe_group(it[1], it[2], it[3])
            else:
                it[1]()

    state["sb"] = 0
    state["ob"] = 0
    state["oba"] = 0
    state["sbanks"] = SBANKS

    def branch_A(l, norm_done=False):
        QA = MGR[:, 0:8192].rearrange("p (c t) -> p c t", c=4)
        KA = [MGR[:, 8192:10240], MGR[:, 10240:12288]]
        VA = MGR[:, 12288:15360].rearrange("p (t c) -> p t c", t=16)
        qab = [[Buf() for _ in range(NG)] for _ in range(4)]
        kab = [[Buf() for _ in range(NG)] for _ in range(2)]
        vab = [Buf() for _ in range(4)]
        vones = Buf()
        mg_switch([b for r in qab for b in r] + [b for r in kab for b in r] + vab + [vones])
        P.op("pool", lambda e: e.memset(VA[:, :, 64:128], 1.0), writes=[vones] + vab)
        for kv in range(2):
            P.op("pool", lambda e, kv=kv: e.memset(KA[kv], 0.0), writes=kab[kv])
        base = l * NU_IN

        def qproj(c, g, wv3, wb):
            bi = proj_fm(wv3, wb, g)
            dve_ts(QA[:, c, sl(g)], PS[bi][:, :], 0.125, None, ALU.mult, None, [psb[bi]], [qab[c][g]])
        pro = []
        for c in range(3):
            wv, wb = wload(win_d[base + c], 1024)
            pro.append((c, wv.rearrange("p (k c) -> p k c", k=8), wb))

        def after_group(g):
            for (c, wv3, wb) in pro:
                qproj(c, g, wv3, wb)
        if norm_done:
            for g in range(NG):
                after_group(g)
        else:
            norm(gsc[:, l, 0:8], modT[:, l, 0:8], [modb[l]], after_group=after_group)
        ada_more(l, 7)
        wv, wb = wload(win_d[base + 3], 1024)
        wv3 = wv.rearrange("p (k c) -> p k c", k=8)
        for g in range(NG):
            qproj(3, g, wv3, wb)
        ada_more(l, 3)
        wv, wb = wload(win_d[base + 4], 1024)
        wv3 = wv.rearrange("p (k c) -> p k c", k=8)
        for g in range(NG):
            bi = proj_fm(wv3, wb, g)
            copy("dve", KA[0][0:64, sl(g)], PS[bi][0:64, :], [psb[bi]], [kab[0][g]])
            copy("dve", KA[1][64:128, sl(g)], PS[bi][64:128, :], [psb[bi]], [kab[1][g]])
        ada_more(l, 3)
        wv, wb = wload(win_d[base + 5], 1024)
        wv3 = wv.rearrange("p (k c) -> p k c", k=8)
        for t4 in range(4):
            proj_v(wv3, wb, t4, VA, vab)
        ada_more(l, 99)
        pending = []
        state["sbanks"] = [0, 1, 2, 5]
        obl = [3, 4, 6, 7]
        for m in range(NT):
            for kv in range(2):
                rows = slice(kv * 64, kv * 64 + 64)
                obi = obl[next_rot("oba", len(obl))]
                vsl = slice(0, 128) if kv == 0 else slice(64, 192)
                kts = [m] if m == 0 else [m - 1, m]
                for ji, j in enumerate(kts):
                    kind = m - j

                    def k_insts(sbi, j=j, kind=kind, rows=rows, kv=kv, m=m):
                        return [
                            (None, KA[kv][:, tl(j)], QA[:, :, tl(m)], True, False),
                            (None, identb[:, :], alibi[:, kind, kv * 4:kv * 4 + 4, :], False, True),
                        ]
                    s_exp_pv(k_insts, [kab[kv][j // 4], cb] + [qab[c][m // 4] for c in range(4)], 0, GS, obi,
                             VA[:, j, vsl], [vab[j // 4], vones], ji == 0, ji == len(kts) - 1, pending)
                    flush(pending, LOOK)
                sink_cols = [l * 8 + kv * 4 + hq for hq in range(4)]
                push_fin(pending, lambda obi=obi, kv=kv, rows=rows, m=m, sink_cols=sink_cols:
                         attn_finish(obi, kv * 64, OT[rows, :, tl(m)], [ot[c][m // 4] for c in range(4)], [], sink_cols=sink_cols))
        flush(pending, 0)
        state["sbanks"] = SBANKS

    def branch_BC(l, which):
        is_b = which == "B"
        base = l * NU_IN + (6 if is_b else 18)
        QK = [MGR[:, i * 2048:(i + 1) * 2048] for i in range(4)]
        VB = MGR[:, 8192:11264].rearrange("p (t c) -> p t c", t=16)
        qb = [[Buf() for _ in range(NG)] for _ in range(4)]
        augb = [Buf() for _ in range(4)]
        vbb = [Buf() for _ in range(4)]
        vones = Buf()
        mg_switch([b for r in qb for b in r] + augb + vbb + [vones])
        P.op("pool", lambda e: e.memset(VB[:, :, 64:128], 1.0), writes=[vones] + vbb)
        if is_b:
            wfv, wfb = wload(wf_d[l], 64)
            wf3 = wfv.rearrange("p (k c) -> p k c", k=8)
            A8 = RS[0:8, :]
            for g in range(NG):
                bi = 5 + next_rot("pj", 3)
                pe_group([(PS[bi][0:8, :], wf3[:, kc, :], HT[:, kc, sl(g)], kc == 0, kc == 7) for kc in range(8)],
                         [wfb] + [ht[kc][g] for kc in range(8)], [psb[bi]])
                act(A8[:, sl(g)], PS[bi][0:8, :], AF.Exp, [psb[bi], cvb], [rsb[g]], bias=negb[:, l:l + 1], scale=-1.0)
                act(A8[:, sl(g)], A8[:, sl(g)], AF.Ln, [rsb[g]], [rsb[g]], bias=1.0, scale=1.0)
            P.op("dve", lambda e: e.tensor_tensor_scan(out=A8, data0=ones8[:, 0:1].to_broadcast([8, S]), data1=A8,
                                                       initial=0.0, op0=ALU.mult, op1=ALU.subtract),
                 reads=rsb + [onesbuf], writes=rsb)
            TB = PT[0:8, 0:4, :].rearrange("p a b -> p (a b)")
            copy("dve", CUM3[0:8, :], A8, rsb, [cum3b])
            dve_tt(A8, A8, CUM3[0:8, :], ALU.subtract, rsb + [cum3b], rsb)
            copy("dve", TB, A8, rsb, ptb)
            dve_tt(A8, A8, TB, ALU.subtract, rsb + ptb[0:4], rsb)
            copy("act", CUM3[32:40, :], TB, ptb[0:4], [cum3b])
            copy("dve", TB, A8, rsb, ptb)
            copy("act", CUM3[64:72, :], TB, ptb[0:4], [cum3b])
        for i in range(4):
            P.op("pool", lambda e, i=i: e.memset(QK[i], 0.0), writes=qb[i] + [augb[i]])
        if is_b:
            cr = [(0, 67, 0), (1, 3, 0), (2, 64, 3), (3, 0, 3)]
            for (i, r0, a0) in cr:
                P.dma("sp", lambda e, i=i, r0=r0, a0=a0: e.dma_start(out=QK[i][r0:r0 + 3, :], in_=augc_d[a0:a0 + 3, :]), augb[i])
        for p in range(4):
            if not is_b:
                bs = p % 2
                P.dma("pool", lambda e, bs=bs, p=p: e.dma_start(out=BIASC[:, bs, :], in_=biasc_d[l * 4 + p], max_dma_last_dim=4096),
                      biascb[bs])
            if is_b:
                he, ho = 2 * p, 2 * p + 1
                mv = [(0, 64, he), (1, 0, ho), (2, 67, he), (3, 3, ho)]
                for (i, r0, h) in mv:
                    P.dma("sp", [lambda e, i=i, r0=r0, h=h, q=q: e.dma_start(out=QK[i][r0 + q:r0 + q + 1, :],
                                                                             in_=CUM3[32 * q + h:32 * q + h + 1, :])
                                 for q in range(3)], augb[i], reads=[cum3b])
            wq, wqb = wload(win_d[base + p * 3 + 0], 1024)
            wq3 = wq.rearrange("p (k c) -> p k c", k=8)
            for g in range(NG):
                bi = proj_fm(wq3, wqb, g)
                dve_ts(QK[0][0:64, sl(g)], PS[bi][0:64, :], 0.125, None, ALU.mult, None, [psb[bi]], [qb[0][g]])
                dve_ts(QK[1][64:128, sl(g)], PS[bi][64:128, :], 0.125, None, ALU.mult, None, [psb[bi]], [qb[1][g]])
            wk, wkb = wload(win_d[base + p * 3 + 1], 1024)
            wk3 = wk.rearrange("p (k c) -> p k c", k=8)
            for g in range(NG):
                bi = proj_fm(wk3, wkb, g)
                copy("dve", QK[2][0:64, sl(g)], PS[bi][0:64, :], [psb[bi]], [qb[2][g]])
                copy("dve", QK[3][64:128, sl(g)], PS[bi][64:128, :], [psb[bi]], [qb[3][g]])
            wv, wvb = wload(win_d[base + p * 3 + 2], 1024)
            wv3 = wv.rearrange("p (k c) -> p k c", k=8)
            for t4 in range(4):
                proj_v(wv3, wvb, t4, VB, vbb)
            pending = []
            for hh in range(2):
                rows = slice(hh * 64, hh * 64 + 64)
                vsl = slice(0, 128) if hh == 0 else slice(64, 192)
                for g in range(NG):
                    obi = 3 + next_rot("ob", 2)
                    if is_b:
                        kts = list(range(0, 4 * g + 4))
                    else:
                        kts = list(range(max(0, 4 * g - 4), 4 * g + 4))
                    for ji, j in enumerate(kts):
                        if is_b:
                            r = j - 4 * g
                            c0 = 128 * r if r > 0 else 0
                            ncols = GS - c0
                            Qt, Kt = QK[hh], QK[2 + hh]

                            def k_insts(sbi, j=j, r=r, c0=c0, ncols=ncols, Qt=Qt, Kt=Kt, g=g):
                                ins = [(None, Kt[:, tl(j)], Qt[:, g * GS + c0:(g + 1) * GS], True, r < 0)]
                                if r >= 0:
                                    ins.append((PS[sbi][:, c0:c0 + 128], identb[:, :], trimask[:, :], False, True))
                                return ins
                            kreads = [qb[hh][g], qb[2 + hh][j // 4], augb[hh], augb[2 + hh], cb]
                        else:
                            m0 = max(j, 4 * g)
                            m1 = min(j + 4, 4 * g + 3)
                            c0 = (m0 - 4 * g) * 128
                            ncols = (m1 - m0 + 1) * 128
                            k0 = m0 - j
                            k1 = m1 - j
                            bs = p % 2

                            def k_insts(sbi, j=j, c0=c0, ncols=ncols, k0=k0, k1=k1, rows=rows, g=g, hh=hh, bs=bs):
                                ins = [(None, QK[2 + hh][:, tl(j)], QK[hh][:, g * GS + c0:g * GS + c0 + ncols], True, False)]
                                has0 = (k0 == 0)
                                has4 = (k1 == 4)
                                ins.append((None, identb[:, :], BIASC[:, bs, hh * 640 + k0 * 128:hh * 640 + (k1 + 1) * 128],
                                            False, not (has0 or has4)))
                                if has0:
                                    ins.append((PS[sbi][:, c0:c0 + 128], identb[:, :], maskc[:, 0, :], False, not has4))
                                if has4:
                                    ins.append((PS[sbi][:, c0 + ncols - 128:c0 + ncols], identb[:, :], maskc[:, 1, :], False, True))
                                return ins
                            kreads = [qb[hh][g], qb[2 + hh][j // 4], biascb[bs], cb]
                        s_exp_pv(k_insts, kreads, c0, ncols, obi, VB[:, j, vsl], [vbb[j // 4], vones],
                                 ji == 0, ji == len(kts) - 1, pending, skip=not is_b)
                        flush(pending, LOOK)
                    push_fin(pending, lambda obi=obi, hh=hh, rows=rows, p=p, g=g:
                             attn_finish(obi, hh * 64, OT[rows, p, sl(g)], [ot[p][g]], []))
            flush(pending, 0)

    def ffn(l, next_ada, post_group=None, extra_bufs=()):
        ada_todo = list(range(24)) if next_ada else []
        for J in range(3):
            nj = FFN_JS[J]
            AT = W1[:, 0:nj * 2048].rearrange("p (j t) -> p j t", j=nj)
            at = [[Buf() for _ in range(NG)] for _ in range(nj)]
            allb = [b for r in at for b in r]
            if J == 0:
                switch_bufs(region["mg"] + [b for r in ot for b in r], allb)
            elif J == 2:
                switch_bufs(region["mg"], allb + list(extra_bufs))
            else:
                switch_bufs(region["mg"], allb)
            region["mg"] = allb

            def gu_tile(jj, g, w4, wb, at=at, AT=AT):
                r2 = next_rot("gu", 2)
                gb = r2
                ub = 2 + r2
                pe_group([(PS[gb][:, :], w4[:, 0, kc, :], HT[:, kc, sl(g)], kc == 0, kc == 7) for kc in range(8)],
                         [wb] + [ht[kc][g] for kc in range(8)], [psb[gb]])
                pe_group([(PS[ub][:, :], w4[:, 1, kc, :], HT[:, kc, sl(g)], kc == 0, kc == 7) for kc in range(8)],
                         [wb] + [ht[kc][g] for kc in range(8)], [psb[ub]])
                ti = next_rot("tmp", 3)
                act(TMPF[:, ti, :], PS[gb][:, :], AF.Silu, [psb[gb]], [tmpb[ti]])
                dve_tt(AT[:, jj, sl(g)], PS[ub][:, :], TMPF[:, ti, :], ALU.mult, [psb[ub], tmpb[ti]], [at[jj][g]])

            for jj in range(nj):
                j = FFN_OFF[J] + jj
                wv, wb = wload(wfi_d[l * 22 + j], 2048)
                w4 = wv.rearrange("p (a k c) -> p a k c", a=2, k=8)
                for g in range(NG):
                    gu_tile(jj, g, w4, wb)
                for _ in range(2 if j < 2 else 1):
                    if ada_todo:
                        ada_block(l + 1, ada_todo.pop(0))
            if J == 2 and next_ada:
                assert not ada_todo
                ada_finish(l + 1, 0)
                ada_finish(l + 1, 1)
            if J < 2 or post_group is None:
                for dco in range(NCH):
                    wo, wob = wload(wfo_d[l * 24 + J * 8 + dco], 1024)
                    wo3 = wo.rearrange("p (j c) -> p j c", j=8)
                    for g in range(NG):
                        bi = 4 + next_rot("pj", 3)
                        pe_group([(PS[bi][:, :], wo3[:, jj, :], AT[:, jj, sl(g)], jj == 0, jj == nj - 1) for jj in range(nj)],
                                 [wob] + [at[jj][g] for jj in range(nj)], [psb[bi]])
                        dve_stt(XT[:, dco, sl(g)], PS[bi][:, :], modT[:, l, 40 + dco:41 + dco], XT[:, dco, sl(g)], ALU.mult, ALU.add,
                                [psb[bi], modb2[l], xt[dco][g]], [xt[dco][g]])
            else:
                wos = []
                for pr in range(4):
                    i = state["ws"] % NS
                    state["ws"] += 1
                    P.dma("pool", [lambda e, i=i, h=h, pr=pr: e.dma_start(out=WS[:, i, h * 1024:(h + 1) * 1024],
                                                                          in_=wfo_d[l * 24 + J * 8 + 2 * pr + h], max_dma_last_dim=4096)
                                   for h in range(2)], wsb[i])
                    for h in range(2):
                        wos.append((WS[:, i, h * 1024:(h + 1) * 1024].rearrange("p (j c) -> p j c", j=8), wsb[i]))
                obanks = [0, 1, 2, 3, 4, 7]
                for g in range(NG):
                    for dco in range(NCH):
                        wo3, wob = wos[dco]
                        bi = obanks[next_rot("wo", len(obanks))]
                        pe_group([(PS[bi][:, :], wo3[:, jj, :], AT[:, jj, sl(g)], jj == 0, jj == nj - 1) for jj in range(nj)],
                                 [wob] + [at[jj][g] for jj in range(nj)], [psb[bi]])
                        dve_stt(XT[:, dco, sl(g)], PS[bi][:, :], modT[:, l, 40 + dco:41 + dco], XT[:, dco, sl(g)], ALU.mult, ALU.add,
                                [psb[bi], modb2[l], xt[dco][g]], [xt[dco][g]])
                    post_group(g)
        newot = [b for r in ot for b in r]
        dummy = [Buf()]
        switch_bufs(region["mg"], newot + dummy)
        region["mg"] = dummy

    fin_bufs = [Buf() for _ in range(3)]
    outbs = [Buf() for _ in range(3)]
    YS = [W1f[:, 8192 + i * 1024:8192 + (i + 1) * 1024] for i in range(3)]

    def out_tiles(g):
        for t in range(4 * g, 4 * g + 4):
            si = t % 3
            for half in range(2):
                bi = (2 * t + half) % 4
                insts = [(PS[bi][:, q * 128:(q + 1) * 128], XT[:, half * 4 + q, tl(t)]) for q in range(4)]

                def tfn(e, insts=insts):
                    r = None
                    for (o, i) in insts:
                        r = e.transpose(out=o, in_=i, identity=identf[:, :])
                    return r
                P.op("pe", tfn, reads=[xt[half * 4 + q][t // 4] for q in range(4)] + [cb], writes=[psb[bi]])
                copy("dve" if half == 0 else "act", YS[si][:, half * 512:(half + 1) * 512], PS[bi][:, :], [psb[bi]], [fin_bufs[si]])
            P.dma("sp", lambda e, t=t, si=si: e.dma_start(out=out_d[t * 128:(t + 1) * 128, :], in_=YS[si]), outbs[si], reads=[fin_bufs[si]])

    def fin_dst(c, g, ti):
        copy("act", XT[:, c, sl(g)], TMPF[:, ti, :], [tmpb[ti]], [xt[c][g]])

    fused_final = False
    for l in range(nl):
        if l == 0:
            ada_first(l)
        branch_A(l, norm_done=(l > 0))
        merge_branch(l, 0)
        if stop == "a%d" % l:
            break
        branch_BC(l, "B")
        merge_branch(l, 1)
        if stop == "b%d" % l:
            break
        branch_BC(l, "C")
        if stop == "m%d" % l:
            merge_branch(l, 2)
            break
        merge_branch(l, 2, post_group=lambda g, l=l: norm_group(g, gsc[:, l, 8:16], modT[:, l, 24:32], [modb2[l]], None))
        if l + 1 < nl:
            ffn(l, True, post_group=lambda g, l=l: norm_group(g, gsc[:, l + 1, 0:8], modT[:, l + 1, 0:8], [modb[l + 1]], None))
        elif stop is None:
            def pg(g):
                norm_group(g, gfin, None, [cb], fin_dst)
                if g >= 1:
                    out_tiles(g - 1)
            ffn(l, False, post_group=pg, extra_bufs=fin_bufs)
            out_tiles(NG - 1)
            fused_final = True
        else:
            ffn(l, False)

    if not fused_final:
        switch_bufs(region["mg"] + [b for r in ot for b in r], fin_bufs)
        for g in range(NG):
            out_tiles(g)
    P.emit(final_waits=outbs)
    return nc


def _win_cols():
    units = []
    for c in range(4):
        units.append(list(range(c * 64, c * 64 + 64)) + list(range((c + 4) * 64, (c + 4) * 64 + 64)))
    units.append(list(range(512, 640)))
    units.append(list(range(640, 768)))
    for p in range(4):
        units.append(list(range(768 + p * 128, 768 + (p + 1) * 128)))
        units.append(list(range(1280 + p * 128, 1280 + (p + 1) * 128)))
        units.append(list(range(1792 + p * 128, 1792 + (p + 1) * 128)))
    for p in range(4):
        units.append(list(range(2312 + p * 128, 2312 + (p + 1) * 128)))
        units.append(list(range(2824 + p * 128, 2824 + (p + 1) * 128)))
        units.append(list(range(3336 + p * 128, 3336 + (p + 1) * 128)))
    for k in range(3):
        for dc in range(8):
            units.append(list(range(3848 + k * 1024 + dc * 128, 3848 + k * 1024 + (dc + 1) * 128)))
    assert len(units) == NU_IN
    return np.array(units)


def _const_tables():
    bf = ml_dtypes.bfloat16
    identf = np.eye(128, dtype=np.float32)
    identb = np.eye(128).astype(bf)
    s = np.arange(128)[:, None]
    q = np.arange(128)[None, :]
    trimask = np.where(s > q, NEG, 0.0).astype(bf)
    slopes = 2.0 ** (-(np.arange(1, 9)))
    alibi = np.zeros((128, 2, 8, 128), np.float32)
    for kind in range(2):
        dist = 128 * kind + q - s
        msk = ((s >= 64) & (q < 64)) if kind == 0 else ((s < 64) & (q >= 64))
        for h in range(8):
            alibi[:, kind, h, :] = np.where(msk, NEG, -slopes[h] * np.abs(dist))
    maskc = np.zeros((128, 2, 128), np.float32)
    maskc[:, 0, :] = np.where((s >= 64) & (q < 64), NEG, 0.0)
    maskc[:, 1, :] = np.where((s < 64) & (q >= 64), NEG, 0.0)
    augc = np.concatenate([-np.ones((3, S), np.float32), np.ones((3, S), np.float32)], 0)
    return dict(identf=identf, identb=identb, trimask=trimask, alibi=alibi.reshape(128, -1).astype(bf),
                maskc=maskc.reshape(128, -1).astype(bf), augc=augc.astype(bf))


def _prep_shared(inp):
    f32 = np.float32
    vecT = lambda v: np.ascontiguousarray(v.reshape(-1, 8, 128).transpose(2, 0, 1).reshape(128, -1)).astype(f32)
    sh = {}
    sh["gmix"] = vecT(inp["norm_mix_g"])
    sh["gffn"] = vecT(inp["norm_ffn_g"])
    sh["gfin"] = vecT(inp["final_norm_g"][None])
    w_ada = inp["w_ada"]
    sh["wada"] = np.ascontiguousarray(
        w_ada.reshape(DEPTH, 8, 128, 24, 256).transpose(0, 3, 2, 1, 4).reshape(DEPTH * 24, 128, 2048))
    sh["bada"] = np.ascontiguousarray(inp["b_ada"].reshape(DEPTH, 48, 128).transpose(2, 0, 1).reshape(128, DEPTH * 48))
    cols = _win_cols()
    w_in = inp["w_in"]
    wu = w_in[:, :, cols]
    sh["win"] = np.ascontiguousarray(
        wu.reshape(DEPTH, 8, 128, NU_IN, 128).transpose(0, 3, 2, 1, 4).reshape(DEPTH * NU_IN, 128, 1024))
    wfc = w_in[:, :, 2304:2312]
    sh["wf"] = np.ascontiguousarray(wfc.reshape(DEPTH, 8, 128, 8).transpose(0, 2, 1, 3).reshape(DEPTH, 128, 64))
    sh["bfor"] = np.ascontiguousarray(inp["b_forget"].T)
    sh["sinksb"] = np.ascontiguousarray(np.broadcast_to(inp["sinks"].reshape(1, DEPTH * 8), (128, DEPTH * 8)))
    rb = inp["rel_bias"]
    s = np.arange(128)[:, None]
    q = np.arange(128)[None, :]
    idx = np.stack([np.clip(128 * kind + q - s, -128, 128) + 128 for kind in range(5)], 0)
    tb = rb[:, :, idx]
    tb = tb.reshape(DEPTH, 4, 2, 5, 128, 128).transpose(0, 1, 4, 2, 3, 5)
    sh["biasc"] = np.ascontiguousarray(tb.reshape(DEPTH * 4, 128, 1280))
    wbr = inp["w_branch"].copy()
    permA = np.concatenate([np.concatenate([np.arange(c * 64, c * 64 + 64), np.arange((c + 4) * 64, (c + 4) * 64 + 64)])
                            for c in range(4)])
    wbr[:, 0] = wbr[:, 0][:, permA, :]
    sh["wbr"] = np.ascontiguousarray(
        wbr.reshape(DEPTH, 3, 4, 128, 8, 128).transpose(0, 1, 4, 3, 2, 5).reshape(DEPTH * 24, 128, 512))
    sh["wout"] = np.ascontiguousarray(
        inp["w_out"].reshape(DEPTH, 8, 128, 8, 128).transpose(0, 3, 2, 1, 4).reshape(DEPTH * 8, 128, 1024))
    wfi = inp["w_ffn_in"].reshape(DEPTH, 8, 128, 2, 22, 128)
    sh["wfi"] = np.ascontiguousarray(wfi.transpose(0, 4, 2, 3, 1, 5).reshape(DEPTH * 22, 128, 2048))
    wfo_src = inp["w_ffn_out"].reshape(DEPTH, 22, 128, 8, 128)
    wfo = np.zeros((DEPTH, 3, 8, 128, 8, 128), np.float32)
    for J in range(3):
        n = FFN_JS[J]
        wfo[:, J, :, :, :n, :] = wfo_src[:, FFN_OFF[J]:FFN_OFF[J] + n].transpose(0, 3, 2, 1, 4)
    sh["wfo"] = np.ascontiguousarray(wfo.reshape(DEPTH * 24, 128, 1024))
    sh.update(_const_tables())
    return {k: (v if v.dtype != np.float64 else v.astype(f32)) for k, v in sh.items()}


_NC_CACHE = {}


def kernel(**inputs):
    inp = {k: np.asarray(v) for k, v in inputs.items()}
    sh = _prep_shared(inp)
    key = "full"
    if key not in _NC_CACHE:
        _NC_CACHE[key] = build()
    nc = _NC_CACHE[key]
    in_maps = []
    for b in range(8):
        m = dict(sh)
        m["x"] = np.ascontiguousarray(inp["x"][b]).astype(np.float32)
        m["cT"] = np.ascontiguousarray(inp["c"][b].reshape(8, 128).T).astype(np.float32)
        in_maps.append(m)
    res = run_bass_kernel_spmd(nc, in_maps, core_ids=list(range(8)))
    out = np.stack([np.asarray(res.results[b]["out"]) for b in range(8)], 0)
    return out.astype(np.float32)
```
